# Optimizing a Trainium2 kernel written in Bass

```python
import math
import jax
import jax.numpy as jnp
from jax import lax
import numpy as np

D_MODEL = 2048
BATCH = 1
SEQ = 8192
DEPTH = 4

N_A_LAYERS = DEPTH // 2
N_B_LAYERS = DEPTH - N_A_LAYERS
PLE_DIM = 256
FFN_DIM = -(-8 * D_MODEL // (3 * 256)) * 256

GMLP_CHUNK = 128
GMLP_WIDTH = D_MODEL
GMLP_GROUP_DIM = 128
GMLP_GROUPS = GMLP_WIDTH // GMLP_GROUP_DIM

HEAD_DIM = 128
N_HEADS = D_MODEL // HEAD_DIM
N_KV_GROUPS = 2
HEADS_PER_GROUP = N_HEADS // N_KV_GROUPS
N_BRANCH = 3
CMP_BLOCK = 32
CMP_STRIDE = 16
CMP_HIDDEN = 256
SLC_BLOCK = 64
N_SELECT = 16
WINDOW = 512
Q_BLOCK = 128
N_BUCKETS = 32
MAX_DISTANCE = 128
EPS = 1e-6
NEG_INF = -1e30

kernel_name = 'hybrid_gmlp_nsa_yoco_trunk'


def rmsnorm(x, g):
    xf = x.astype(jnp.float32)
    y = xf * lax.rsqrt(jnp.mean(xf * xf, axis=-1, keepdims=True) + EPS)
    return (y * g.astype(jnp.float32)).astype(x.dtype)


def t5_bucket(dist):
    n = jnp.maximum(dist, 0)
    max_exact = N_BUCKETS // 2
    nf = jnp.maximum(n, 1).astype(jnp.float32)
    large = max_exact + (jnp.log(nf / max_exact) / math.log(MAX_DISTANCE / max_exact)
                         * (N_BUCKETS - max_exact)).astype(jnp.int32)
    large = jnp.minimum(large, N_BUCKETS - 1)
    return jnp.where(n < max_exact, n, large)


def rel_bias_heads(table, dist):
    b = table[t5_bucket(dist)]
    b = jnp.transpose(b, (2, 0, 1)).reshape(N_KV_GROUPS, HEADS_PER_GROUP, *dist.shape)
    return b.astype(jnp.float32)


def swiglu(h, w_in, w_out):
    g, u = jnp.split(h @ w_in, 2, axis=-1)
    return (jax.nn.silu(g) * u) @ w_out


def ple_add(x, p_i, norm_g, w_proj, w_gate):
    gate = jax.nn.sigmoid(rmsnorm(x, norm_g) @ w_gate)
    return x + (p_i @ w_proj) * gate


def gmlp_mixer(h, w_in, norm_v, w_s, b_s, w_out):
    B, S, _ = h.shape
    z = jax.nn.gelu(h @ w_in)
    u, v = jnp.split(z, 2, axis=-1)
    v = rmsnorm(v, norm_v)
    v = v.reshape(B, S // GMLP_CHUNK, GMLP_CHUNK, GMLP_GROUPS, GMLP_GROUP_DIM)
    causal = jnp.tril(jnp.ones((GMLP_CHUNK, GMLP_CHUNK), dtype=bool))
    ws = jnp.where(causal, w_s, 0)
    sv = jnp.einsum('gts,bcsgd->bctgd', ws, v) + jnp.transpose(b_s)[None, None, :, :, None]
    return (u * sv.reshape(B, S, GMLP_WIDTH)) @ w_out


def compress_blocks(t, pe, w1, w2):
    B, G, S, Dk = t.shape
    halves = t.reshape(B, G, S // CMP_STRIDE, CMP_STRIDE, Dk)
    blocks = jnp.concatenate([halves[:, :, :-1], halves[:, :, 1:]], axis=3) + pe
    flat = blocks.reshape(B, G, blocks.shape[2], CMP_BLOCK * Dk)
    return jax.nn.gelu(flat @ w1) @ w2


def shared_kv(x, kv_norm, kv_w, k_norm, cmp_pe_k, cmp_pe_v, cmp_wk1, cmp_wk2, cmp_wv1, cmp_wv2):
    B, S, _ = x.shape
    h = rmsnorm(x, kv_norm)
    kv = (h @ kv_w).reshape(B, S, 2 * N_BRANCH, N_KV_GROUPS, HEAD_DIM)
    kv = jnp.transpose(kv, (2, 0, 3, 1, 4))
    k_c, v_c, k_s, v_s, k_w, v_w = kv[0], kv[1], kv[2], kv[3], kv[4], kv[5]
    k_cmp = rmsnorm(compress_blocks(k_c, cmp_pe_k, cmp_wk1, cmp_wk2), k_norm[0])
    v_cmp = compress_blocks(v_c, cmp_pe_v, cmp_wv1, cmp_wv2)
    k_s = rmsnorm(k_s, k_norm[1])
    k_w = rmsnorm(k_w, k_norm[2])
    k_slc_b = k_s.reshape(B, N_KV_GROUPS, S // SLC_BLOCK, SLC_BLOCK, HEAD_DIM)
    v_slc_b = v_s.reshape(B, N_KV_GROUPS, S // SLC_BLOCK, SLC_BLOCK, HEAD_DIM)
    pad = ((0, 0), (0, 0), (WINDOW, 0), (0, 0))
    return (k_cmp, v_cmp, k_slc_b, v_slc_b, jnp.pad(k_w, pad), jnp.pad(v_w, pad))


def nsa_mixer(h, k_cmp, v_cmp, k_slc_b, v_slc_b, k_win_pad, v_win_pad, w_in, q_norm, rel_table, w_out):
    B, S, _ = h.shape
    n_qb = S // Q_BLOCK
    n_cmp = k_cmp.shape[2]
    n_slc = k_slc_b.shape[2]
    n_sel = min(N_SELECT, n_slc)
    scale = HEAD_DIM ** -0.5

    proj = h @ w_in
    q = rmsnorm(proj[..., :N_HEADS * HEAD_DIM].reshape(B, S, N_HEADS, HEAD_DIM), q_norm)
    gates = jax.nn.sigmoid(proj[..., N_HEADS * HEAD_DIM:].astype(jnp.float32))
    q_blocks = q.reshape(B, n_qb, Q_BLOCK, N_KV_GROUPS, HEADS_PER_GROUP, HEAD_DIM).transpose(1, 0, 3, 4, 2, 5)
    g_blocks = gates.reshape(B, n_qb, Q_BLOCK, N_KV_GROUPS, HEADS_PER_GROUP, N_BRANCH).transpose(1, 0, 3, 4, 2, 5)
    starts = jnp.arange(n_qb, dtype=jnp.int32) * Q_BLOCK

    cmp_end = jnp.arange(n_cmp, dtype=jnp.int32) * CMP_STRIDE + CMP_BLOCK - 1
    c0 = jnp.arange(n_cmp, dtype=jnp.int32) * CMP_STRIDE
    s0 = jnp.arange(n_slc, dtype=jnp.int32) * SLC_BLOCK
    lo = jnp.maximum(c0[:, None], s0[None, :])
    hi = jnp.minimum(c0[:, None] + CMP_BLOCK, s0[None, :] + SLC_BLOCK)
    overlap = jnp.maximum(hi - lo, 0).astype(jnp.float32) / CMP_BLOCK
    blk = jnp.arange(n_slc, dtype=jnp.int32)
    bi = jnp.arange(B)[:, None, None, None]
    gi = jnp.arange(N_KV_GROUPS)[None, :, None, None]
    gi5 = jnp.arange(N_KV_GROUPS)[None, :, None, None, None]
    hi5 = jnp.arange(HEADS_PER_GROUP)[None, None, :, None, None]
    tab = jnp.transpose(rel_table.reshape(N_BUCKETS, N_KV_GROUPS, HEADS_PER_GROUP), (1, 2, 0))

    def block_fn(args):
        qb, gb, s = args
        t = s + jnp.arange(Q_BLOCK, dtype=jnp.int32)
        lc = jnp.einsum('bghqd,bgkd->bghqk', qb, k_cmp, preferred_element_type=jnp.float32) * scale \
            + rel_bias_heads(rel_table, t[:, None] - cmp_end[None, :])
        mc = cmp_end[None, :] <= t[:, None]
        pc = jax.nn.softmax(jnp.where(mc, lc, NEG_INF), axis=-1) * jnp.any(mc, axis=-1)[:, None]
        o_cmp = jnp.einsum('bghqk,bgkd->bghqd', pc.astype(v_cmp.dtype), v_cmp)
        imp = jnp.einsum('bghqk,kj->bgqj', pc, overlap)
        cur = t // SLC_BLOCK
        forced = (blk[None, :] == 0) | (blk[None, :] == cur[:, None]) | (blk[None, :] == cur[:, None] - 1)
        valid = blk[None, :] <= cur[:, None]
        score = jnp.where(forced, jnp.inf, jnp.where(valid, imp, -jnp.inf))
        _, sel = lax.top_k(score, n_sel)
        k_sel = k_slc_b[bi, gi, sel].reshape(B, N_KV_GROUPS, Q_BLOCK, n_sel * SLC_BLOCK, HEAD_DIM)
        v_sel = v_slc_b[bi, gi, sel].reshape(B, N_KV_GROUPS, Q_BLOCK, n_sel * SLC_BLOCK, HEAD_DIM)
        pos = (sel[..., None] * SLC_BLOCK + jnp.arange(SLC_BLOCK, dtype=jnp.int32)).reshape(B, N_KV_GROUPS, Q_BLOCK, -1)
        dist = t[None, None, :, None] - pos
        bias_s = tab[gi5, hi5, t5_bucket(dist)[:, :, None]].astype(jnp.float32)
        ls = jnp.einsum('bghqd,bgqkd->bghqk', qb, k_sel, preferred_element_type=jnp.float32) * scale + bias_s
        ps = jax.nn.softmax(jnp.where(dist[:, :, None] >= 0, ls, NEG_INF), axis=-1)
        o_slc = jnp.einsum('bghqk,bgqkd->bghqd', ps.astype(v_sel.dtype), v_sel)
        kw = lax.dynamic_slice_in_dim(k_win_pad, s, WINDOW + Q_BLOCK, axis=2)
        vw = lax.dynamic_slice_in_dim(v_win_pad, s, WINDOW + Q_BLOCK, axis=2)
        posw = s - WINDOW + jnp.arange(WINDOW + Q_BLOCK, dtype=jnp.int32)
        dw = t[:, None] - posw[None, :]
        mw = (dw >= 0) & (dw < WINDOW) & (posw[None, :] >= 0)
        lw = jnp.einsum('bghqd,bgkd->bghqk', qb, kw, preferred_element_type=jnp.float32) * scale \
            + rel_bias_heads(rel_table, dw)
        pw = jax.nn.softmax(jnp.where(mw, lw, NEG_INF), axis=-1)
        o_win = jnp.einsum('bghqk,bgkd->bghqd', pw.astype(vw.dtype), vw)
        out = gb[..., 0:1] * o_cmp + gb[..., 1:2] * o_slc + gb[..., 2:3] * o_win
        return out.astype(qb.dtype)

    o = lax.map(block_fn, (q_blocks, g_blocks, starts))
    o = o.transpose(1, 0, 4, 2, 3, 5).reshape(B, S, N_HEADS * HEAD_DIM)
    return o @ w_out


def _normal(key, shape, scale):
    return scale * jax.random.normal(key, shape, jnp.float32)


def _gain(key, shape):
    return 1.0 + 0.02 * jax.random.normal(key, shape, jnp.float32)


def setup_inputs(seed: int = 0) -> dict:
    key = jax.random.key(seed)
    ks = jax.random.split(key, 27)
    q_cols = N_HEADS * HEAD_DIM + N_HEADS * N_BRANCH
    kv_cols = 2 * N_BRANCH * N_KV_GROUPS * HEAD_DIM
    return {
        'x': jax.random.normal(ks[0], (BATCH, SEQ, D_MODEL), jnp.float32),
        'p': jax.random.normal(ks[1], (DEPTH, BATCH, SEQ, PLE_DIM), jnp.float32),
        'norm_mix': _gain(ks[2], (DEPTH, D_MODEL)),
        'norm_ffn': _gain(ks[3], (DEPTH, D_MODEL)),
        'norm_ple': _gain(ks[4], (DEPTH, D_MODEL)),
        'a_w_in': _normal(ks[5], (N_A_LAYERS, D_MODEL, 2 * GMLP_WIDTH), D_MODEL ** -0.5),
        'a_norm_v': _gain(ks[6], (N_A_LAYERS, GMLP_WIDTH)),
        'a_w_s': _normal(ks[7], (N_A_LAYERS, GMLP_GROUPS, GMLP_CHUNK, GMLP_CHUNK), GMLP_CHUNK ** -0.5),
        'a_b_s': 1.0 + _normal(ks[8], (N_A_LAYERS, GMLP_GROUPS, GMLP_CHUNK), 0.1),
        'a_w_out': _normal(ks[9], (N_A_LAYERS, GMLP_WIDTH, D_MODEL), GMLP_WIDTH ** -0.5),
        'kv_norm': _gain(ks[10], (D_MODEL,)),
        'kv_w': _normal(ks[11], (D_MODEL, kv_cols), D_MODEL ** -0.5),
        'k_norm': _gain(ks[12], (N_BRANCH, HEAD_DIM)),
        'cmp_pe_k': _normal(ks[13], (CMP_BLOCK, HEAD_DIM), 0.1),
        'cmp_pe_v': _normal(ks[14], (CMP_BLOCK, HEAD_DIM), 0.1),
        'cmp_wk1': _normal(ks[15], (CMP_BLOCK * HEAD_DIM, CMP_HIDDEN), (CMP_BLOCK * HEAD_DIM) ** -0.5),
        'cmp_wk2': _normal(ks[16], (CMP_HIDDEN, HEAD_DIM), CMP_HIDDEN ** -0.5),
        'cmp_wv1': _normal(ks[17], (CMP_BLOCK * HEAD_DIM, CMP_HIDDEN), (CMP_BLOCK * HEAD_DIM) ** -0.5),
        'cmp_wv2': _normal(ks[18], (CMP_HIDDEN, HEAD_DIM), CMP_HIDDEN ** -0.5),
        'b_w_in': _normal(ks[19], (N_B_LAYERS, D_MODEL, q_cols), D_MODEL ** -0.5),
        'b_q_norm': _gain(ks[20], (N_B_LAYERS, HEAD_DIM)),
        'b_w_out': _normal(ks[21], (N_B_LAYERS, N_HEADS * HEAD_DIM, D_MODEL), (N_HEADS * HEAD_DIM) ** -0.5),
        'rel_bias': _normal(ks[22], (N_BUCKETS, N_HEADS), 0.5),
        'ffn_w_in': _normal(ks[23], (DEPTH, D_MODEL, 2 * FFN_DIM), D_MODEL ** -0.5),
        'ffn_w_out': _normal(ks[24], (DEPTH, FFN_DIM, D_MODEL), FFN_DIM ** -0.5),
        'ple_w': _normal(ks[25], (DEPTH, PLE_DIM, D_MODEL), PLE_DIM ** -0.5),
        'ple_gate': _normal(ks[26], (DEPTH, D_MODEL, D_MODEL), D_MODEL ** -0.5),
    }


def reference(x, p, norm_mix, norm_ffn, norm_ple, a_w_in, a_norm_v, a_w_s, a_b_s, a_w_out,
              kv_norm, kv_w, k_norm, cmp_pe_k, cmp_pe_v, cmp_wk1, cmp_wk2, cmp_wv1, cmp_wv2,
              b_w_in, b_q_norm, b_w_out, rel_bias, ffn_w_in, ffn_w_out, ple_w, ple_gate):
    kvs = None
    for i in range(DEPTH):
        h = rmsnorm(x, norm_mix[i])
        if i < N_A_LAYERS:
            x = x + gmlp_mixer(h, a_w_in[i], a_norm_v[i], a_w_s[i], a_b_s[i], a_w_out[i])
        else:
            j = i - N_A_LAYERS
            x = x + nsa_mixer(h, *kvs, b_w_in[j], b_q_norm[j], rel_bias, b_w_out[j])
        x = x + swiglu(rmsnorm(x, norm_ffn[i]), ffn_w_in[i], ffn_w_out[i])
        x = ple_add(x, p[i], norm_ple[i], ple_w[i], ple_gate[i])
        if i == N_A_LAYERS - 1:
            kvs = shared_kv(x, kv_norm, kv_w, k_norm, cmp_pe_k, cmp_pe_v, cmp_wk1, cmp_wk2, cmp_wv1, cmp_wv2)
    return x
```

```python
import numpy as np
import ml_dtypes
import concourse.bass as bass
import concourse.mybir as mybir
from concourse.bass_utils import run_bass_kernel_spmd

F32 = mybir.dt.float32
BF16 = mybir.dt.bfloat16
AF = mybir.ActivationFunctionType
ALU = mybir.AluOpType

NCORES = 8
P = 128
SEQ = 8192
T = SEQ // NCORES
NT = T // P
D = 2048
KC = D // P
FF = 5632
HB = 11
PLE = 256
EPS = 1e-6
SLAB = 4096
NSLOT = 4

NV_MIX, NV_FFN, NV_PLE, NV_AV, NV_KV = 0, 4, 8, 12, 14
NV = 15


class Ctx:
    def __init__(self, nc):
        self.nc = nc
        self.E = {"pe": nc.tensor, "act": nc.scalar, "dve": nc.vector, "pool": nc.gpsimd, "sp": nc.sync}
        self.prog = {}
        self.seen = {}
        self.nsem = 0
        for e in ("pe", "act", "dve", "pool"):
            self.prog[e] = [self.sem("pg_" + e), 0]
        self.misc = [[self.sem("misc"), 0] for _ in range(12)]
        self.misc_i = 0

    def sem(self, name):
        self.nsem += 1
        return self.nc.alloc_semaphore(f"{name}_{self.nsem}")

    def mark(self, e, ins):
        p = self.prog[e]
        if p[1] >= 30000:
            p = self.prog[e] = [self.sem("pg_" + e), 0]
        p[1] += 1
        ins.then_inc(p[0], 1)
        return (p[0], p[1])

    def wait(self, e, *toks):
        for tok in toks:
            if tok is None:
                continue
            if isinstance(tok, list):
                self.wait(e, *tok)
                continue
            sem, val = tok
            k = (e, sem)
            if self.seen.get(k, 0) >= val:
                continue
            self.seen[k] = val
            self.E[e].wait_ge(sem, val)

    def dma(self, e, out, in_, waits=()):
        m = self.misc[self.misc_i % len(self.misc)]
        self.misc_i += 1
        if m[1] > 0:
            self.wait(e, (m[0], m[1]))
        self.wait(e, *waits)
        self.E[e].dma_start(out=out, in_=in_).then_inc(m[0], 16)
        m[1] += 16
        return (m[0], m[1])


class Ring:
    def __init__(self, c, nslot=NSLOT):
        self.c = c
        nc = c.nc
        self.n = nslot
        self.slots = [nc.alloc_sbuf_tensor(f"wslab{i}", [P, SLAB], BF16) for i in range(nslot)]
        self.sems = [c.sem("wr") for _ in range(nslot)]
        self.cnt = [0] * nslot
        self.free = [True] * nslot
        self.free_tok = [None] * nslot
        self.plan = []
        self.ready = {}
        self.issued = 0
        self.consumed = 0

    def add(self, src, kc, cols):
        self.plan.append((src, kc, cols))

    def view(self, s, kc, cols):
        return self.slots[s][:, 0:kc * cols].rearrange("p (k n) -> p k n", k=kc)

    def pump(self):
        while self.issued < len(self.plan):
            i = self.issued
            fs = [s for s in range(self.n) if self.free[s]]
            if not fs:
                break
            s = fs[0]
            src, kc, cols = self.plan[i]
            self.c.wait("pool", self.free_tok[s])
            self.c.nc.gpsimd.dma_start(out=self.view(s, kc, cols), in_=src).then_inc(self.sems[s], 16)
            self.cnt[s] += 1
            self.ready[i] = ((self.sems[s], 16 * self.cnt[s]), s)
            self.free[s] = False
            self.issued += 1

    def next(self):
        i = self.consumed
        self.consumed += 1
        self.pump()
        assert self.issued > i, "weight ring deadlock"
        tok, s = self.ready[i]
        _, kc, cols = self.plan[i]
        return self.view(s, kc, cols), tok, s

    def release(self, s, tok):
        self.free[s] = True
        self.free_tok[s] = tok
        self.pump()


class Psum:
    def __init__(self, c):
        self.c = c
        self.banks = [c.nc.alloc_psum_tensor(f"psb{i}", [P, 512], F32) for i in range(8)]
        self.free_tok = [None] * 8
        self.held = [False] * 8
        self.i = 0

    def get(self):
        for _ in range(8):
            b = self.i % 8
            self.i += 1
            if not self.held[b]:
                break
        else:
            raise RuntimeError("all PSUM banks held")
        self.held[b] = True
        self.c.wait("pe", self.free_tok[b])
        self.free_tok[b] = None
        return b

    def release(self, b, tok):
        self.held[b] = False
        self.free_tok[b] = tok


class Reg:
    def __init__(self, c):
        self.c = c
        self.toks = {}

    def acquire(self, e):
        self.c.wait(e, *[t for e2, t in self.toks.items() if e2 != e])

    def all(self):
        return list(self.toks.values())

    def done(self, e, tok):
        self.toks[e] = tok


def wslab(w2d, r0, nrows, c0, ncols):
    return w2d[r0:r0 + nrows, c0:c0 + ncols].rearrange("(kc p) n -> p kc n", p=P)


class Core:
    def __init__(self, nc):
        self.nc = nc
        self.c = Ctx(nc)
        self.ring = Ring(self.c)
        self.ps = Psum(self.c)
        a = nc.alloc_sbuf_tensor
        self.XT = a("XT", [P, KC, T], F32)
        self.HT = a("HT", [P, KC, T], BF16)
        self.B1 = a("B1", [P, KC * T], BF16)
        self.B2 = a("B2", [P, KC * T], BF16)
        self.SCR = a("SCR", [P, 6 * T], BF16)
        self.SQ = self.SCR[:, 0:2 * T].rearrange("p (a t) -> p a t", a=2)
        self.TMPB = self.SQ
        self.JUNK = self.SCR[:, 0:4 * T]
        self.BIAS = self.SCR[:, 0:4 * T].bitcast(F32)
        self.RSTD = self.SCR[:, 2 * T:4 * T].bitcast(F32)
        self.TMPF = self.RSTD.rearrange("p (a t) -> p a t", a=2)
        self.WST = self.SCR[:, 4 * T:6 * T].rearrange("p (g t) -> p g t", g=KC)
        self.PT = self.SCR[:, 4 * T:6 * T].rearrange("p (a t) -> p a t", a=2)
        self.rSQ, self.rRS, self.rX = Reg(self.c), Reg(self.c), Reg(self.c)
        self.NRM = a("NRM", [P, NV, KC], F32)
        self.ONES = a("ONES", [P, P], BF16)
        self.SMALL = a("SMALL", [P, 64], F32)
        self.EPSB = a("EPSB", [P, 1], F32)
        self.xt_tok = None
        self.ht_read = None
        self.ht_tok = None
        self.b1_read = None
        self.b2_read = None
        self.small_rd = None
        self.tmp_i = 0

    def rmsnorm(self, nv_idx):
        c, nc = self.c, self.nc
        pb = [self.ps.get(), self.ps.get()]
        self.rSQ.acquire("act")
        sq_rd = [None, None]
        last = None
        for k in range(KC):
            j = k % 2
            c.wait("act", self.xt_tok, sq_rd[j])
            ins = nc.scalar.activation(out=self.SQ[:, j, :], in_=self.XT[:, k, :], func=AF.Square)
            t = c.mark("act", ins)
            c.wait("pe", t)
            for h in range(2):
                ins = nc.tensor.matmul(self.ps.banks[pb[h]][:, :], self.ONES[:, :], self.SQ[:, j, h * 512:(h + 1) * 512],
                                       start=(k == 0), stop=(k == KC - 1))
            sq_rd[j] = last = c.mark("pe", ins)
        self.rSQ.done("act", t)
        self.rSQ.done("pe", last)
        c.wait("act", last)
        self.rRS.acquire("act")
        rel = None
        for h in range(2):
            sl = slice(h * 512, (h + 1) * 512)
            ins = nc.scalar.activation(out=self.RSTD[:, sl], in_=self.ps.banks[pb[h]][:, :], func=AF.Sqrt,
                                       bias=self.EPSB[:, 0:1], scale=1.0 / D)
            t = c.mark("act", ins)
            self.ps.release(pb[h], t)
            c.wait("dve", t)
            ins = nc.vector.reciprocal(out=self.RSTD[:, sl], in_=self.RSTD[:, sl])
            rel = c.mark("dve", ins)
        self.rRS.done("act", t)
        c.wait("dve", rel, self.ht_read, self.xt_tok)
        for k in range(KC):
            ins = nc.vector.scalar_tensor_tensor(out=self.HT[:, k, :], in0=self.XT[:, k, :],
                                                 scalar=self.NRM[:, nv_idx, k:k + 1], in1=self.RSTD[:, :],
                                                 op0=ALU.mult, op1=ALU.mult)
        self.ht_tok = c.mark("dve", ins)
        self.rRS.done("dve", self.ht_tok)
        return self.ht_tok

    def linear_fm(self, nslab, kc, rhs_fn, rhs_toks, epilogue):
        c, nc = self.c, self.nc
        last = None
        for s in range(nslab):
            view, rtok, slot = self.ring.next()
            c.wait("pe", rtok, *rhs_toks)
            ncol = view.shape[2] // P
            for mi in range(ncol):
                for h in range(2):
                    b = self.ps.get()
                    for k in range(kc):
                        ins = nc.tensor.matmul(self.ps.banks[b][:, :], view[:, k, mi * P:(mi + 1) * P], rhs_fn(k, h),
                                               start=(k == 0), stop=(k == kc - 1))
                    last = c.mark("pe", ins)
                    rel = epilogue(s * ncol + mi, h, self.ps.banks[b], last)
                    self.ps.release(b, rel)
            self.ring.release(slot, last)
        return last

    def add_to_xt(self, m, h, ps, tok):
        c, nc = self.c, self.nc
        sl = slice(h * 512, (h + 1) * 512)
        c.wait("dve", tok)
        ins = nc.vector.tensor_tensor(out=self.XT[:, m, sl], in0=ps[:, :], in1=self.XT[:, m, sl], op=ALU.add)
        self.xt_tok = c.mark("dve", ins)
        return self.xt_tok

    def plan_gmlp(self, w_in, w_out):
        for s in range(8):
            self.ring.add(wslab(w_in, 0, D, D + s * 256, 256), KC, 256)
        for s in range(8):
            self.ring.add(wslab(w_in, 0, D, s * 256, 256), KC, 256)
        for s in range(8):
            self.ring.add(wslab(w_out, 0, D, s * 256, 256), KC, 256)

    def gmlp(self, l, wsT_d, bs_d):
        c, nc = self.c, self.nc
        GV = self.B1[:, :].rearrange("p (t f) -> p t f", t=NT)
        SV = self.B2[:, :].rearrange("p (g t) -> p g t", g=KC)
        t_ws = c.dma("pool", self.WST[:, :, :], wsT_d[l].rearrange("g s t -> s g t"), waits=self.rX.all())
        c.wait("dve", t_ws, self.tri_tok)
        for g in range(KC):
            ins = nc.vector.tensor_tensor(out=self.WST[:, g, :], in0=self.WST[:, g, :], in1=self.TRI[:, :], op=ALU.mult)
        t_wst = c.mark("dve", ins)
        self.rX.done("dma", t_ws)
        self.rX.done("dve", t_wst)

        self.rmsnorm(NV_MIX + l)
        c.wait("act", self.b1_read)
        last = None
        for s in range(8):
            view, rtok, slot = self.ring.next()
            c.wait("pe", rtok, self.ht_tok)
            for t in range(NT):
                b = self.ps.get()
                for k in range(KC):
                    ins = nc.tensor.matmul(self.ps.banks[b][:, 0:256], self.HT[:, k, t * P:(t + 1) * P], view[:, k, :],
                                           start=(k == 0), stop=(k == KC - 1))
                last = c.mark("pe", ins)
                c.wait("act", last)
                ins = nc.scalar.activation(out=GV[:, t, s * 256:(s + 1) * 256], in_=self.ps.banks[b][:, 0:256],
                                           func=AF.Gelu_apprx_tanh)
                ta = c.mark("act", ins)
                self.ps.release(b, ta)
            self.ring.release(slot, last)
        SS = self.SMALL
        c.wait("dve", self.small_rd)
        ins = nc.vector.memset(SS[:, 0:16], 0.0)
        t0 = c.mark("dve", ins)
        c.wait("act", t0, ta)
        self.rSQ.acquire("act")
        self.rRS.acquire("act")
        for t in range(NT):
            ins = nc.scalar.activation(out=self.JUNK[:, 0:D], in_=GV[:, t, :], func=AF.Square, accum_out=SS[:, t:t + 1])
        t1 = c.mark("act", ins)
        self.rSQ.done("act", t1)
        self.rRS.done("act", t1)
        c.wait("act", t1)
        ins = nc.scalar.activation(out=SS[:, 8:16], in_=SS[:, 0:8], func=AF.Sqrt, bias=self.EPSB[:, 0:1], scale=1.0 / D)
        t1b = c.mark("act", ins)
        c.wait("dve", t1b)
        ins = nc.vector.reciprocal(out=SS[:, 8:16], in_=SS[:, 8:16])
        t2 = c.mark("dve", ins)
        c.wait("dve", t2)
        for t in range(NT):
            ins = nc.vector.tensor_scalar(out=GV[:, t, :], in0=GV[:, t, :], scalar1=SS[:, 8 + t:9 + t], scalar2=None,
                                          op0=ALU.mult)
        t_vn = c.mark("dve", ins)
        self.small_rd = t_vn
        t_bs = c.dma("sp", self.BIAS[:, :], bs_d[l:l + 1, :].to_broadcast([P, KC * P]), waits=self.rSQ.all() + self.rRS.all())
        self.rSQ.done("dma", t_bs)
        self.rRS.done("dma", t_bs)
        c.wait("pe", t_vn, t_wst)
        c.wait("dve", self.b2_read, t_bs)
        for t in range(NT):
            bb = [self.ps.get() for _ in range(4)]
            for g in range(KC):
                ins = nc.tensor.matmul(self.ps.banks[bb[g // 4]][:, (g % 4) * P:(g % 4 + 1) * P],
                                       GV[:, t, g * P:(g + 1) * P], self.WST[:, g, :], start=True, stop=True)
            tp = c.mark("pe", ins)
            c.wait("dve", tp)
            for g in range(KC):
                ins = nc.vector.scalar_tensor_tensor(out=SV[:, g, t * P:(t + 1) * P],
                                                     in0=self.ps.banks[bb[g // 4]][:, (g % 4) * P:(g % 4 + 1) * P],
                                                     scalar=self.NRM[:, NV_AV + l, g:g + 1],
                                                     in1=self.BIAS[:, g * P:(g + 1) * P], op0=ALU.mult, op1=ALU.add)
                if g % 4 == 3:
                    td = c.mark("dve", ins)
                    self.ps.release(bb[g // 4], td)
        self.b1_read = tp
        self.rX.done("pe", tp)
        self.rSQ.done("dve", td)
        self.rRS.done("dve", td)
        t_sv = td

        self.rSQ.acquire("act")
        tb_rd = [None, None]

        def epi_u(m, h, ps, tok):
            j = self.tmp_i % 2
            self.tmp_i += 1
            sl = slice(h * 512, (h + 1) * 512)
            c.wait("act", tok, tb_rd[j])
            ins = nc.scalar.activation(out=self.TMPB[:, j, 0:512], in_=ps[:, :], func=AF.Gelu_apprx_tanh)
            ta = c.mark("act", ins)
            self.rSQ.done("act", ta)
            c.wait("dve", ta, t_sv)
            ins = nc.vector.tensor_tensor(out=SV[:, m, sl], in0=self.TMPB[:, j, 0:512], in1=SV[:, m, sl], op=ALU.mult)
            tb_rd[j] = self.gt_tok = c.mark("dve", ins)
            self.rSQ.done("dve", self.gt_tok)
            return ta

        last = self.linear_fm(8, KC, lambda k, h: self.HT[:, k, h * 512:(h + 1) * 512], [self.ht_tok], epi_u)
        self.ht_read = last
        last = self.linear_fm(8, KC, lambda k, h: SV[:, k, h * 512:(h + 1) * 512], [self.gt_tok], self.add_to_xt)
        self.b2_read = last

    def plan_ffn(self, w_in, w_out):
        for hb in range(HB):
            for j in range(2):
                self.ring.add(wslab(w_in, 0, D, hb * 512 + j * 256, 256), KC, 256)
                self.ring.add(wslab(w_in, 0, D, FF + hb * 512 + j * 256, 256), KC, 256)
            for j in range(2):
                self.ring.add(wslab(w_out, hb * 512 + j * 256, 256, 0, D), 2, D)

    def ffn(self, l):
        c, nc = self.c, self.nc
        self.rmsnorm(NV_FFN + l)
        AB = self.B1[:, 0:3 * 4 * T].rearrange("p (r k t) -> p r k t", r=3, k=4)
        ab_rd = [self.b1_read, self.b1_read, self.b1_read]
        sg_rd = [None, None]
        SG = self.TMPB
        self.rSQ.acquire("act")
        rhs = lambda k, h: self.HT[:, k, h * 512:(h + 1) * 512]
        last_pe = None
        tp = None
        for hb in range(HB):
            r = hb % 3
            for j in range(2):
                def epi_g(m, h, ps, tok):
                    c.wait("act", tok, sg_rd[m] if h == 0 else None)
                    ins = nc.scalar.activation(out=SG[:, m, h * 512:(h + 1) * 512], in_=ps[:, :], func=AF.Silu)
                    self.sg_tok = c.mark("act", ins)
                    return self.sg_tok

                def epi_u(m, h, ps, tok, j=j, r=r):
                    c.wait("dve", tok, self.sg_tok, ab_rd[r])
                    ins = nc.vector.tensor_tensor(out=AB[:, r, 2 * j + m, h * 512:(h + 1) * 512], in0=ps[:, :],
                                                  in1=SG[:, m, h * 512:(h + 1) * 512], op=ALU.mult)
                    t = c.mark("dve", ins)
                    sg_rd[m] = t
                    self.ab_tok = t
                    return t

                self.linear_fm(1, KC, rhs, [self.ht_tok], epi_g)
                last_pe = self.linear_fm(1, KC, rhs, [self.ht_tok], epi_u)
            vA, rA, sA = self.ring.next()
            vB, rB, sB = self.ring.next()
            c.wait("pe", rA, rB, self.ab_tok)
            for m in range(KC):
                for h in range(2):
                    b = self.ps.get()
                    for kk in range(4):
                        v = vA if kk < 2 else vB
                        ins = nc.tensor.matmul(self.ps.banks[b][:, :], v[:, kk % 2, m * P:(m + 1) * P],
                                               AB[:, r, kk, h * 512:(h + 1) * 512], start=(kk == 0), stop=(kk == 3))
                    tp = c.mark("pe", ins)
                    rel = self.add_to_xt(m, h, self.ps.banks[b], tp)
                    self.ps.release(b, rel)
            self.ring.release(sA, tp)
            self.ring.release(sB, tp)
            ab_rd[r] = tp
        self.ht_read = last_pe
        self.b1_read = tp
        self.rSQ.done("act", self.sg_tok)
        self.rSQ.done("dve", self.ab_tok)

    def plan_ple(self, w_proj, w_gate):
        self.ring.add(wslab(w_proj, 0, PLE, 0, D), 2, D)
        for s in range(8):
            self.ring.add(wslab(w_gate, 0, D, s * 256, 256), KC, 256)

    def ple_load(self, pT_l):
        c = self.c
        self.pt_tok = c.dma("pool", self.PT[:, :, :], pT_l.rearrange("(kc p) t -> p kc t", p=P), waits=self.rX.all())
        self.rX.done("dma", self.pt_tok)

    def ple(self, l):
        c, nc = self.c, self.nc
        self.rmsnorm(NV_PLE + l)
        vW, rW, sW = self.ring.next()
        c.wait("pe", rW, self.pt_tok)
        state = {}
        self.rRS.acquire("act")
        tf_rd = [None, None]

        def epi(m, h, ps, tok):
            j = self.tmp_i % 2
            self.tmp_i += 1
            sl = slice(h * 512, (h + 1) * 512)
            c.wait("act", tok, tf_rd[j])
            ins = nc.scalar.activation(out=self.TMPF[:, j, :], in_=ps[:, :], func=AF.Sigmoid)
            ta = c.mark("act", ins)
            self.rRS.done("act", ta)
            b = self.ps.get()
            for k in range(2):
                ins = nc.tensor.matmul(self.ps.banks[b][:, :], vW[:, k, m * P:(m + 1) * P], self.PT[:, k, sl],
                                       start=(k == 0), stop=(k == 1))
            tp = c.mark("pe", ins)
            state["tp"] = tp
            c.wait("dve", ta, tp)
            ins = nc.vector.tensor_tensor(out=self.TMPF[:, j, :], in0=self.ps.banks[b][:, :], in1=self.TMPF[:, j, :], op=ALU.mult)
            t1 = c.mark("dve", ins)
            self.ps.release(b, t1)
            c.wait("dve", t1)
            ins = nc.vector.tensor_tensor(out=self.XT[:, m, sl], in0=self.TMPF[:, j, :], in1=self.XT[:, m, sl], op=ALU.add)
            self.xt_tok = tf_rd[j] = c.mark("dve", ins)
            self.rRS.done("dve", self.xt_tok)
            return ta

        last = self.linear_fm(8, KC, lambda k, h: self.HT[:, k, h * 512:(h + 1) * 512], [self.ht_tok], epi)
        self.ring.release(sW, state["tp"])
        self.ht_read = last
        self.rX.done("pe", state["tp"])

    def plan_kv(self, kv_w):
        for s in range(6):
            self.ring.add(wslab(kv_w, 0, D, s * 256, 256), KC, 256)

    def kv_proj(self, knorm_tile, kvT_d, kvV_d):
        c, nc = self.c, self.nc
        self.rmsnorm(NV_KV)
        KVT = self.B2[:, 0:4 * 2 * T].rearrange("p (i g t) -> p i g t", i=4, g=2)
        KVV = self.B1[:, 0:2 * NT * 256].rearrange("p (i t f) -> p i t f", i=2, t=NT)
        c.wait("act", self.b1_read, self.b2_read)
        c.wait("dve", self.b1_read, self.b2_read)
        self.rSQ.acquire("act")
        self.rRS.acquire("act")
        tb_rd = [None, None]
        tf_rd = [None, None]
        fm_idx = {0: 0, 1: 1, 2: 2, 4: 3}
        tm_idx = {3: 0, 5: 1}
        outs = []
        last = None
        for kind in range(6):
            view, rtok, slot = self.ring.next()
            c.wait("pe", rtok, self.ht_tok)
            if kind in tm_idx:
                for t in range(NT):
                    b = self.ps.get()
                    for k in range(KC):
                        ins = nc.tensor.matmul(self.ps.banks[b][:, 0:256], self.HT[:, k, t * P:(t + 1) * P], view[:, k, :],
                                               start=(k == 0), stop=(k == KC - 1))
                    last = c.mark("pe", ins)
                    c.wait("act", last)
                    ins = nc.scalar.copy(out=KVV[:, tm_idx[kind], t, :], in_=self.ps.banks[b][:, 0:256])
                    ta = c.mark("act", ins)
                    self.ps.release(b, ta)
                outs.append(c.dma("sp", kvV_d[tm_idx[kind]].rearrange("(t p) f -> p t f", p=P), KVV[:, tm_idx[kind], :, :], waits=[ta]))
            else:
                i = fm_idx[kind]
                for g in range(2):
                    for h in range(2):
                        sl = slice(h * 512, (h + 1) * 512)
                        b = self.ps.get()
                        for k in range(KC):
                            ins = nc.tensor.matmul(self.ps.banks[b][:, :], view[:, k, g * P:(g + 1) * P], self.HT[:, k, sl],
                                                   start=(k == 0), stop=(k == KC - 1))
                        last = c.mark("pe", ins)
                        if kind in (0, 1):
                            c.wait("act", last)
                            ins = nc.scalar.copy(out=KVT[:, i, g, sl], in_=self.ps.banks[b][:, :])
                            ta = c.mark("act", ins)
                            self.ps.release(b, ta)
                        else:
                            j = self.tmp_i % 2
                            self.tmp_i += 1
                            c.wait("act", last, tb_rd[j])
                            ins = nc.scalar.activation(out=self.TMPB[:, j, 0:512], in_=self.ps.banks[b][:, :], func=AF.Square)
                            t1 = c.mark("act", ins)
                            b2 = self.ps.get()
                            c.wait("pe", t1)
                            ins = nc.tensor.matmul(self.ps.banks[b2][:, :], self.ONES[:, :], self.TMPB[:, j, 0:512], start=True, stop=True)
                            t2 = c.mark("pe", ins)
                            tb_rd[j] = t2
                            c.wait("act", t2, tf_rd[j])
                            ins = nc.scalar.activation(out=self.TMPF[:, j, :], in_=self.ps.banks[b2][:, :], func=AF.Sqrt,
                                                       bias=self.EPSB[:, 0:1], scale=1.0 / P)
                            t3 = c.mark("act", ins)
                            self.ps.release(b2, t3)
                            c.wait("dve", t3)
                            ins = nc.vector.reciprocal(out=self.TMPF[:, j, :], in_=self.TMPF[:, j, :])
                            t4 = c.mark("dve", ins)
                            c.wait("dve", t4)
                            kn = 1 if kind == 2 else 2
                            ins = nc.vector.scalar_tensor_tensor(out=KVT[:, i, g, sl], in0=self.ps.banks[b][:, :],
                                                                 scalar=knorm_tile[:, kn:kn + 1],
                                                                 in1=self.TMPF[:, j, :], op0=ALU.mult, op1=ALU.mult)
                            ta = c.mark("dve", ins)
                            tf_rd[j] = ta
                            self.ps.release(b, ta)
                            self.rSQ.done("act", t1); self.rSQ.done("pe", t2)
                            self.rRS.done("act", t3); self.rRS.done("dve", ta)
                outs.append(c.dma("sp", kvT_d[i], KVT[:, i, :, :], waits=[ta]))
            self.ring.release(slot, last)
        self.ht_read = last
        return outs

    def setup_common(self, nrm_d, tri_d=None):
        c, nc = self.c, self.nc
        a = nc.alloc_sbuf_tensor
        ins = nc.vector.memset(self.ONES[:, :], 1.0)
        ins = nc.vector.memset(self.EPSB[:, :], EPS)
        t = c.mark("dve", ins)
        c.wait("act", t)
        c.wait("pe", t)
        t_n = c.dma("sp", self.NRM[:, :, :], nrm_d)
        c.wait("dve", t_n)
        if tri_d is not None:
            self.TRI = a("TRI", [P, P], BF16)
            self.tri_tok = c.dma("pool", self.TRI[:, :], tri_d)


def build_A(n_layers=2, do_kv=True, stop=None):
    nc = bass.Bass("TRN2", target_bir_lowering=False)
    dt = lambda name, shape, kind="ExternalInput", dtype=F32: nc.dram_tensor(name, list(shape), dtype, kind=kind).ap()
    xT_d = dt("xT", [D, T])
    pT_d = dt("pT", [4, PLE, T])
    nrm_d = dt("nrm", [P, NV, KC])
    tri_d = dt("tri", [P, P])
    knorm_d = dt("knormT", [P, 3])
    a_w_in = dt("a_w_in", [2, D, 2 * D])
    a_wsT = dt("a_wsT", [2, KC, P, P])
    a_b_s = dt("a_b_s", [2, KC * P])
    a_w_out = dt("a_w_out", [2, D, D])
    ffn_w_in = dt("ffn_w_in", [4, D, 2 * FF])
    ffn_w_out = dt("ffn_w_out", [4, FF, D])
    ple_w = dt("ple_w", [4, PLE, D])
    ple_gate = dt("ple_gate", [4, D, D])
    kv_w = dt("kv_w", [D, 1536])
    xo_d = dt("xT_out", [D, T], kind="ExternalOutput")
    kvT_d = dt("kvT", [4, P, 2, T], kind="ExternalOutput", dtype=BF16)
    kvV_d = dt("kvV", [2, T, 256], kind="ExternalOutput", dtype=BF16)

    k = Core(nc)
    c = k.c
    for l in range(n_layers):
        lastl = (l == n_layers - 1)
        k.plan_gmlp(a_w_in[l], a_w_out[l])
        if lastl and stop == "gmlp":
            break
        k.plan_ffn(ffn_w_in[l], ffn_w_out[l])
        if lastl and stop == "ffn":
            break
        k.plan_ple(ple_w[l], ple_gate[l])
    if do_kv:
        k.plan_kv(kv_w)
    k.setup_common(nrm_d, tri_d)
    KN = nc.alloc_sbuf_tensor("KN", [P, 3], F32)
    t_kn = c.dma("sp", KN[:, :], knorm_d)
    c.wait("dve", t_kn)
    k.xt_tok = c.dma("sp", k.XT[:, :, :], xT_d.rearrange("(kc p) t -> p kc t", p=P))
    c.wait("dve", k.xt_tok)
    for l in range(n_layers):
        lastl = (l == n_layers - 1)
        k.gmlp(l, a_wsT, a_b_s)
        if lastl and stop == "gmlp":
            break
        k.ple_load(pT_d[l])
        k.ffn(l)
        if lastl and stop == "ffn":
            break
        k.ple(l)
    outs = [c.dma("sp", xo_d.rearrange("(kc p) t -> p kc t", p=P), k.XT[:, :, :], waits=[k.xt_tok])]
    if do_kv:
        outs += k.kv_proj(KN, kvT_d, kvV_d)
    c.wait("sp", *outs)
    return nc


def host_inputs_A(inputs, core):
    f = lambda a: np.ascontiguousarray(a, dtype=np.float32)
    tok = slice(core * T, (core + 1) * T)
    x = inputs["x"][0, tok, :]
    p = inputs["p"][:, 0, tok, :]
    vecs = np.zeros((NV, D), np.float32)
    vecs[NV_MIX:NV_MIX + 4] = inputs["norm_mix"]
    vecs[NV_FFN:NV_FFN + 4] = inputs["norm_ffn"]
    vecs[NV_PLE:NV_PLE + 4] = inputs["norm_ple"]
    vecs[NV_AV:NV_AV + 2] = inputs["a_norm_v"]
    vecs[NV_KV] = inputs["kv_norm"]
    nrm = vecs.reshape(NV, KC, P).transpose(2, 0, 1)
    tri = (np.arange(P)[:, None] <= np.arange(P)[None, :]).astype(np.float32)
    return {
        "xT": f(x.T), "pT": f(p.transpose(0, 2, 1)), "nrm": f(nrm), "tri": tri,
        "knormT": f(inputs["k_norm"].T),
        "a_w_in": f(inputs["a_w_in"]), "a_wsT": f(np.transpose(inputs["a_w_s"], (0, 1, 3, 2))),
        "a_b_s": f(inputs["a_b_s"].reshape(2, KC * P)), "a_w_out": f(inputs["a_w_out"]),
        "ffn_w_in": f(inputs["ffn_w_in"]), "ffn_w_out": f(inputs["ffn_w_out"]),
        "ple_w": f(inputs["ple_w"]), "ple_gate": f(inputs["ple_gate"]), "kv_w": f(inputs["kv_w"]),
    }


NEG = -30000.0
HSCALE = 128.0 ** -0.5
NCMP = 511
NB = 33


class Flat:
    def __init__(self, ap):
        self.ap = ap
        self.off = 0

    def take(self, n):
        v = self.ap[:, self.off:self.off + n]
        self.off += n
        assert self.off <= self.ap.shape[1], (self.off, self.ap.shape)
        return v


class CoreB(Core):
    def carve(self):
        X = Flat(self.XT[:, :, :].rearrange("p a t -> p (a t)").bitcast(BF16))
        r = lambda v, s, **kw: v.rearrange(s, **kw)
        self.D0 = r(X.take(2048), "p (h t) -> p h t", h=16)
        self.D1 = r(X.take(2048), "p (h t) -> p h t", h=16)
        self.D4 = X.take(128)
        self.FC = r(X.take(2048), "p (h t) -> p h t", h=16)
        self.SHC = r(X.take(4096), "p (j c i) -> p j c i", j=8, c=4)
        self.SELF = r(X.take(4096), "p (k s) -> p k s", k=32)
        self.SELN = r(X.take(2048), "p (j r s) -> p j r s", j=8, r=2)
        self.SELG = r(X.take(6144), "p (c m) -> p c m", c=48)
        self.FV = r(X.take(1024), "p (j b) -> p j b", j=8)
        self.OV = r(X.take(512), "p (c b) -> p c b", c=4)
        self.KCT = r(X.take(1024), "p (g i) -> p g i", g=2)
        self.VC = r(X.take(1024), "p (c g d) -> p c g d", c=4, g=2)
        self.KSL = r(X.take(2304), "p (g t) -> p g t", g=2)
        self.VSL = r(X.take(2304), "p (t f) -> p t f", t=9)
        self.IDENT = X.take(128)
        Y = Flat(self.B2[:, :])
        self.KWL = r(Y.take(3072), "p (g t) -> p g t", g=2)
        self.VWL = r(Y.take(3072), "p (t f) -> p t f", t=12)
        self.KB = Y.take(1088).bitcast(F32)
        self.PC = r(Y.take(2048), "p (c n) -> p c n", c=4)
        self.PR = r(Y.take(1536), "p (c n) -> p c n", c=3)
        self.ACC = r(Y.take(2048).bitcast(F32), "p (a n) -> p a n", a=2)
        self.RINV = Y.take(1024).bitcast(F32)
        self.GREP = Y.take(1024).bitcast(F32)
        self.SCORE = Y.take(256).bitcast(F32)
        self.WORK = Y.take(256).bitcast(F32)
        self.SEL = Y.take(256).bitcast(F32)
        self.M8 = Y.take(32).bitcast(F32)
        self.SNT = Y.take(128)
        self.SELB = Y.take(128)
        self.GS = self.SCR[:, 4 * T:5 * T]
        self.QT = self.B1[:, :].rearrange("p (h t) -> p h t", h=16)
        self.OT = self.HT

    def setup_B(self, d):
        c, nc = self.c, self.nc
        X = Flat(self.XT[:, :, :].rearrange("p a t -> p (a t)").bitcast(BF16))
        MM = X.take(NB * 128).rearrange("p (b t) -> p b t", b=NB)
        TAB = X.take(2 * NB * 16).bitcast(F32).rearrange("p (b h) -> p b h", b=NB)
        ACCD = X.take(2 * 2048).bitcast(F32).rearrange("p (h t) -> p h t", h=16)
        OUTB = X.take(2048).rearrange("p (h t) -> p h t", h=16)
        HID = X.take(1024).rearrange("p (c i) -> p c i", c=2)
        PET = X.take(64).rearrange("p (a j) -> p a j", a=2)
        CB = X.take(8).bitcast(F32)
        KST = X.take(1024).rearrange("p (g i) -> p g i", g=2)
        VST = X.take(1024).rearrange("p (c g d) -> p c g d", c=4, g=2)
        SQT = X.take(512)
        RS = X.take(1024).bitcast(F32)
        TT = self.B1[:, :].rearrange("p (a t) -> p a t", a=2)
        t_tab = c.dma("sp", TAB[:, 0:32, :], d["relb"].to_broadcast([P, 512]).rearrange("p (b h) -> p b h", b=32))
        c.wait("dve", t_tab)
        for b in range(31):
            ins = nc.vector.tensor_tensor(out=TAB[:, b, :], in0=TAB[:, b, :], in1=TAB[:, 31, :], op=ALU.subtract)
        ins = nc.vector.memset(TAB[:, 31, :], 0.0)
        ins = nc.vector.memset(TAB[:, 32, :], NEG)
        t_prev = c.mark("dve", ins)
        dma_out = []
        for name, rows, dst in (("M0", P, d["D0"]), ("M1", P, d["D1"]), ("MC", 17, d["FCd"])):
            t_m = c.dma("pool", MM[0:rows, :, :], d[name], waits=[t_prev] + dma_out[-1:])
            c.wait("dve", t_m, t_prev)
            for h in range(16):
                ins = nc.vector.tensor_scalar(out=ACCD[0:rows, h, :], in0=MM[0:rows, 0, :], scalar1=TAB[0:rows, 0, h:h + 1],
                                              scalar2=None, op0=ALU.mult)
                for b in range(1, NB):
                    ins = nc.vector.scalar_tensor_tensor(out=ACCD[0:rows, h, :], in0=MM[0:rows, b, :],
                                                         scalar=TAB[0:rows, b, h:h + 1], in1=ACCD[0:rows, h, :],
                                                         op0=ALU.mult, op1=ALU.add)
            t1 = c.mark("dve", ins)
            c.wait("dve", t1, *dma_out[-1:])
            ins = nc.vector.tensor_copy(out=OUTB[0:rows, :, :], in_=ACCD[0:rows, :, :])
            t_prev = c.mark("dve", ins)
            dma_out.append(c.dma("sp", dst, OUTB[0:rows, :, :], waits=[t_prev]))
        ins = nc.vector.memset(HID[:, :, :], 0.0)
        t_h0 = c.mark("dve", ins)
        ins = nc.vector.memset(KST[:, :, :], 0.0)
        t_h0 = c.mark("dve", ins)
        t_pe = c.dma("pool", PET[:, :, :], d["peT"])
        c.wait("pe", t_pe, t_h0)
        c.wait("act", t_h0)
        tt_rd = [None, None]
        n = 0
        st_tok = None
        for kvi in range(2):
            w1a, r1a, s1a = self.ring.next()
            w1b, r1b, s1b = self.ring.next()
            w2, r2, s2 = self.ring.next()
            c.wait("pe", r1a, r1b, r2)
            w1 = lambda j: (w1a if j < 16 else w1b)[:, j % 16, :]
            bcb = self.ps.get()
            for hc in range(2):
                for j in range(32):
                    ins = nc.tensor.matmul(self.ps.banks[bcb][:, hc:hc + 1], w1(j)[:, hc * P:(hc + 1) * P], PET[:, kvi, j:j + 1],
                                           start=(j == 0), stop=(j == 31))
            tcb = c.mark("pe", ins)
            c.wait("act", tcb)
            ins = nc.scalar.copy(out=CB[:, 2 * kvi:2 * kvi + 2], in_=self.ps.banks[bcb][:, 0:2])
            t_cb = c.mark("act", ins)
            self.ps.release(bcb, t_cb)
            for g in range(2):
                a = n % 2
                n += 1
                t_tt = c.dma("sp", TT[:, a, :], d["kcvT"][kvi, :, g, :], waits=[tt_rd[a]])
                c.wait("pe", t_tt)
                for hc in range(2):
                    b = self.ps.get()
                    for j in range(32):
                        rhs = TT[:, a, j:j + 16 * (NCMP - 1) + 1:16]
                        ins = nc.tensor.matmul(self.ps.banks[b][:, 0:NCMP], w1(j)[:, hc * P:(hc + 1) * P], rhs,
                                               start=(j == 0), stop=(j == 31))
                    tp = c.mark("pe", ins)
                    c.wait("act", tp, t_cb)
                    ins = nc.scalar.activation(out=HID[:, hc, 0:NCMP], in_=self.ps.banks[b][:, 0:NCMP], func=AF.Gelu_apprx_tanh,
                                               bias=CB[:, 2 * kvi + hc:2 * kvi + hc + 1])
                    th = c.mark("act", ins)
                    self.ps.release(b, th)
                tt_rd[a] = tp
                c.wait("pe", th)
                if kvi == 0:
                    b = self.ps.get()
                    for hc in range(2):
                        ins = nc.tensor.matmul(self.ps.banks[b][:, :], w2[:, hc, :], HID[:, hc, :], start=(hc == 0), stop=(hc == 1))
                    t1 = c.mark("pe", ins)
                    c.wait("act", t1)
                    ins = nc.scalar.activation(out=SQT[:, :], in_=self.ps.banks[b][:, :], func=AF.Square)
                    t2 = c.mark("act", ins)
                    b2 = self.ps.get()
                    c.wait("pe", t2)
                    ins = nc.tensor.matmul(self.ps.banks[b2][:, :], self.ONES[:, :], SQT[:, :], start=True, stop=True)
                    t3 = c.mark("pe", ins)
                    c.wait("act", t3)
                    ins = nc.scalar.activation(out=RS[:, :], in_=self.ps.banks[b2][:, :], func=AF.Sqrt, bias=self.EPSB[:, 0:1], scale=1.0 / P)
                    t4 = c.mark("act", ins)
                    self.ps.release(b2, t4)
                    c.wait("dve", t4)
                    ins = nc.vector.reciprocal(out=RS[:, :], in_=RS[:, :])
                    t5 = c.mark("dve", ins)
                    c.wait("dve", t5)
                    ins = nc.vector.scalar_tensor_tensor(out=KST[:, g, 0:NCMP], in0=self.ps.banks[b][:, 0:NCMP], scalar=self.KN[:, 0:1],
                                                         in1=RS[:, 0:NCMP], op0=ALU.mult, op1=ALU.mult)
                    st_tok = c.mark("dve", ins)
                    self.ps.release(b, st_tok)
                    c.wait("pe", st_tok)
                    c.wait("act", st_tok)
                else:
                    for ch in range(4):
                        b = self.ps.get()
                        for hc in range(2):
                            ins = nc.tensor.matmul(self.ps.banks[b][:, 0:P], HID[:, hc, ch * P:(ch + 1) * P], w2[:, hc, :],
                                                   start=(hc == 0), stop=(hc == 1))
                        t1 = c.mark("pe", ins)
                        c.wait("act", t1)
                        ins = nc.scalar.copy(out=VST[:, ch, g, :], in_=self.ps.banks[b][:, 0:P])
                        st_tok = c.mark("act", ins)
                        self.ps.release(b, st_tok)
                    c.wait("pe", st_tok)
            self.ring.release(s1a, tp)
            self.ring.release(s1b, tp)
            self.ring.release(s2, t1)
            if kvi == 0:
                dma_out.append(c.dma("sp", d["KCTd"], KST[:, :, :], waits=[st_tok]))
            else:
                dma_out.append(c.dma("sp", d["VCd"], VST[:, :, :, :], waits=[st_tok]))
        self.setup_done = dma_out
        self.b1_read = tp
        self.setup_toks = [t_prev, st_tok, tp] + dma_out

    def plan_setup_B(self, d):
        for kvi in range(2):
            w1 = d["cmp_w1"][kvi]
            self.ring.add(wslab(w1, 0, 2048, 0, 256), KC, 256)
            self.ring.add(wslab(w1, 2048, 2048, 0, 256), KC, 256)
            self.ring.add(wslab(d["cmp_w2"][kvi], 0, 256, 0, P), 2, P)

    def plan_nsa(self, w_in, w_out, d):
        for s in range(8):
            self.ring.add(wslab(w_in, 0, D, s * 256, 256), KC, 256)
        self.ring.add(wslab(w_in, 0, D, D, 48), KC, 48)
        for g in range(2):
            self.ring.add(d["ksT_full"][:, g, 0:4096].rearrange("p (a t) -> p a t", a=1), 1, 4096)
            self.ring.add(d["ksT_full"][:, g, 4096:8192].rearrange("p (a t) -> p a t", a=1), 1, 4096)
            self.ring.add(d["vs_full"][0:4096, g * P:(g + 1) * P].rearrange("(t p) f -> p t f", p=P), 32, P)
            self.ring.add(d["vs_full"][4096:8192, g * P:(g + 1) * P].rearrange("(t p) f -> p t f", p=P), 32, P)
        for s in range(8):
            self.ring.add(wslab(w_out, 0, D, s * 256, 256), KC, 256)

    def normed_evac(self, b, tok, out_ap, scal_ap, st):
        c, nc = self.c, self.nc
        j = self.tmp_i % 2
        self.tmp_i += 1
        c.wait("act", tok, st["tb"][j])
        ins = nc.scalar.activation(out=self.TMPB[:, j, 0:512], in_=self.ps.banks[b][:, :], func=AF.Square)
        t1 = c.mark("act", ins)
        b2 = self.ps.get()
        c.wait("pe", t1)
        ins = nc.tensor.matmul(self.ps.banks[b2][:, :], self.ONES[:, :], self.TMPB[:, j, 0:512], start=True, stop=True)
        t2 = c.mark("pe", ins)
        st["tb"][j] = t2
        c.wait("act", t2, st["tf"][j])
        ins = nc.scalar.activation(out=self.TMPF[:, j, :], in_=self.ps.banks[b2][:, :], func=AF.Sqrt, bias=self.EPSB[:, 0:1], scale=1.0 / P)
        t3 = c.mark("act", ins)
        self.ps.release(b2, t3)
        c.wait("dve", t3)
        ins = nc.vector.reciprocal(out=self.TMPF[:, j, :], in_=self.TMPF[:, j, :])
        t4 = c.mark("dve", ins)
        c.wait("dve", t4)
        ins = nc.vector.scalar_tensor_tensor(out=out_ap, in0=self.ps.banks[b][:, :], scalar=scal_ap, in1=self.TMPF[:, j, :],
                                             op0=ALU.mult, op1=ALU.mult)
        ta = c.mark("dve", ins)
        st["tf"][j] = ta
        self.rSQ.done("act", t1); self.rSQ.done("pe", t2)
        self.rRS.done("act", t3); self.rRS.done("dve", ta)
        return ta

    def attn_unit(self, Q4, tiles, keepP=None):
        c, nc = self.c, self.nc
        ob, rb = self.ps.get(), self.ps.get()
        n = len(tiles)

        def qk(i):
            t = tiles[i]
            sb = self.ps.get()
            adds = t.get("adds", [])
            s3 = self.ps.banks[sb][:, :].rearrange("p (h t) -> p h t", h=4)
            ins = nc.tensor.matmul(s3, t["kT"], Q4, start=True, stop=(len(adds) == 0))
            for ai, (l_, r_) in enumerate(adds):
                ins = nc.tensor.matmul(s3, l_, r_, start=False, stop=(ai == len(adds) - 1))
            return sb, c.mark("pe", ins)

        cur = qk(0)
        last = None
        for i in range(n):
            nxt = qk(i + 1) if i + 1 < n else None
            sb, tq = cur
            t = tiles[i]
            if keepP is not None:
                pbuf = keepP[:, i, :]
                prd = None
            else:
                s = self.pr_i % 3
                self.pr_i += 1
                pbuf = self.PR[:, s, :]
                prd = self.pr_rd[s]
            c.wait("act", tq, prd)
            if t.get("kb") is not None:
                ins = nc.scalar.activation(out=pbuf, in_=self.ps.banks[sb][:, :], func=AF.Exp, bias=t["kb"])
            else:
                ins = nc.scalar.activation(out=pbuf, in_=self.ps.banks[sb][:, :], func=AF.Exp)
            te = c.mark("act", ins)
            self.ps.release(sb, te)
            c.wait("pe", te)
            nc.tensor.matmul(self.ps.banks[ob][:, :], t["v"], pbuf, start=(i == 0), stop=(i == n - 1))
            ins = nc.tensor.matmul(self.ps.banks[rb][:, :], self.ONES[:, :], pbuf, start=(i == 0), stop=(i == n - 1))
            last = c.mark("pe", ins)
            if keepP is None:
                self.pr_rd[s] = last
            cur = nxt
        return ob, rb, last

    def combine(self, ob, rb, tok, hh, g, j, br, first):
        c, nc = self.c, self.nc
        gb = self.ps.get()
        c.wait("pe", self.gs_tok)
        for hi in range(4):
            h = 8 * g + 4 * hh + hi
            ins = nc.tensor.matmul(self.ps.banks[gb][:, hi * P:(hi + 1) * P], self.SELG[0:48, 3 * h + br, :],
                                   self.GS[0:48, j * P:(j + 1) * P], start=True, stop=True)
        tg = c.mark("pe", ins)
        c.wait("dve", tok, tg, self.rinv_rd)
        ins = nc.vector.tensor_scalar(out=self.RINV[:, :], in0=self.ps.banks[rb][:, :], scalar1=1e-30, scalar2=None, op0=ALU.add)
        t0 = c.mark("dve", ins)
        c.wait("dve", t0)
        ins = nc.vector.reciprocal(out=self.RINV[:, :], in_=self.RINV[:, :])
        t1 = c.mark("dve", ins)
        self.ps.release(rb, t1)
        c.wait("dve", t1)
        ins = nc.vector.tensor_tensor(out=self.GREP[:, :], in0=self.ps.banks[gb][:, :], in1=self.RINV[:, :], op=ALU.mult)
        t2 = c.mark("dve", ins)
        self.ps.release(gb, t2)
        c.wait("dve", t2)
        if first:
            ins = nc.vector.tensor_tensor(out=self.ACC[:, hh, :], in0=self.ps.banks[ob][:, :], in1=self.GREP[:, :], op=ALU.mult)
            t3 = c.mark("dve", ins)
        else:
            ins = nc.vector.tensor_tensor(out=self.GREP[:, :], in0=self.ps.banks[ob][:, :], in1=self.GREP[:, :], op=ALU.mult)
            t3a = c.mark("dve", ins)
            c.wait("dve", t3a)
            ins = nc.vector.tensor_tensor(out=self.ACC[:, hh, :], in0=self.ACC[:, hh, :], in1=self.GREP[:, :], op=ALU.add)
            t3 = c.mark("dve", ins)
        self.ps.release(ob, t3)
        return t1, t3

    def nsa(self, l, jl, d):
        c, nc = self.c, self.nc
        self.rmsnorm(NV_MIX + l)
        t_spill = c.dma("sp", d["xsp"], self.XT[:, :, :], waits=[self.ht_tok, self.xt_tok])
        st = {"tb": [None, None], "tf": [None, None]}
        self.rSQ.acquire("act")
        self.rRS.acquire("act")
        c.wait("dve", self.b1_read)

        def epi_q(m, h, ps, tok):
            b = self.ps.banks.index(ps)
            return self.normed_evac(b, tok, self.QT[:, m, h * 512:(h + 1) * 512], self.QN[:, jl:jl + 1], st)

        rhs = lambda k, h: self.HT[:, k, h * 512:(h + 1) * 512]
        self.linear_fm(8, KC, rhs, [self.ht_tok], epi_q)
        view, rtok, slot = self.ring.next()
        c.wait("pe", rtok)
        self.rX.acquire("act")
        for h in range(2):
            b = self.ps.get()
            for k in range(KC):
                ins = nc.tensor.matmul(self.ps.banks[b][0:48, :], view[:, k, :], rhs(k, h), start=(k == 0), stop=(k == KC - 1))
            tp = c.mark("pe", ins)
            c.wait("act", tp)
            ins = nc.scalar.activation(out=self.GS[0:48, h * 512:(h + 1) * 512], in_=self.ps.banks[b][0:48, :], func=AF.Sigmoid)
            self.gs_tok = c.mark("act", ins)
            self.ps.release(b, self.gs_tok)
        self.ring.release(slot, tp)
        self.ht_read = tp
        self.rX.done("act", self.gs_tok)
        q_tok = st["tf"][0], st["tf"][1]
        ld = []
        w8 = [t_spill, self.b2_read]
        L = lambda dst, src: ld.append(c.dma("sp", dst, src, waits=w8))
        L(self.D0[:, :, :], d["D0"]); L(self.D1[:, :, :], d["D1"]); L(self.D4, d["D4"])
        L(self.FC[0:17, :, :], d["FCd"]); L(self.SHC[0:17, :, :, :], d["SHC"]); L(self.SELF[:, :, :], d["SELF"])
        L(self.SELN[:, :, :, :], d["SELN"]); L(self.SELG[0:48, :, :], d["SELG"]); L(self.FV[:, :, :], d["FV"])
        L(self.OV[:, :, :], d["OV"]); L(self.KCT[:, :, :], d["KCTd"]); L(self.VC[:, :, :, :], d["VCd"])
        L(self.KSL[:, :, :], d["ksT_loc"]); L(self.VSL[:, :, :], d["vs_loc"].rearrange("(t p) f -> p t f", p=P))
        L(self.IDENT, d["IDENT"])
        L(self.KWL[:, :, :], d["kwT_loc"]); L(self.VWL[:, :, :], d["vw_loc"].rearrange("(t p) f -> p t f", p=P))
        L(self.KB, d["KB"])
        for e in ("pe", "act", "dve"):
            c.wait(e, *ld)
        c.wait("pe", *q_tok)
        c.wait("act", self.ht_read)
        c.wait("dve", self.ht_read)
        self.pr_i = 0
        self.pr_rd = [None, None, None]
        self.rinv_rd = None
        KBW = self.KB[:, 0:12]
        KBS = self.KB[:, 12:21]
        KBF = self.KB[:, 21:21 + 512].rearrange("p (j k) -> p j k", j=8)
        last_ot = None
        last_far = None
        self.pc_rd = None
        self.sel_rd = None
        for g in range(2):
            kA, rkA, skA = self.ring.next()
            kB, rkB, skB = self.ring.next()
            vA, rvA, svA = self.ring.next()
            vB, rvB, svB = self.ring.next()
            c.wait("pe", rkA, rkB, rvA, rvB)
            for j in range(NT):
                ib = self.ps.get()
                acc_t = [None, None]
                for hh in range(2):
                    h0 = 8 * g + 4 * hh
                    Q4 = self.QT[:, h0:h0 + 4, j * P:(j + 1) * P]
                    tiles = [dict(kT=self.KCT[:, g, ch * P:(ch + 1) * P], v=self.VC[:, ch, g, :],
                                  adds=[(self.SHC[0:17, j, ch, :], self.FC[0:17, h0:h0 + 4, :])]) for ch in range(4)]
                    c.wait("act", self.pc_rd)
                    ob, rb, tk = self.attn_unit(Q4, tiles, keepP=self.PC)
                    t1, t3 = self.combine(ob, rb, tk, hh, g, j, 0, True)
                    acc_t[hh] = t3
                    c.wait("dve", t1)
                    for ch in range(4):
                        ins = nc.vector.tensor_tensor(out=self.PC[:, ch, :], in0=self.PC[:, ch, :], in1=self.RINV[:, :], op=ALU.mult)
                    tn = c.mark("dve", ins)
                    self.rinv_rd = tn
                    c.wait("pe", tn)
                    for hi in range(4):
                        for ch in range(4):
                            ins = nc.tensor.matmul(self.ps.banks[ib][:, 0:P], self.PC[:, ch, hi * P:(hi + 1) * P], self.OV[:, ch, :],
                                                   start=(hh == 0 and hi == 0 and ch == 0), stop=(hh == 1 and hi == 3 and ch == 3))
                    self.pc_rd = c.mark("pe", ins)
                pc_rd = self.pc_rd
                c.wait("dve", pc_rd, self.sel_rd)
                ins = nc.vector.tensor_tensor(out=self.SCORE, in0=self.ps.banks[ib][:, 0:P], in1=self.FV[:, j, :], op=ALU.add)
                ts = c.mark("dve", ins)
                self.ps.release(ib, ts)
                c.wait("dve", ts)
                ins = nc.vector.max(out=self.M8[:, 0:8], in_=self.SCORE)
                ts = c.mark("dve", ins); c.wait("dve", ts)
                ins = nc.vector.match_replace(out=self.WORK, in_to_replace=self.M8[:, 0:8], in_values=self.SCORE, imm_value=-1e38)
                ts = c.mark("dve", ins); c.wait("dve", ts)
                ins = nc.vector.max(out=self.M8[:, 8:16], in_=self.WORK)
                ts = c.mark("dve", ins); c.wait("dve", ts)
                ins = nc.vector.match_replace(out=self.WORK, in_to_replace=self.M8[:, 8:16], in_values=self.WORK, imm_value=-1e38)
                ts = c.mark("dve", ins); c.wait("dve", ts)
                ins = nc.vector.tensor_tensor(out=self.WORK, in0=self.SCORE, in1=self.WORK, op=ALU.subtract)
                ts = c.mark("dve", ins); c.wait("dve", ts)
                ins = nc.vector.tensor_scalar(out=self.WORK, in0=self.WORK, scalar1=1.0, scalar2=None, op0=ALU.min)
                ts = c.mark("dve", ins); c.wait("dve", ts)
                ins = nc.vector.tensor_scalar(out=self.SEL, in0=self.SCORE, scalar1=-1e29, scalar2=None, op0=ALU.is_gt)
                ts = c.mark("dve", ins); c.wait("dve", ts)
                ins = nc.vector.tensor_tensor(out=self.SEL, in0=self.SEL, in1=self.WORK, op=ALU.mult)
                ts = c.mark("dve", ins); c.wait("dve", ts)
                ins = nc.vector.tensor_scalar(out=self.SELB, in0=self.SEL, scalar1=-NEG, scalar2=NEG, op0=ALU.mult, op1=ALU.add)
                ts = c.mark("dve", ins)
                tb_ = self.ps.get()
                c.wait("pe", ts)
                ins = nc.tensor.matmul(self.ps.banks[tb_][:, 0:P], self.SELB, self.IDENT, start=True, stop=True)
                tt = c.mark("pe", ins)
                self.sel_rd = tt
                c.wait("act", tt, last_far)
                ins = nc.scalar.copy(out=self.SNT, in_=self.ps.banks[tb_][:, 0:P])
                t_snt = c.mark("act", ins)
                self.ps.release(tb_, t_snt)
                c.wait("pe", t_snt)
                for hh in range(2):
                    h0 = 8 * g + 4 * hh
                    Q4 = self.QT[:, h0:h0 + 4, j * P:(j + 1) * P]
                    SN4 = lambda rows: self.SNT[rows, :].unsqueeze(1).to_broadcast([rows.stop - rows.start, 4, P])
                    tiles = []
                    for r in range(5):
                        lt = j + r
                        t = dict(kT=self.KWL[:, g, lt * P:(lt + 1) * P], v=self.VWL[:, lt, g * P:(g + 1) * P], kb=KBW[:, lt:lt + 1], adds=[])
                        if r == 4:
                            t["adds"].append((self.IDENT, self.D0[:, h0:h0 + 4, :]))
                        elif r == 3:
                            t["adds"].append((self.IDENT, self.D1[:, h0:h0 + 4, :]))
                        elif r == 0:
                            t["adds"].append((self.IDENT, self.D4.unsqueeze(1).to_broadcast([P, 4, P])))
                        tiles.append(t)
                    ob, rb, tk = self.attn_unit(Q4, tiles)
                    self.combine(ob, rb, tk, hh, g, j, 2, False)
                    tiles = []
                    for r in range(2):
                        lt = j + r
                        tiles.append(dict(kT=self.KSL[:, g, lt * P:(lt + 1) * P], v=self.VSL[:, lt, g * P:(g + 1) * P], kb=KBS[:, lt:lt + 1],
                                          adds=[(self.IDENT, (self.D1 if r == 0 else self.D0)[:, h0:h0 + 4, :]),
                                                (self.SELN[:, j, r, :], SN4(slice(0, P)))]))
                    for kg in range(64):
                        kv_ = kA if kg < 32 else kB
                        vv_ = vA if kg < 32 else vB
                        a = kg // 32
                        tiles.append(dict(kT=kv_[:, 0, (kg % 32) * P:(kg % 32 + 1) * P], v=vv_[:, kg % 32, :], kb=KBF[:, j, kg:kg + 1],
                                          adds=[(self.SELF[64 * a:64 * a + 64, kg % 32, :], SN4(slice(64 * a, 64 * a + 64)))]))
                    ob, rb, tk = self.attn_unit(Q4, tiles)
                    last_far = tk
                    t1, t3 = self.combine(ob, rb, tk, hh, g, j, 1, False)
                    self.rinv_rd = t1
                    c.wait("dve", t3)
                    ins = nc.vector.tensor_copy(out=self.OT[:, h0:h0 + 4, j * P:(j + 1) * P],
                                                in_=self.ACC[:, hh, :].rearrange("p (h t) -> p h t", h=4))
                    last_ot = c.mark("dve", ins)
            for s_ in (skA, skB, svA, svB):
                self.ring.release(s_, last_far)
        t_x = c.dma("sp", self.XT[:, :, :], d["xsp"], waits=[last_far, last_ot, t_spill])
        self.xt_tok = t_x
        c.wait("dve", t_x)
        c.wait("act", t_x)
        self.ht_tok = last_ot
        self.b2_read = last_far
        self.b1_read = last_far
        self.rX.done("pe", last_far)
        last = self.linear_fm(8, KC, lambda k, h: self.OT[:, k, h * 512:(h + 1) * 512], [last_ot], self.add_to_xt)
        self.ht_read = last


def build_B(n_layers=2, dbg=False):
    nc = bass.Bass("TRN2", target_bir_lowering=False)
    dt = lambda name, shape, kind="ExternalInput", dtype=F32: nc.dram_tensor(name, list(shape), dtype, kind=kind).ap()
    d = {}
    xT_d = dt("xT", [D, T])
    pT_d = dt("pT", [2, PLE, T])
    nrm_d = dt("nrm", [P, NV, KC])
    knorm_d = dt("knormT", [P, 3])
    qn_d = dt("qnT", [P, 2])
    b_w_in = dt("b_w_in", [2, D, D + 48])
    b_w_out = dt("b_w_out", [2, D, D])
    if not dbg:
        ffn_w_in = dt("ffn_w_in", [2, D, 2 * FF])
        ffn_w_out = dt("ffn_w_out", [2, FF, D])
        ple_w = dt("ple_w", [2, PLE, D])
        ple_gate = dt("ple_gate", [2, D, D])
    d["cmp_w1"] = dt("cmp_w1", [2, 4096, 256])
    d["cmp_w2"] = dt("cmp_w2", [2, 256, P])
    d["peT"] = dt("peT", [P, 2, 32])
    d["relb"] = dt("relb", [1, 512])
    d["kcvT"] = dt("kcvT", [2, P, 2, SEQ], dtype=BF16)
    d["ksT_full"] = dt("ksT_full", [P, 2, SEQ], dtype=BF16)
    d["vs_full"] = dt("vs_full", [SEQ, 256], dtype=BF16)
    d["kwT_loc"] = dt("kwT_loc", [P, 2, 12 * P], dtype=BF16)
    d["vw_loc"] = dt("vw_loc", [12 * P, 256], dtype=BF16)
    d["ksT_loc"] = dt("ksT_loc", [P, 2, 9 * P], dtype=BF16)
    d["vs_loc"] = dt("vs_loc", [9 * P, 256], dtype=BF16)
    for name, shape in (("M0", [P, NB, P]), ("M1", [P, NB, P]), ("MC", [17, NB, P]), ("D4", [P, P]), ("SHC", [17, 8, 4, P]),
                        ("SELF", [P, 32, P]), ("SELN", [P, 8, 2, P]), ("SELG", [48, 48, P]), ("FV", [P, 8, P]), ("OV", [P, 4, P]),
                        ("IDENT", [P, P])):
        d[name] = dt(name, shape, dtype=BF16)
    d["KB"] = dt("KB", [P, 544])
    xo_d = dt("xT_out", [D, T], kind="ExternalOutput")
    it = lambda name, shape, dtype=BF16: nc.dram_tensor(name, list(shape), dtype, kind="Internal").ap()
    d["D0"] = it("D0s", [P, 16, P]); d["D1"] = it("D1s", [P, 16, P]); d["FCd"] = it("FCs", [17, 16, P])
    d["KCTd"] = it("KCTs", [P, 2, 512]); d["VCd"] = it("VCs", [P, 4, 2, P])
    d["xsp"] = it("xsp", [P, KC, T], F32)

    k = CoreB(nc)
    c = k.c
    k.plan_setup_B(d)
    for jl in range(n_layers):
        k.plan_nsa(b_w_in[jl], b_w_out[jl], d)
        if dbg:
            break
        k.plan_ffn(ffn_w_in[jl], ffn_w_out[jl])
        k.plan_ple(ple_w[jl], ple_gate[jl])
    k.setup_common(nrm_d, None)
    k.KN = nc.alloc_sbuf_tensor("KN", [P, 3], F32)
    k.QN = nc.alloc_sbuf_tensor("QN", [P, 2], F32)
    t_kn = c.dma("sp", k.KN[:, :], knorm_d)
    t_qn = c.dma("sp", k.QN[:, :], qn_d)
    c.wait("dve", t_kn, t_qn)
    ins = nc.vector.tensor_scalar(out=k.QN[:, :], in0=k.QN[:, :], scalar1=HSCALE, scalar2=None, op0=ALU.mult)
    c.mark("dve", ins)
    k.carve()
    k.setup_B(d)
    k.xt_tok = c.dma("sp", k.XT[:, :, :], xT_d.rearrange("(kc p) t -> p kc t", p=P), waits=k.setup_toks)
    c.wait("dve", k.xt_tok)
    k.b2_read = None
    for jl in range(n_layers):
        l = 2 + jl
        k.nsa(l, jl, d)
        if dbg:
            break
        k.ple_load(pT_d[jl])
        k.ffn(l)
        k.ple(l)
    outs = [c.dma("sp", xo_d.rearrange("(kc p) t -> p kc t", p=P), k.XT[:, :, :], waits=[k.xt_tok])]
    c.wait("sp", *outs)
    return nc


def t5_bucket_np(n):
    n = np.maximum(n, 0)
    nf = np.maximum(n, 1).astype(np.float32)
    large = 16 + (np.log(nf / np.float32(16)) / np.float32(np.log(128 / 16)) * np.float32(16)).astype(np.int32)
    large = np.minimum(large, 31)
    return np.where(n < 16, n, large)


def onehot_table(delta):
    b = np.where(delta < 0, 32, t5_bucket_np(delta))
    return (b[..., None] == np.arange(NB)).astype(np.float32)


def host_tables_B(core):
    bf = lambda a: np.ascontiguousarray(a).astype(ml_dtypes.bfloat16)
    sig = np.arange(P)[:, None]
    tau = np.arange(P)[None, :]
    t = {}
    t["M0"] = bf(onehot_table(tau - sig).transpose(0, 2, 1))
    t["M1"] = bf(onehot_table(128 + tau - sig).transpose(0, 2, 1))
    io = np.arange(16)[:, None] - 9
    mc = onehot_table(tau - 16 * io - 31)
    fut = np.zeros((1, P, NB), np.float32); fut[:, :, 32] = 1.0
    t["MC"] = bf(np.concatenate([mc, fut], 0).transpose(0, 2, 1))
    t["D4"] = bf(np.where(tau >= sig, NEG, 0.0))
    shc = np.zeros((17, 8, 4, P), np.float32)
    seln = np.zeros((P, 8, 2, P), np.float32)
    fv = np.zeros((P, 8, P), np.float32)
    kbf = np.zeros((P, 8, 64), np.float32)
    blk = np.arange(P)
    for j in range(8):
        qg = 8 * core + j
        idx = np.arange(512).reshape(4, P)
        for i_ in range(16):
            shc[i_, j] = (idx == 8 * qg + i_ - 9)
        shc[16, j] = (idx > 8 * qg + 6) | (idx >= NCMP)
        for r in range(2):
            kt = qg - 1 + r
            if kt >= 0:
                seln[:, j, r, :] = (blk[:, None] == 2 * kt + (np.arange(P)[None, :] // 64))
        tt = 128 * qg + np.arange(P)
        cur = tt // 64
        f = np.where(blk[None, :] <= cur[:, None], 0.0, -1e30).astype(np.float32)
        f[:, 0] = 1e30
        for q in range(P):
            if cur[q] >= 1:
                f[q, cur[q] - 1] = 3e30
            f[q, cur[q]] = 2e30
        fv[:, j, :] = f
        kg = np.arange(64)
        kbf[:, j, :] = np.where((kg == qg) | (kg == qg - 1) | (kg > qg), NEG, 0.0)[None, :]
    t["SHC"] = bf(shc); t["SELN"] = bf(seln); t["FV"] = bf(fv)
    selfar = np.zeros((P, 32, P), np.float32)
    for k_ in range(32):
        selfar[:, k_, :] = ((np.arange(P)[:, None] % 64) == 2 * k_ + (np.arange(P)[None, :] // 64))
    t["SELF"] = bf(selfar)
    t["SELG"] = bf(np.broadcast_to(np.eye(48, dtype=np.float32)[:, :, None], (48, 48, P)))
    c0 = np.arange(512) * 16
    s0 = np.arange(P) * 64
    lo = np.maximum(c0[:, None], s0[None, :]); hi = np.minimum(c0[:, None] + 32, s0[None, :] + 64)
    ov = np.maximum(hi - lo, 0).astype(np.float32) / 32.0
    ov[NCMP:] = 0
    t["OV"] = bf(ov.reshape(4, P, P).transpose(1, 0, 2))
    t["IDENT"] = bf(np.eye(P, dtype=np.float32))
    kb = np.zeros((P, 544), np.float32)
    for lt in range(12):
        kb[:, lt] = NEG if (8 * core - 4 + lt) < 0 else 0.0
    for lt in range(9):
        kb[:, 12 + lt] = NEG if (8 * core - 1 + lt) < 0 else 0.0
    kb[:, 21:21 + 512] = kbf.reshape(P, 512)
    t["KB"] = kb
    return t


def host_inputs_B(inputs, core, xT, kvT, kvV):
    f = lambda a: np.ascontiguousarray(a, dtype=np.float32)
    tok = slice(core * T, (core + 1) * T)
    p = inputs["p"][2:4, 0, tok, :]
    vecs = np.zeros((NV, D), np.float32)
    vecs[NV_MIX:NV_MIX + 4] = inputs["norm_mix"]
    vecs[NV_FFN:NV_FFN + 4] = inputs["norm_ffn"]
    vecs[NV_PLE:NV_PLE + 4] = inputs["norm_ple"]
    nrm = vecs.reshape(NV, KC, P).transpose(2, 0, 1)
    fullT = lambda i: np.ascontiguousarray(np.concatenate([kvT[c_, i] for c_ in range(NCORES)], axis=-1))
    fullV = lambda i: np.ascontiguousarray(np.concatenate([kvV[c_, i] for c_ in range(NCORES)], axis=0))

    def locT(full, halo):
        out = np.zeros((P, 2, (8 + halo) * P), full.dtype)
        lo = core * T - halo * P
        s = max(lo, 0)
        out[:, :, s - lo:] = full[:, :, s:core * T + T]
        return out

    def locV(full, halo):
        out = np.zeros(((8 + halo) * P, 256), full.dtype)
        lo = core * T - halo * P
        s = max(lo, 0)
        out[s - lo:] = full[s:core * T + T]
        return out

    ksT, kwT = fullT(2), fullT(3)
    vs, vw = fullV(0), fullV(1)
    m = {
        "xT": f(xT), "pT": f(p.transpose(0, 2, 1)), "nrm": f(nrm), "knormT": f(inputs["k_norm"].T), "qnT": f(inputs["b_q_norm"].T),
        "b_w_in": f(inputs["b_w_in"]), "b_w_out": f(inputs["b_w_out"]),
        "ffn_w_in": f(inputs["ffn_w_in"][2:4]), "ffn_w_out": f(inputs["ffn_w_out"][2:4]),
        "ple_w": f(inputs["ple_w"][2:4]), "ple_gate": f(inputs["ple_gate"][2:4]),
        "cmp_w1": f(np.stack([inputs["cmp_wk1"], inputs["cmp_wv1"]])), "cmp_w2": f(np.stack([inputs["cmp_wk2"], inputs["cmp_wv2"]])),
        "peT": f(np.stack([inputs["cmp_pe_k"].T, inputs["cmp_pe_v"].T], axis=1)),
        "relb": f(inputs["rel_bias"].reshape(1, 512)),
        "kcvT": np.ascontiguousarray(np.stack([fullT(0), fullT(1)])), "ksT_full": ksT, "vs_full": vs,
        "kwT_loc": locT(kwT, 4), "vw_loc": locV(vw, 4), "ksT_loc": locT(ksT, 1), "vs_loc": locV(vs, 1),
    }
    m.update(host_tables_B(core))
    return m


_NC_CACHE = {}


def kernel(**inputs):
    inputs = {k_: np.asarray(v) for k_, v in inputs.items()}
    if "A" not in _NC_CACHE:
        _NC_CACHE["A"] = build_A()
    resA = run_bass_kernel_spmd(_NC_CACHE["A"], [host_inputs_A(inputs, c_) for c_ in range(NCORES)], core_ids=list(range(NCORES)))
    kvT = np.stack([r["kvT"] for r in resA.results])
    kvV = np.stack([r["kvV"] for r in resA.results])
    if "B" not in _NC_CACHE:
        _NC_CACHE["B"] = build_B()
    in_B = [host_inputs_B(inputs, c_, resA.results[c_]["xT_out"], kvT, kvV) for c_ in range(NCORES)]
    resB = run_bass_kernel_spmd(_NC_CACHE["B"], in_B, core_ids=list(range(NCORES)))
    out = np.concatenate([r["xT_out"].T for r in resB.results], axis=0)
    return np.ascontiguousarray(out[None].astype(np.float32))
```

```python
import numpy as np
import ml_dtypes
import concourse.bass as bass
import concourse.mybir as mybir
from concourse.bass_utils import run_bass_kernel_spmd

F32 = mybir.dt.float32
BF16 = mybir.dt.bfloat16
AF = mybir.ActivationFunctionType
ALU = mybir.AluOpType

NCORES = 8
P = 128
SEQ = 8192
T = SEQ // NCORES
NT = T // P
D = 2048
KC = D // P
FF = 5632
HB = 11
PLE = 256
EPS = 1e-6
SLAB = 4096
NSLOT = 4

NV_MIX, NV_FFN, NV_PLE, NV_AV, NV_KV = 0, 4, 8, 12, 14
NV = 15


class Ctx:
    def __init__(self, nc):
        self.nc = nc
        self.E = {"pe": nc.tensor, "act": nc.scalar, "dve": nc.vector, "pool": nc.gpsimd, "sp": nc.sync}
        self.prog = {}
        self.seen = {}
        self.nsem = 0
        for e in ("pe", "act", "dve", "pool"):
            self.prog[e] = [self.sem("pg_" + e), 0]
        self.misc = [[self.sem("misc"), 0] for _ in range(12)]
        self.misc_i = 0

    def sem(self, name):
        self.nsem += 1
        return self.nc.alloc_semaphore(f"{name}_{self.nsem}")

    def mark(self, e, ins):
        p = self.prog[e]
        if p[1] >= 30000:
            p = self.prog[e] = [self.sem("pg_" + e), 0]
        p[1] += 1
        ins.then_inc(p[0], 1)
        return (p[0], p[1])

    def wait(self, e, *toks):
        for tok in toks:
            if tok is None:
                continue
            if isinstance(tok, list):
                self.wait(e, *tok)
                continue
            sem, val = tok
            k = (e, sem)
            if self.seen.get(k, 0) >= val:
                continue
            self.seen[k] = val
            self.E[e].wait_ge(sem, val)

    def dma(self, e, out, in_, waits=()):
        m = self.misc[self.misc_i % len(self.misc)]
        self.misc_i += 1
        if m[1] > 0:
            self.wait(e, (m[0], m[1]))
        self.wait(e, *waits)
        self.E[e].dma_start(out=out, in_=in_).then_inc(m[0], 16)
        m[1] += 16
        return (m[0], m[1])


class Ring:
    def __init__(self, c, nslot=NSLOT):
        self.c = c
        nc = c.nc
        self.n = nslot
        self.slots = [nc.alloc_sbuf_tensor(f"wslab{i}", [P, SLAB], BF16) for i in range(nslot)]
        self.sems = [c.sem("wr") for _ in range(nslot)]
        self.cnt = [0] * nslot
        self.free = [True] * nslot
        self.free_tok = [None] * nslot
        self.plan = []
        self.ready = {}
        self.issued = 0
        self.consumed = 0

    def add(self, src, kc, cols):
        self.plan.append((src, kc, cols))

    def view(self, s, kc, cols):
        return self.slots[s][:, 0:kc * cols].rearrange("p (k n) -> p k n", k=kc)

    def pump(self):
        while self.issued < len(self.plan):
            i = self.issued
            fs = [s for s in range(self.n) if self.free[s]]
            if not fs:
                break
            s = fs[0]
            src, kc, cols = self.plan[i]
            self.c.wait("pool", self.free_tok[s])
            self.c.nc.gpsimd.dma_start(out=self.view(s, kc, cols), in_=src).then_inc(self.sems[s], 16)
            self.cnt[s] += 1
            self.ready[i] = ((self.sems[s], 16 * self.cnt[s]), s)
            self.free[s] = False
            self.issued += 1

    def next(self):
        i = self.consumed
        self.consumed += 1
        self.pump()
        assert self.issued > i, "weight ring deadlock"
        tok, s = self.ready[i]
        _, kc, cols = self.plan[i]
        return self.view(s, kc, cols), tok, s

    def release(self, s, tok):
        self.free[s] = True
        self.free_tok[s] = tok
        self.pump()


class Psum:
    def __init__(self, c):
        self.c = c
        self.banks = [c.nc.alloc_psum_tensor(f"psb{i}", [P, 512], F32) for i in range(8)]
        self.free_tok = [None] * 8
        self.held = [False] * 8
        self.i = 0

    def get(self):
        for _ in range(8):
            b = self.i % 8
            self.i += 1
            if not self.held[b]:
                break
        else:
            raise RuntimeError("all PSUM banks held")
        self.held[b] = True
        self.c.wait("pe", self.free_tok[b])
        self.free_tok[b] = None
        return b

    def release(self, b, tok):
        self.held[b] = False
        self.free_tok[b] = tok


class Reg:
    def __init__(self, c):
        self.c = c
        self.toks = {}

    def acquire(self, e):
        self.c.wait(e, *[t for e2, t in self.toks.items() if e2 != e])

    def all(self):
        return list(self.toks.values())

    def done(self, e, tok):
        self.toks[e] = tok


def wslab(w2d, r0, nrows, c0, ncols):
    return w2d[r0:r0 + nrows, c0:c0 + ncols].rearrange("(kc p) n -> p kc n", p=P)


class Core:
    def __init__(self, nc):
        self.nc = nc
        self.c = Ctx(nc)
        self.ring = Ring(self.c)
        self.ps = Psum(self.c)
        a = nc.alloc_sbuf_tensor
        self.XT = a("XT", [P, KC, T], F32)
        self.HT = a("HT", [P, KC, T], BF16)
        self.B1 = a("B1", [P, KC * T], BF16)
        self.B2 = a("B2", [P, KC * T], BF16)
        self.SCR = a("SCR", [P, 6 * T], BF16)
        self.SQ = self.SCR[:, 0:2 * T].rearrange("p (a t) -> p a t", a=2)
        self.TMPB = self.SQ
        self.JUNK = self.SCR[:, 0:4 * T]
        self.BIAS = self.SCR[:, 0:4 * T].bitcast(F32)
        self.RSTD = self.SCR[:, 2 * T:4 * T].bitcast(F32)
        self.TMPF = self.RSTD.rearrange("p (a t) -> p a t", a=2)
        self.WST = self.SCR[:, 4 * T:6 * T].rearrange("p (g t) -> p g t", g=KC)
        self.PT = self.SCR[:, 4 * T:6 * T].rearrange("p (a t) -> p a t", a=2)
        self.rSQ, self.rRS, self.rX = Reg(self.c), Reg(self.c), Reg(self.c)
        self.NRM = a("NRM", [P, NV, KC], F32)
        self.ONES = a("ONES", [P, P], BF16)
        self.SMALL = a("SMALL", [P, 64], F32)
        self.EPSB = a("EPSB", [P, 1], F32)
        self.xt_tok = None
        self.ht_read = None
        self.ht_tok = None
        self.b1_read = None
        self.b2_read = None
        self.small_rd = None
        self.tmp_i = 0

    def rmsnorm(self, nv_idx):
        c, nc = self.c, self.nc
        pb = [self.ps.get(), self.ps.get()]
        self.rSQ.acquire("act")
        sq_rd = [None, None]
        last = None
        for k in range(KC):
            j = k % 2
            c.wait("act", self.xt_tok, sq_rd[j])
            ins = nc.scalar.activation(out=self.SQ[:, j, :], in_=self.XT[:, k, :], func=AF.Square)
            t = c.mark("act", ins)
            c.wait("pe", t)
            for h in range(2):
                ins = nc.tensor.matmul(self.ps.banks[pb[h]][:, :], self.ONES[:, :], self.SQ[:, j, h * 512:(h + 1) * 512],
                                       start=(k == 0), stop=(k == KC - 1))
            sq_rd[j] = last = c.mark("pe", ins)
        self.rSQ.done("act", t)
        self.rSQ.done("pe", last)
        c.wait("act", last)
        self.rRS.acquire("act")
        rel = None
        for h in range(2):
            sl = slice(h * 512, (h + 1) * 512)
            ins = nc.scalar.activation(out=self.RSTD[:, sl], in_=self.ps.banks[pb[h]][:, :], func=AF.Sqrt,
                                       bias=self.EPSB[:, 0:1], scale=1.0 / D)
            t = c.mark("act", ins)
            self.ps.release(pb[h], t)
            c.wait("dve", t)
            ins = nc.vector.reciprocal(out=self.RSTD[:, sl], in_=self.RSTD[:, sl])
            rel = c.mark("dve", ins)
        self.rRS.done("act", t)
        c.wait("dve", rel, self.ht_read, self.xt_tok)
        for k in range(KC):
            ins = nc.vector.scalar_tensor_tensor(out=self.HT[:, k, :], in0=self.XT[:, k, :],
                                                 scalar=self.NRM[:, nv_idx, k:k + 1], in1=self.RSTD[:, :],
                                                 op0=ALU.mult, op1=ALU.mult)
        self.ht_tok = c.mark("dve", ins)
        self.rRS.done("dve", self.ht_tok)
        return self.ht_tok

    def linear_fm(self, nslab, kc, rhs_fn, rhs_toks, epilogue):
        c, nc = self.c, self.nc
        last = None
        for s in range(nslab):
            view, rtok, slot = self.ring.next()
            c.wait("pe", rtok, *rhs_toks)
            ncol = view.shape[2] // P
            for mi in range(ncol):
                for h in range(2):
                    b = self.ps.get()
                    for k in range(kc):
                        ins = nc.tensor.matmul(self.ps.banks[b][:, :], view[:, k, mi * P:(mi + 1) * P], rhs_fn(k, h),
                                               start=(k == 0), stop=(k == kc - 1))
                    last = c.mark("pe", ins)
                    rel = epilogue(s * ncol + mi, h, self.ps.banks[b], last)
                    self.ps.release(b, rel)
            self.ring.release(slot, last)
        return last

    def add_to_xt(self, m, h, ps, tok):
        c, nc = self.c, self.nc
        sl = slice(h * 512, (h + 1) * 512)
        c.wait("dve", tok)
        ins = nc.vector.tensor_tensor(out=self.XT[:, m, sl], in0=ps[:, :], in1=self.XT[:, m, sl], op=ALU.add)
        self.xt_tok = c.mark("dve", ins)
        return self.xt_tok

    def plan_gmlp(self, w_in, w_out):
        for s in range(8):
            self.ring.add(wslab(w_in, 0, D, D + s * 256, 256), KC, 256)
        for s in range(8):
            self.ring.add(wslab(w_in, 0, D, s * 256, 256), KC, 256)
        for s in range(8):
            self.ring.add(wslab(w_out, 0, D, s * 256, 256), KC, 256)

    def gmlp(self, l, wsT_d, bs_d):
        c, nc = self.c, self.nc
        GV = self.B1[:, :].rearrange("p (t f) -> p t f", t=NT)
        SV = self.B2[:, :].rearrange("p (g t) -> p g t", g=KC)
        t_ws = c.dma("pool", self.WST[:, :, :], wsT_d[l].rearrange("g s t -> s g t"), waits=self.rX.all())
        c.wait("dve", t_ws, self.tri_tok)
        for g in range(KC):
            ins = nc.vector.tensor_tensor(out=self.WST[:, g, :], in0=self.WST[:, g, :], in1=self.TRI[:, :], op=ALU.mult)
        t_wst = c.mark("dve", ins)
        self.rX.done("dma", t_ws)
        self.rX.done("dve", t_wst)

        self.rmsnorm(NV_MIX + l)
        c.wait("act", self.b1_read)
        last = None
        for s in range(8):
            view, rtok, slot = self.ring.next()
            c.wait("pe", rtok, self.ht_tok)
            for t in range(NT):
                b = self.ps.get()
                for k in range(KC):
                    ins = nc.tensor.matmul(self.ps.banks[b][:, 0:256], self.HT[:, k, t * P:(t + 1) * P], view[:, k, :],
                                           start=(k == 0), stop=(k == KC - 1))
                last = c.mark("pe", ins)
                c.wait("act", last)
                ins = nc.scalar.activation(out=GV[:, t, s * 256:(s + 1) * 256], in_=self.ps.banks[b][:, 0:256],
                                           func=AF.Gelu_apprx_tanh)
                ta = c.mark("act", ins)
                self.ps.release(b, ta)
            self.ring.release(slot, last)
        SS = self.SMALL
        c.wait("dve", self.small_rd)
        ins = nc.vector.memset(SS[:, 0:16], 0.0)
        t0 = c.mark("dve", ins)
        c.wait("act", t0, ta)
        self.rSQ.acquire("act")
        self.rRS.acquire("act")
        for t in range(NT):
            ins = nc.scalar.activation(out=self.JUNK[:, 0:D], in_=GV[:, t, :], func=AF.Square, accum_out=SS[:, t:t + 1])
        t1 = c.mark("act", ins)
        self.rSQ.done("act", t1)
        self.rRS.done("act", t1)
        c.wait("act", t1)
        ins = nc.scalar.activation(out=SS[:, 8:16], in_=SS[:, 0:8], func=AF.Sqrt, bias=self.EPSB[:, 0:1], scale=1.0 / D)
        t1b = c.mark("act", ins)
        c.wait("dve", t1b)
        ins = nc.vector.reciprocal(out=SS[:, 8:16], in_=SS[:, 8:16])
        t2 = c.mark("dve", ins)
        c.wait("dve", t2)
        for t in range(NT):
            ins = nc.vector.tensor_scalar(out=GV[:, t, :], in0=GV[:, t, :], scalar1=SS[:, 8 + t:9 + t], scalar2=None,
                                          op0=ALU.mult)
        t_vn = c.mark("dve", ins)
        self.small_rd = t_vn
        t_bs = c.dma("sp", self.BIAS[:, :], bs_d[l:l + 1, :].to_broadcast([P, KC * P]), waits=self.rSQ.all() + self.rRS.all())
        self.rSQ.done("dma", t_bs)
        self.rRS.done("dma", t_bs)
        c.wait("pe", t_vn, t_wst)
        c.wait("dve", self.b2_read, t_bs)
        for t in range(NT):
            bb = [self.ps.get() for _ in range(4)]
            for g in range(KC):
                ins = nc.tensor.matmul(self.ps.banks[bb[g // 4]][:, (g % 4) * P:(g % 4 + 1) * P],
                                       GV[:, t, g * P:(g + 1) * P], self.WST[:, g, :], start=True, stop=True)
            tp = c.mark("pe", ins)
            c.wait("dve", tp)
            for g in range(KC):
                ins = nc.vector.scalar_tensor_tensor(out=SV[:, g, t * P:(t + 1) * P],
                                                     in0=self.ps.banks[bb[g // 4]][:, (g % 4) * P:(g % 4 + 1) * P],
                                                     scalar=self.NRM[:, NV_AV + l, g:g + 1],
                                                     in1=self.BIAS[:, g * P:(g + 1) * P], op0=ALU.mult, op1=ALU.add)
                if g % 4 == 3:
                    td = c.mark("dve", ins)
                    self.ps.release(bb[g // 4], td)
        self.b1_read = tp
        self.rX.done("pe", tp)
        self.rSQ.done("dve", td)
        self.rRS.done("dve", td)
        t_sv = td

        self.rSQ.acquire("act")
        tb_rd = [None, None]

        def epi_u(m, h, ps, tok):
            j = self.tmp_i % 2
            self.tmp_i += 1
            sl = slice(h * 512, (h + 1) * 512)
            c.wait("act", tok, tb_rd[j])
            ins = nc.scalar.activation(out=self.TMPB[:, j, 0:512], in_=ps[:, :], func=AF.Gelu_apprx_tanh)
            ta = c.mark("act", ins)
            self.rSQ.done("act", ta)
            c.wait("dve", ta, t_sv)
            ins = nc.vector.tensor_tensor(out=SV[:, m, sl], in0=self.TMPB[:, j, 0:512], in1=SV[:, m, sl], op=ALU.mult)
            tb_rd[j] = self.gt_tok = c.mark("dve", ins)
            self.rSQ.done("dve", self.gt_tok)
            return ta

        last = self.linear_fm(8, KC, lambda k, h: self.HT[:, k, h * 512:(h + 1) * 512], [self.ht_tok], epi_u)
        self.ht_read = last
        last = self.linear_fm(8, KC, lambda k, h: SV[:, k, h * 512:(h + 1) * 512], [self.gt_tok], self.add_to_xt)
        self.b2_read = last

    def plan_ffn(self, w_in, w_out):
        for hb in range(HB):
            for j in range(2):
                self.ring.add(wslab(w_in, 0, D, hb * 512 + j * 256, 256), KC, 256)
                self.ring.add(wslab(w_in, 0, D, FF + hb * 512 + j * 256, 256), KC, 256)
            for j in range(2):
                self.ring.add(wslab(w_out, hb * 512 + j * 256, 256, 0, D), 2, D)

    def ffn(self, l):
        c, nc = self.c, self.nc
        self.rmsnorm(NV_FFN + l)
        AB = self.B1[:, 0:3 * 4 * T].rearrange("p (r k t) -> p r k t", r=3, k=4)
        ab_rd = [self.b1_read, self.b1_read, self.b1_read]
        sg_rd = [None, None]
        SG = self.TMPB
        self.rSQ.acquire("act")
        rhs = lambda k, h: self.HT[:, k, h * 512:(h + 1) * 512]
        last_pe = None
        tp = None
        for hb in range(HB):
            r = hb % 3
            for j in range(2):
                def epi_g(m, h, ps, tok):
                    c.wait("act", tok, sg_rd[m] if h == 0 else None)
                    ins = nc.scalar.activation(out=SG[:, m, h * 512:(h + 1) * 512], in_=ps[:, :], func=AF.Silu)
                    self.sg_tok = c.mark("act", ins)
                    return self.sg_tok

                def epi_u(m, h, ps, tok, j=j, r=r):
                    c.wait("dve", tok, self.sg_tok, ab_rd[r])
                    ins = nc.vector.tensor_tensor(out=AB[:, r, 2 * j + m, h * 512:(h + 1) * 512], in0=ps[:, :],
                                                  in1=SG[:, m, h * 512:(h + 1) * 512], op=ALU.mult)
                    t = c.mark("dve", ins)
                    sg_rd[m] = t
                    self.ab_tok = t
                    return t

                self.linear_fm(1, KC, rhs, [self.ht_tok], epi_g)
                last_pe = self.linear_fm(1, KC, rhs, [self.ht_tok], epi_u)
            vA, rA, sA = self.ring.next()
            vB, rB, sB = self.ring.next()
            c.wait("pe", rA, rB, self.ab_tok)
            for m in range(KC):
                for h in range(2):
                    b = self.ps.get()
                    for kk in range(4):
                        v = vA if kk < 2 else vB
                        ins = nc.tensor.matmul(self.ps.banks[b][:, :], v[:, kk % 2, m * P:(m + 1) * P],
                                               AB[:, r, kk, h * 512:(h + 1) * 512], start=(kk == 0), stop=(kk == 3))
                    tp = c.mark("pe", ins)
                    rel = self.add_to_xt(m, h, self.ps.banks[b], tp)
                    self.ps.release(b, rel)
            self.ring.release(sA, tp)
            self.ring.release(sB, tp)
            ab_rd[r] = tp
        self.ht_read = last_pe
        self.b1_read = tp
        self.rSQ.done("act", self.sg_tok)
        self.rSQ.done("dve", self.ab_tok)

    def plan_ple(self, w_proj, w_gate):
        self.ring.add(wslab(w_proj, 0, PLE, 0, D), 2, D)
        for s in range(8):
            self.ring.add(wslab(w_gate, 0, D, s * 256, 256), KC, 256)

    def ple_load(self, pT_l):
        c = self.c
        self.pt_tok = c.dma("pool", self.PT[:, :, :], pT_l.rearrange("(kc p) t -> p kc t", p=P), waits=self.rX.all())
        self.rX.done("dma", self.pt_tok)

    def ple(self, l):
        c, nc = self.c, self.nc
        self.rmsnorm(NV_PLE + l)
        vW, rW, sW = self.ring.next()
        c.wait("pe", rW, self.pt_tok)
        state = {}
        self.rRS.acquire("act")
        tf_rd = [None, None]

        def epi(m, h, ps, tok):
            j = self.tmp_i % 2
            self.tmp_i += 1
            sl = slice(h * 512, (h + 1) * 512)
            c.wait("act", tok, tf_rd[j])
            ins = nc.scalar.activation(out=self.TMPF[:, j, :], in_=ps[:, :], func=AF.Sigmoid)
            ta = c.mark("act", ins)
            self.rRS.done("act", ta)
            b = self.ps.get()
            for k in range(2):
                ins = nc.tensor.matmul(self.ps.banks[b][:, :], vW[:, k, m * P:(m + 1) * P], self.PT[:, k, sl],
                                       start=(k == 0), stop=(k == 1))
            tp = c.mark("pe", ins)
            state["tp"] = tp
            c.wait("dve", ta, tp)
            ins = nc.vector.tensor_tensor(out=self.TMPF[:, j, :], in0=self.ps.banks[b][:, :], in1=self.TMPF[:, j, :], op=ALU.mult)
            t1 = c.mark("dve", ins)
            self.ps.release(b, t1)
            c.wait("dve", t1)
            ins = nc.vector.tensor_tensor(out=self.XT[:, m, sl], in0=self.TMPF[:, j, :], in1=self.XT[:, m, sl], op=ALU.add)
            self.xt_tok = tf_rd[j] = c.mark("dve", ins)
            self.rRS.done("dve", self.xt_tok)
            return ta

        last = self.linear_fm(8, KC, lambda k, h: self.HT[:, k, h * 512:(h + 1) * 512], [self.ht_tok], epi)
        self.ring.release(sW, state["tp"])
        self.ht_read = last
        self.rX.done("pe", state["tp"])

    def plan_kv(self, kv_w):
        for s in range(6):
            self.ring.add(wslab(kv_w, 0, D, s * 256, 256), KC, 256)

    def kv_proj(self, knorm_tile, kvT_d, kvV_d):
        c, nc = self.c, self.nc
        self.rmsnorm(NV_KV)
        KVT = self.B2[:, 0:4 * 2 * T].rearrange("p (i g t) -> p i g t", i=4, g=2)
        KVV = self.B1[:, 0:2 * NT * 256].rearrange("p (i t f) -> p i t f", i=2, t=NT)
        c.wait("act", self.b1_read, self.b2_read)
        c.wait("dve", self.b1_read, self.b2_read)
        self.rSQ.acquire("act")
        self.rRS.acquire("act")
        tb_rd = [None, None]
        tf_rd = [None, None]
        fm_idx = {0: 0, 1: 1, 2: 2, 4: 3}
        tm_idx = {3: 0, 5: 1}
        outs = []
        last = None
        for kind in range(6):
            view, rtok, slot = self.ring.next()
            c.wait("pe", rtok, self.ht_tok)
            if kind in tm_idx:
                for t in range(NT):
                    b = self.ps.get()
                    for k in range(KC):
                        ins = nc.tensor.matmul(self.ps.banks[b][:, 0:256], self.HT[:, k, t * P:(t + 1) * P], view[:, k, :],
                                               start=(k == 0), stop=(k == KC - 1))
                    last = c.mark("pe", ins)
                    c.wait("act", last)
                    ins = nc.scalar.copy(out=KVV[:, tm_idx[kind], t, :], in_=self.ps.banks[b][:, 0:256])
                    ta = c.mark("act", ins)
                    self.ps.release(b, ta)
                outs.append(c.dma("sp", kvV_d[tm_idx[kind]].rearrange("(t p) f -> p t f", p=P), KVV[:, tm_idx[kind], :, :], waits=[ta]))
            else:
                i = fm_idx[kind]
                for g in range(2):
                    for h in range(2):
                        sl = slice(h * 512, (h + 1) * 512)
                        b = self.ps.get()
                        for k in range(KC):
                            ins = nc.tensor.matmul(self.ps.banks[b][:, :], view[:, k, g * P:(g + 1) * P], self.HT[:, k, sl],
                                                   start=(k == 0), stop=(k == KC - 1))
                        last = c.mark("pe", ins)
                        if kind in (0, 1):
                            c.wait("act", last)
                            ins = nc.scalar.copy(out=KVT[:, i, g, sl], in_=self.ps.banks[b][:, :])
                            ta = c.mark("act", ins)
                            self.ps.release(b, ta)
                        else:
                            j = self.tmp_i % 2
                            self.tmp_i += 1
                            c.wait("act", last, tb_rd[j])
                            ins = nc.scalar.activation(out=self.TMPB[:, j, 0:512], in_=self.ps.banks[b][:, :], func=AF.Square)
                            t1 = c.mark("act", ins)
                            b2 = self.ps.get()
                            c.wait("pe", t1)
                            ins = nc.tensor.matmul(self.ps.banks[b2][:, :], self.ONES[:, :], self.TMPB[:, j, 0:512], start=True, stop=True)
                            t2 = c.mark("pe", ins)
                            tb_rd[j] = t2
                            c.wait("act", t2, tf_rd[j])
                            ins = nc.scalar.activation(out=self.TMPF[:, j, :], in_=self.ps.banks[b2][:, :], func=AF.Sqrt,
                                                       bias=self.EPSB[:, 0:1], scale=1.0 / P)
                            t3 = c.mark("act", ins)
                            self.ps.release(b2, t3)
                            c.wait("dve", t3)
                            ins = nc.vector.reciprocal(out=self.TMPF[:, j, :], in_=self.TMPF[:, j, :])
                            t4 = c.mark("dve", ins)
                            c.wait("dve", t4)
                            kn = 1 if kind == 2 else 2
                            ins = nc.vector.scalar_tensor_tensor(out=KVT[:, i, g, sl], in0=self.ps.banks[b][:, :],
                                                                 scalar=knorm_tile[:, kn:kn + 1],
                                                                 in1=self.TMPF[:, j, :], op0=ALU.mult, op1=ALU.mult)
                            ta = c.mark("dve", ins)
                            tf_rd[j] = ta
                            self.ps.release(b, ta)
                            self.rSQ.done("act", t1); self.rSQ.done("pe", t2)
                            self.rRS.done("act", t3); self.rRS.done("dve", ta)
                outs.append(c.dma("sp", kvT_d[i], KVT[:, i, :, :], waits=[ta]))
            self.ring.release(slot, last)
        self.ht_read = last
        return outs

    def setup_common(self, nrm_d, tri_d=None):
        c, nc = self.c, self.nc
        a = nc.alloc_sbuf_tensor
        ins = nc.vector.memset(self.ONES[:, :], 1.0)
        ins = nc.vector.memset(self.EPSB[:, :], EPS)
        t = c.mark("dve", ins)
        c.wait("act", t)
        c.wait("pe", t)
        t_n = c.dma("sp", self.NRM[:, :, :], nrm_d)
        c.wait("dve", t_n)
        if tri_d is not None:
            self.TRI = a("TRI", [P, P], BF16)
            self.tri_tok = c.dma("pool", self.TRI[:, :], tri_d)


def build_A(n_layers=2, do_kv=True, stop=None):
    nc = bass.Bass("TRN2", target_bir_lowering=False)
    dt = lambda name, shape, kind="ExternalInput", dtype=F32: nc.dram_tensor(name, list(shape), dtype, kind=kind).ap()
    xT_d = dt("xT", [D, T])
    pT_d = dt("pT", [4, PLE, T])
    nrm_d = dt("nrm", [P, NV, KC])
    tri_d = dt("tri", [P, P])
    knorm_d = dt("knormT", [P, 3])
    a_w_in = dt("a_w_in", [2, D, 2 * D])
    a_wsT = dt("a_wsT", [2, KC, P, P])
    a_b_s = dt("a_b_s", [2, KC * P])
    a_w_out = dt("a_w_out", [2, D, D])
    ffn_w_in = dt("ffn_w_in", [4, D, 2 * FF])
    ffn_w_out = dt("ffn_w_out", [4, FF, D])
    ple_w = dt("ple_w", [4, PLE, D])
    ple_gate = dt("ple_gate", [4, D, D])
    kv_w = dt("kv_w", [D, 1536])
    xo_d = dt("xT_out", [D, T], kind="ExternalOutput")
    kvT_d = dt("kvT", [4, P, 2, T], kind="ExternalOutput", dtype=BF16)
    kvV_d = dt("kvV", [2, T, 256], kind="ExternalOutput", dtype=BF16)

    k = Core(nc)
    c = k.c
    for l in range(n_layers):
        lastl = (l == n_layers - 1)
        k.plan_gmlp(a_w_in[l], a_w_out[l])
        if lastl and stop == "gmlp":
            break
        k.plan_ffn(ffn_w_in[l], ffn_w_out[l])
        if lastl and stop == "ffn":
            break
        k.plan_ple(ple_w[l], ple_gate[l])
    if do_kv:
        k.plan_kv(kv_w)
    k.setup_common(nrm_d, tri_d)
    KN = nc.alloc_sbuf_tensor("KN", [P, 3], F32)
    t_kn = c.dma("sp", KN[:, :], knorm_d)
    c.wait("dve", t_kn)
    k.xt_tok = c.dma("sp", k.XT[:, :, :], xT_d.rearrange("(kc p) t -> p kc t", p=P))
    c.wait("dve", k.xt_tok)
    for l in range(n_layers):
        lastl = (l == n_layers - 1)
        k.gmlp(l, a_wsT, a_b_s)
        if lastl and stop == "gmlp":
            break
        k.ple_load(pT_d[l])
        k.ffn(l)
        if lastl and stop == "ffn":
            break
        k.ple(l)
    outs = [c.dma("sp", xo_d.rearrange("(kc p) t -> p kc t", p=P), k.XT[:, :, :], waits=[k.xt_tok])]
    if do_kv:
        outs += k.kv_proj(KN, kvT_d, kvV_d)
    c.wait("sp", *outs)
    return nc


def host_inputs_A(inputs, core):
    f = lambda a: np.ascontiguousarray(a, dtype=np.float32)
    tok = slice(core * T, (core + 1) * T)
    x = inputs["x"][0, tok, :]
    p = inputs["p"][:, 0, tok, :]
    vecs = np.zeros((NV, D), np.float32)
    vecs[NV_MIX:NV_MIX + 4] = inputs["norm_mix"]
    vecs[NV_FFN:NV_FFN + 4] = inputs["norm_ffn"]
    vecs[NV_PLE:NV_PLE + 4] = inputs["norm_ple"]
    vecs[NV_AV:NV_AV + 2] = inputs["a_norm_v"]
    vecs[NV_KV] = inputs["kv_norm"]
    nrm = vecs.reshape(NV, KC, P).transpose(2, 0, 1)
    tri = (np.arange(P)[:, None] <= np.arange(P)[None, :]).astype(np.float32)
    return {
        "xT": f(x.T), "pT": f(p.transpose(0, 2, 1)), "nrm": f(nrm), "tri": tri,
        "knormT": f(inputs["k_norm"].T),
        "a_w_in": f(inputs["a_w_in"]), "a_wsT": f(np.transpose(inputs["a_w_s"], (0, 1, 3, 2))),
        "a_b_s": f(inputs["a_b_s"].reshape(2, KC * P)), "a_w_out": f(inputs["a_w_out"]),
        "ffn_w_in": f(inputs["ffn_w_in"]), "ffn_w_out": f(inputs["ffn_w_out"]),
        "ple_w": f(inputs["ple_w"]), "ple_gate": f(inputs["ple_gate"]), "kv_w": f(inputs["kv_w"]),
    }


NEG = -30000.0
HSCALE = 128.0 ** -0.5
NCMP = 511
NB = 33


class Flat:
    def __init__(self, ap):
        self.ap = ap
        self.off = 0

    def take(self, n):
        v = self.ap[:, self.off:self.off + n]
        self.off += n
        assert self.off <= self.ap.shape[1], (self.off, self.ap.shape)
        return v


class CoreB(Core):
    def carve(self):
        X = Flat(self.XT[:, :, :].rearrange("p a t -> p (a t)").bitcast(BF16))
        r = lambda v, s, **kw: v.rearrange(s, **kw)
        self.D0 = r(X.take(2048), "p (h t) -> p h t", h=16)
        self.D1 = r(X.take(2048), "p (h t) -> p h t", h=16)
        self.D4 = X.take(128)
        self.FC = r(X.take(2048), "p (h t) -> p h t", h=16)
        self.SHC = r(X.take(4096), "p (j c i) -> p j c i", j=8, c=4)
        self.SELF = r(X.take(4096), "p (k s) -> p k s", k=32)
        self.SELN = r(X.take(2048), "p (j r s) -> p j r s", j=8, r=2)
        self.SELG = r(X.take(6144), "p (c m) -> p c m", c=48)
        self.FV = r(X.take(1024), "p (j b) -> p j b", j=8)
        self.OV = r(X.take(512), "p (c b) -> p c b", c=4)
        self.KCT = r(X.take(1024), "p (g i) -> p g i", g=2)
        self.VC = r(X.take(1024), "p (c g d) -> p c g d", c=4, g=2)
        self.IDENT = X.take(128)
        Y = Flat(self.B2[:, :])
        self.LKW = [Y.take(640) for _ in range(2)]
        self.LVW = [r(Y.take(640), "p (t d) -> p t d", t=5) for _ in range(2)]
        self.LKS = [Y.take(256) for _ in range(2)]
        self.LVS = [r(Y.take(256), "p (t d) -> p t d", t=2) for _ in range(2)]
        self.KB = Y.take(1152).bitcast(F32)
        self.PC = r(Y.take(2048), "p (c n) -> p c n", c=4)
        self.PR = r(Y.take(1536), "p (c n) -> p c n", c=3)
        self.ACC = r(Y.take(2048).bitcast(F32), "p (a n) -> p a n", a=2)
        self.RINV = Y.take(1024).bitcast(F32)
        self.GREP = Y.take(1024).bitcast(F32)
        self.SCORE = Y.take(256).bitcast(F32)
        self.WORK = Y.take(256).bitcast(F32)
        self.SEL = Y.take(256).bitcast(F32)
        self.M8 = Y.take(32).bitcast(F32)
        self.SNT = Y.take(128)
        self.SELB = Y.take(128)
        self.RACC = Y.take(1024).bitcast(F32)
        self.RHL = r(Y.take(1024), "p (a n) -> p a n", a=2)
        self.GS = self.SCR[:, 4 * T:5 * T]
        self.QT = self.B1[:, :].rearrange("p (h t) -> p h t", h=16)
        self.OT = self.HT

    def setup_B(self, d):
        c, nc = self.c, self.nc
        X = Flat(self.XT[:, :, :].rearrange("p a t -> p (a t)").bitcast(BF16))
        MM = X.take(NB * 128).rearrange("p (b t) -> p b t", b=NB)
        TAB = X.take(2 * NB * 16).bitcast(F32).rearrange("p (b h) -> p b h", b=NB)
        ACCD = X.take(2 * 2048).bitcast(F32).rearrange("p (h t) -> p h t", h=16)
        OUTB = X.take(2048).rearrange("p (h t) -> p h t", h=16)
        HID = X.take(1024).rearrange("p (c i) -> p c i", c=2)
        PET = X.take(64).rearrange("p (a j) -> p a j", a=2)
        CB = X.take(8).bitcast(F32)
        KST = X.take(1024).rearrange("p (g i) -> p g i", g=2)
        VST = X.take(1024).rearrange("p (c g d) -> p c g d", c=4, g=2)
        SQT = X.take(512)
        RS = X.take(1024).bitcast(F32)
        TT = self.B1[:, :].rearrange("p (a t) -> p a t", a=2)
        t_tab = c.dma("sp", TAB[:, 0:32, :], d["relb"].to_broadcast([P, 512]).rearrange("p (b h) -> p b h", b=32))
        c.wait("dve", t_tab)
        for b in range(31):
            ins = nc.vector.tensor_tensor(out=TAB[:, b, :], in0=TAB[:, b, :], in1=TAB[:, 31, :], op=ALU.subtract)
        ins = nc.vector.memset(TAB[:, 31, :], 0.0)
        ins = nc.vector.memset(TAB[:, 32, :], NEG)
        t_prev = c.mark("dve", ins)
        dma_out = []
        for name, rows, dst in (("M0", P, d["D0"]), ("M1", P, d["D1"]), ("MC", 17, d["FCd"])):
            t_m = c.dma("pool", MM[0:rows, :, :], d[name], waits=[t_prev] + dma_out[-1:])
            c.wait("dve", t_m, t_prev)
            for h in range(16):
                ins = nc.vector.tensor_scalar(out=ACCD[0:rows, h, :], in0=MM[0:rows, 0, :], scalar1=TAB[0:rows, 0, h:h + 1],
                                              scalar2=None, op0=ALU.mult)
                for b in range(1, NB):
                    ins = nc.vector.scalar_tensor_tensor(out=ACCD[0:rows, h, :], in0=MM[0:rows, b, :],
                                                         scalar=TAB[0:rows, b, h:h + 1], in1=ACCD[0:rows, h, :],
                                                         op0=ALU.mult, op1=ALU.add)
            t1 = c.mark("dve", ins)
            c.wait("dve", t1, *dma_out[-1:])
            ins = nc.vector.tensor_copy(out=OUTB[0:rows, :, :], in_=ACCD[0:rows, :, :])
            t_prev = c.mark("dve", ins)
            dma_out.append(c.dma("sp", dst, OUTB[0:rows, :, :], waits=[t_prev]))
        ins = nc.vector.memset(HID[:, :, :], 0.0)
        t_h0 = c.mark("dve", ins)
        ins = nc.vector.memset(KST[:, :, :], 0.0)
        t_h0 = c.mark("dve", ins)
        t_pe = c.dma("pool", PET[:, :, :], d["peT"])
        c.wait("pe", t_pe, t_h0)
        c.wait("act", t_h0)
        tt_rd = [None, None]
        n = 0
        st_tok = None
        for kvi in range(2):
            w1a, r1a, s1a = self.ring.next()
            w1b, r1b, s1b = self.ring.next()
            w2, r2, s2 = self.ring.next()
            c.wait("pe", r1a, r1b, r2)
            w1 = lambda j: (w1a if j < 16 else w1b)[:, j % 16, :]
            bcb = self.ps.get()
            for hc in range(2):
                for j in range(32):
                    ins = nc.tensor.matmul(self.ps.banks[bcb][:, hc:hc + 1], w1(j)[:, hc * P:(hc + 1) * P], PET[:, kvi, j:j + 1],
                                           start=(j == 0), stop=(j == 31))
            tcb = c.mark("pe", ins)
            c.wait("act", tcb)
            ins = nc.scalar.copy(out=CB[:, 2 * kvi:2 * kvi + 2], in_=self.ps.banks[bcb][:, 0:2])
            t_cb = c.mark("act", ins)
            self.ps.release(bcb, t_cb)
            for g in range(2):
                a = n % 2
                n += 1
                t_tt = c.dma("sp", TT[:, a, :], d["kcvT"][kvi, :, g, :], waits=[tt_rd[a]])
                c.wait("pe", t_tt)
                for hc in range(2):
                    b = self.ps.get()
                    for j in range(32):
                        rhs = TT[:, a, j:j + 16 * (NCMP - 1) + 1:16]
                        ins = nc.tensor.matmul(self.ps.banks[b][:, 0:NCMP], w1(j)[:, hc * P:(hc + 1) * P], rhs,
                                               start=(j == 0), stop=(j == 31))
                    tp = c.mark("pe", ins)
                    c.wait("act", tp, t_cb)
                    ins = nc.scalar.activation(out=HID[:, hc, 0:NCMP], in_=self.ps.banks[b][:, 0:NCMP], func=AF.Gelu_apprx_tanh,
                                               bias=CB[:, 2 * kvi + hc:2 * kvi + hc + 1])
                    th = c.mark("act", ins)
                    self.ps.release(b, th)
                tt_rd[a] = tp
                c.wait("pe", th)
                if kvi == 0:
                    b = self.ps.get()
                    for hc in range(2):
                        ins = nc.tensor.matmul(self.ps.banks[b][:, :], w2[:, hc, :], HID[:, hc, :], start=(hc == 0), stop=(hc == 1))
                    t1 = c.mark("pe", ins)
                    c.wait("act", t1)
                    ins = nc.scalar.activation(out=SQT[:, :], in_=self.ps.banks[b][:, :], func=AF.Square)
                    t2 = c.mark("act", ins)
                    b2 = self.ps.get()
                    c.wait("pe", t2)
                    ins = nc.tensor.matmul(self.ps.banks[b2][:, :], self.ONES[:, :], SQT[:, :], start=True, stop=True)
                    t3 = c.mark("pe", ins)
                    c.wait("act", t3)
                    ins = nc.scalar.activation(out=RS[:, :], in_=self.ps.banks[b2][:, :], func=AF.Sqrt, bias=self.EPSB[:, 0:1], scale=1.0 / P)
                    t4 = c.mark("act", ins)
                    self.ps.release(b2, t4)
                    c.wait("dve", t4)
                    ins = nc.vector.reciprocal(out=RS[:, :], in_=RS[:, :])
                    t5 = c.mark("dve", ins)
                    c.wait("dve", t5)
                    ins = nc.vector.scalar_tensor_tensor(out=KST[:, g, 0:NCMP], in0=self.ps.banks[b][:, 0:NCMP], scalar=self.KN[:, 0:1],
                                                         in1=RS[:, 0:NCMP], op0=ALU.mult, op1=ALU.mult)
                    st_tok = c.mark("dve", ins)
                    self.ps.release(b, st_tok)
                    c.wait("pe", st_tok)
                    c.wait("act", st_tok)
                else:
                    for ch in range(4):
                        b = self.ps.get()
                        for hc in range(2):
                            ins = nc.tensor.matmul(self.ps.banks[b][:, 0:P], HID[:, hc, ch * P:(ch + 1) * P], w2[:, hc, :],
                                                   start=(hc == 0), stop=(hc == 1))
                        t1 = c.mark("pe", ins)
                        c.wait("act", t1)
                        ins = nc.scalar.copy(out=VST[:, ch, g, :], in_=self.ps.banks[b][:, 0:P])
                        st_tok = c.mark("act", ins)
                        self.ps.release(b, st_tok)
                    c.wait("pe", st_tok)
            self.ring.release(s1a, tp)
            self.ring.release(s1b, tp)
            self.ring.release(s2, t1)
            if kvi == 0:
                dma_out.append(c.dma("sp", d["KCTd"], KST[:, :, :], waits=[st_tok]))
            else:
                dma_out.append(c.dma("sp", d["VCd"], VST[:, :, :, :], waits=[st_tok]))
        self.setup_done = dma_out
        self.b1_read = tp
        self.setup_toks = [t_prev, st_tok, tp] + dma_out

    def plan_setup_B(self, d):
        for kvi in range(2):
            w1 = d["cmp_w1"][kvi]
            self.ring.add(wslab(w1, 0, 2048, 0, 256), KC, 256)
            self.ring.add(wslab(w1, 2048, 2048, 0, 256), KC, 256)
            self.ring.add(wslab(d["cmp_w2"][kvi], 0, 256, 0, P), 2, P)

    def plan_nsa(self, w_in, w_out, d):
        for s in range(8):
            self.ring.add(wslab(w_in, 0, D, s * 256, 256), KC, 256)
        self.ring.add(wslab(w_in, 0, D, D, 48), KC, 48)
        for g in range(2):
            self.ring.add(d["ksT_full"][:, g, 0:4096].rearrange("p (a t) -> p a t", a=1), 1, 4096)
            self.ring.add(d["ksT_full"][:, g, 4096:8192].rearrange("p (a t) -> p a t", a=1), 1, 4096)
            self.ring.add(d["vs_full"][0:4096, g * P:(g + 1) * P].rearrange("(t p) f -> p t f", p=P), 32, P)
            self.ring.add(d["vs_full"][4096:8192, g * P:(g + 1) * P].rearrange("(t p) f -> p t f", p=P), 32, P)
        for s in range(8):
            self.ring.add(wslab(w_out, 0, D, s * 256, 256), KC, 256)

    def normed_evac(self, b, tok, out_ap, scal_ap, st):
        c, nc = self.c, self.nc
        j = self.tmp_i % 2
        self.tmp_i += 1
        c.wait("act", tok, st["tb"][j])
        ins = nc.scalar.activation(out=self.TMPB[:, j, 0:512], in_=self.ps.banks[b][:, :], func=AF.Square)
        t1 = c.mark("act", ins)
        b2 = self.ps.get()
        c.wait("pe", t1)
        ins = nc.tensor.matmul(self.ps.banks[b2][:, :], self.ONES[:, :], self.TMPB[:, j, 0:512], start=True, stop=True)
        t2 = c.mark("pe", ins)
        st["tb"][j] = t2
        c.wait("act", t2, st["tf"][j])
        ins = nc.scalar.activation(out=self.TMPF[:, j, :], in_=self.ps.banks[b2][:, :], func=AF.Sqrt, bias=self.EPSB[:, 0:1], scale=1.0 / P)
        t3 = c.mark("act", ins)
        self.ps.release(b2, t3)
        c.wait("dve", t3)
        ins = nc.vector.reciprocal(out=self.TMPF[:, j, :], in_=self.TMPF[:, j, :])
        t4 = c.mark("dve", ins)
        c.wait("dve", t4)
        ins = nc.vector.scalar_tensor_tensor(out=out_ap, in0=self.ps.banks[b][:, :], scalar=scal_ap, in1=self.TMPF[:, j, :],
                                             op0=ALU.mult, op1=ALU.mult)
        ta = c.mark("dve", ins)
        st["tf"][j] = ta
        self.rSQ.done("act", t1); self.rSQ.done("pe", t2)
        self.rRS.done("act", t3); self.rRS.done("dve", ta)
        return ta

    def attn_unit(self, Q4, tiles, keepP=None):
        c, nc = self.c, self.nc
        ob = self.ps.get()
        n = len(tiles)

        def qk(i):
            t = tiles[i]
            sb = self.ps.get()
            adds = t.get("adds", [])
            s3 = self.ps.banks[sb][:, :].rearrange("p (h t) -> p h t", h=4)
            ins = nc.tensor.matmul(s3, t["kT"], Q4, start=True, stop=(len(adds) == 0))
            for ai, (l_, r_) in enumerate(adds):
                ins = nc.tensor.matmul(s3, l_, r_, start=False, stop=(ai == len(adds) - 1))
            return sb, c.mark("pe", ins)

        LOOK = 2
        pend = [qk(i) for i in range(min(LOOK, n))]
        last = None
        ta = None
        for i in range(n):
            if i + LOOK < n:
                pend.append(qk(i + LOOK))
            sb, tq = pend.pop(0)
            t = tiles[i]
            if keepP is not None:
                pbuf = keepP[:, i, :]
                prd = None
            else:
                s = self.pr_i % 3
                self.pr_i += 1
                pbuf = self.PR[:, s, :]
                prd = self.pr_rd[s]
            c.wait("act", tq, prd)
            if t.get("kb") is not None:
                ins = nc.scalar.activation(out=pbuf, in_=self.ps.banks[sb][:, :], func=AF.Exp, bias=t["kb"])
            else:
                ins = nc.scalar.activation(out=pbuf, in_=self.ps.banks[sb][:, :], func=AF.Exp)
            te = c.mark("act", ins)
            self.ps.release(sb, te)
            c.wait("pe", te)
            ins = nc.tensor.matmul(self.ps.banks[ob][:, :], t["v"], pbuf, start=(i == 0), stop=(i == n - 1))
            last = c.mark("pe", ins)
            c.wait("dve", te, ta, self.racc_rd if i == 0 else None)
            if i == 0:
                ins = nc.vector.tensor_copy(out=self.RACC, in_=pbuf)
            else:
                ins = nc.vector.tensor_tensor(out=self.RACC, in0=self.RACC, in1=pbuf, op=ALU.add)
            ta = c.mark("dve", ins)
            if keepP is None:
                self.pr_rd[s] = [last, ta]
        c.wait("dve", ta, self.rhl_rd)
        ins = nc.vector.tensor_copy(out=self.RHL[:, 0, :], in_=self.RACC)
        t_h = c.mark("dve", ins)
        c.wait("dve", t_h)
        ins = nc.vector.tensor_tensor(out=self.RHL[:, 1, :], in0=self.RACC, in1=self.RHL[:, 0, :], op=ALU.subtract)
        t_l = c.mark("dve", ins)
        self.racc_rd = t_l
        rb = self.ps.get()
        c.wait("pe", t_l)
        nc.tensor.matmul(self.ps.banks[rb][:, :], self.ONES[:, :], self.RHL[:, 0, :], start=True, stop=False)
        ins = nc.tensor.matmul(self.ps.banks[rb][:, :], self.ONES[:, :], self.RHL[:, 1, :], start=False, stop=True)
        last = c.mark("pe", ins)
        self.rhl_rd = last
        return ob, rb, last

    def combine(self, ob, rb, tok, hh, g, j, br, first):
        c, nc = self.c, self.nc
        gb = self.ps.get()
        c.wait("pe", self.gs_tok)
        for hi in range(4):
            h = 8 * g + 4 * hh + hi
            ins = nc.tensor.matmul(self.ps.banks[gb][:, hi * P:(hi + 1) * P], self.SELG[0:48, 3 * h + br, :],
                                   self.GS[0:48, j * P:(j + 1) * P], start=True, stop=True)
        tg = c.mark("pe", ins)
        c.wait("dve", tok, tg, self.rinv_rd)
        ins = nc.vector.tensor_scalar(out=self.RINV[:, :], in0=self.ps.banks[rb][:, :], scalar1=1e-30, scalar2=None, op0=ALU.add)
        t0 = c.mark("dve", ins)
        c.wait("dve", t0)
        ins = nc.vector.reciprocal(out=self.RINV[:, :], in_=self.RINV[:, :])
        t1 = c.mark("dve", ins)
        self.ps.release(rb, t1)
        c.wait("dve", t1)
        ins = nc.vector.tensor_tensor(out=self.GREP[:, :], in0=self.ps.banks[gb][:, :], in1=self.RINV[:, :], op=ALU.mult)
        t2 = c.mark("dve", ins)
        self.ps.release(gb, t2)
        c.wait("dve", t2)
        if first:
            ins = nc.vector.tensor_tensor(out=self.ACC[:, hh, :], in0=self.ps.banks[ob][:, :], in1=self.GREP[:, :], op=ALU.mult)
            t3 = c.mark("dve", ins)
        else:
            ins = nc.vector.tensor_tensor(out=self.GREP[:, :], in0=self.ps.banks[ob][:, :], in1=self.GREP[:, :], op=ALU.mult)
            t3a = c.mark("dve", ins)
            c.wait("dve", t3a)
            ins = nc.vector.tensor_tensor(out=self.ACC[:, hh, :], in0=self.ACC[:, hh, :], in1=self.GREP[:, :], op=ALU.add)
            t3 = c.mark("dve", ins)
        self.ps.release(ob, t3)
        return t1, t3

    def nsa(self, l, jl, d):
        c, nc = self.c, self.nc
        self.rmsnorm(NV_MIX + l)
        t_spill = c.dma("sp", d["xsp"], self.XT[:, :, :], waits=[self.ht_tok, self.xt_tok])
        st = {"tb": [None, None], "tf": [None, None]}
        self.rSQ.acquire("act")
        self.rRS.acquire("act")
        c.wait("dve", self.b1_read)

        def epi_q(m, h, ps, tok):
            b = self.ps.banks.index(ps)
            return self.normed_evac(b, tok, self.QT[:, m, h * 512:(h + 1) * 512], self.QN[:, jl:jl + 1], st)

        rhs = lambda k, h: self.HT[:, k, h * 512:(h + 1) * 512]
        self.linear_fm(8, KC, rhs, [self.ht_tok], epi_q)
        view, rtok, slot = self.ring.next()
        c.wait("pe", rtok)
        self.rX.acquire("act")
        for h in range(2):
            b = self.ps.get()
            for k in range(KC):
                ins = nc.tensor.matmul(self.ps.banks[b][0:48, :], view[:, k, :], rhs(k, h), start=(k == 0), stop=(k == KC - 1))
            tp = c.mark("pe", ins)
            c.wait("act", tp)
            ins = nc.scalar.activation(out=self.GS[0:48, h * 512:(h + 1) * 512], in_=self.ps.banks[b][0:48, :], func=AF.Sigmoid)
            self.gs_tok = c.mark("act", ins)
            self.ps.release(b, self.gs_tok)
        self.ring.release(slot, tp)
        self.ht_read = tp
        self.rX.done("act", self.gs_tok)
        q_tok = st["tf"][0], st["tf"][1]
        ld = []
        w8 = [t_spill, self.b2_read]
        L = lambda dst, src: ld.append(c.dma("sp", dst, src, waits=w8))
        L(self.D0[:, :, :], d["D0"]); L(self.D1[:, :, :], d["D1"]); L(self.D4, d["D4"])
        L(self.FC[0:17, :, :], d["FCd"]); L(self.SHC[0:17, :, :, :], d["SHC"]); L(self.SELF[:, :, :], d["SELF"])
        L(self.SELN[:, :, :, :], d["SELN"]); L(self.SELG[0:48, :, :], d["SELG"]); L(self.FV[:, :, :], d["FV"])
        L(self.OV[:, :, :], d["OV"]); L(self.KCT[:, :, :], d["KCTd"]); L(self.VC[:, :, :, :], d["VCd"])
        L(self.IDENT, d["IDENT"])
        L(self.KB, d["KB"])
        for e in ("pe", "act", "dve"):
            c.wait(e, *ld)
        c.wait("pe", *q_tok)
        c.wait("act", self.ht_read)
        c.wait("dve", self.ht_read)
        self.pr_i = 0
        self.pr_rd = [None, None, None]
        self.racc_rd = None
        self.rhl_rd = None
        self.rinv_rd = None
        KBW = self.KB[:, 0:40]
        KBS = self.KB[:, 40:56]
        KBF = self.KB[:, 56:56 + 512].rearrange("p (j k) -> p j k", j=8)
        loc_tok = {}
        loc_rd = [None, None]

        def load_local(it):
            g_, j_ = divmod(it, NT)
            b_ = it % 2
            w_ = w8 + [loc_rd[b_]]
            loc_tok[it] = [
                c.dma("sp", self.LKW[b_], d["kw_loc"][g_, j_], waits=w_),
                c.dma("sp", self.LVW[b_][:, :, :], d["vw_loc"][g_, j_].rearrange("(t p) e -> p t e", p=P), waits=w_),
                c.dma("sp", self.LKS[b_], d["ks_loc"][g_, j_], waits=w_),
                c.dma("sp", self.LVS[b_][:, :, :], d["vs_loc"][g_, j_].rearrange("(t p) e -> p t e", p=P), waits=w_),
            ]

        load_local(0)
        last_ot = None
        last_far = None
        self.pc_rd = None
        self.sel_rd = None
        for g in range(2):
            kA, rkA, skA = self.ring.next()
            kB, rkB, skB = self.ring.next()
            vA, rvA, svA = self.ring.next()
            vB, rvB, svB = self.ring.next()
            c.wait("pe", rkA, rkB, rvA, rvB)
            for j in range(NT):
                it = g * NT + j
                lb = it % 2
                if it + 1 < 2 * NT:
                    load_local(it + 1)
                c.wait("pe", *loc_tok[it])
                ib = self.ps.get()
                acc_t = [None, None]
                for hh in range(2):
                    h0 = 8 * g + 4 * hh
                    Q4 = self.QT[:, h0:h0 + 4, j * P:(j + 1) * P]
                    tiles = [dict(kT=self.KCT[:, g, ch * P:(ch + 1) * P], v=self.VC[:, ch, g, :],
                                  adds=[(self.SHC[0:17, j, ch, :], self.FC[0:17, h0:h0 + 4, :])]) for ch in range(4)]
                    c.wait("act", self.pc_rd)
                    ob, rb, tk = self.attn_unit(Q4, tiles, keepP=self.PC)
                    t1, t3 = self.combine(ob, rb, tk, hh, g, j, 0, True)
                    acc_t[hh] = t3
                    c.wait("dve", t1)
                    for ch in range(4):
                        ins = nc.vector.tensor_tensor(out=self.PC[:, ch, :], in0=self.PC[:, ch, :], in1=self.RINV[:, :], op=ALU.mult)
                    tn = c.mark("dve", ins)
                    self.rinv_rd = tn
                    c.wait("pe", tn)
                    for hi in range(4):
                        for ch in range(4):
                            ins = nc.tensor.matmul(self.ps.banks[ib][:, 0:P], self.PC[:, ch, hi * P:(hi + 1) * P], self.OV[:, ch, :],
                                                   start=(hh == 0 and hi == 0 and ch == 0), stop=(hh == 1 and hi == 3 and ch == 3))
                    self.pc_rd = c.mark("pe", ins)
                pc_rd = self.pc_rd
                c.wait("dve", pc_rd, self.sel_rd)
                ins = nc.vector.tensor_tensor(out=self.SCORE, in0=self.ps.banks[ib][:, 0:P], in1=self.FV[:, j, :], op=ALU.add)
                ts = c.mark("dve", ins)
                self.ps.release(ib, ts)
                c.wait("dve", ts)
                ins = nc.vector.max(out=self.M8[:, 0:8], in_=self.SCORE)
                ts = c.mark("dve", ins); c.wait("dve", ts)
                ins = nc.vector.match_replace(out=self.WORK, in_to_replace=self.M8[:, 0:8], in_values=self.SCORE, imm_value=-1e38)
                ts = c.mark("dve", ins); c.wait("dve", ts)
                ins = nc.vector.max(out=self.M8[:, 8:16], in_=self.WORK)
                ts = c.mark("dve", ins); c.wait("dve", ts)
                ins = nc.vector.match_replace(out=self.WORK, in_to_replace=self.M8[:, 8:16], in_values=self.WORK, imm_value=-1e38)
                ts = c.mark("dve", ins); c.wait("dve", ts)
                ins = nc.vector.tensor_tensor(out=self.WORK, in0=self.SCORE, in1=self.WORK, op=ALU.subtract)
                ts = c.mark("dve", ins); c.wait("dve", ts)
                ins = nc.vector.tensor_scalar(out=self.WORK, in0=self.WORK, scalar1=1.0, scalar2=None, op0=ALU.min)
                ts = c.mark("dve", ins); c.wait("dve", ts)
                ins = nc.vector.tensor_scalar(out=self.SEL, in0=self.SCORE, scalar1=-1e29, scalar2=None, op0=ALU.is_gt)
                ts = c.mark("dve", ins); c.wait("dve", ts)
                ins = nc.vector.tensor_tensor(out=self.SEL, in0=self.SEL, in1=self.WORK, op=ALU.mult)
                ts = c.mark("dve", ins); c.wait("dve", ts)
                ins = nc.vector.tensor_scalar(out=self.SELB, in0=self.SEL, scalar1=-NEG, scalar2=NEG, op0=ALU.mult, op1=ALU.add)
                ts = c.mark("dve", ins)
                tb_ = self.ps.get()
                c.wait("pe", ts)
                ins = nc.tensor.matmul(self.ps.banks[tb_][:, 0:P], self.SELB, self.IDENT, start=True, stop=True)
                tt = c.mark("pe", ins)
                self.sel_rd = tt
                c.wait("act", tt, last_far)
                ins = nc.scalar.copy(out=self.SNT, in_=self.ps.banks[tb_][:, 0:P])
                t_snt = c.mark("act", ins)
                self.ps.release(tb_, t_snt)
                c.wait("pe", t_snt)
                for hh in range(2):
                    h0 = 8 * g + 4 * hh
                    Q4 = self.QT[:, h0:h0 + 4, j * P:(j + 1) * P]
                    SN4 = lambda rows: self.SNT[rows, :].unsqueeze(1).to_broadcast([rows.stop - rows.start, 4, P])
                    tiles = []
                    for r in range(5):
                        t = dict(kT=self.LKW[lb][:, r * P:(r + 1) * P], v=self.LVW[lb][:, r, :], kb=KBW[:, 5 * j + r:5 * j + r + 1], adds=[])
                        if r == 4:
                            t["adds"].append((self.IDENT, self.D0[:, h0:h0 + 4, :]))
                        elif r == 3:
                            t["adds"].append((self.IDENT, self.D1[:, h0:h0 + 4, :]))
                        elif r == 0:
                            t["adds"].append((self.IDENT, self.D4.unsqueeze(1).to_broadcast([P, 4, P])))
                        tiles.append(t)
                    ob, rb, tk = self.attn_unit(Q4, tiles)
                    self.combine(ob, rb, tk, hh, g, j, 2, False)
                    tiles = []
                    for r in range(2):
                        tiles.append(dict(kT=self.LKS[lb][:, r * P:(r + 1) * P], v=self.LVS[lb][:, r, :], kb=KBS[:, 2 * j + r:2 * j + r + 1],
                                          adds=[(self.IDENT, (self.D1 if r == 0 else self.D0)[:, h0:h0 + 4, :]),
                                                (self.SELN[:, j, r, :], SN4(slice(0, P)))]))
                    for kg in range(8 * (j + 1)):
                        kv_ = kA if kg < 32 else kB
                        vv_ = vA if kg < 32 else vB
                        a = kg // 32
                        tiles.append(dict(kT=kv_[:, 0, (kg % 32) * P:(kg % 32 + 1) * P], v=vv_[:, kg % 32, :], kb=KBF[:, j, kg:kg + 1],
                                          adds=[(self.SELF[64 * a:64 * a + 64, kg % 32, :], SN4(slice(64 * a, 64 * a + 64)))]))
                    ob, rb, tk = self.attn_unit(Q4, tiles)
                    last_far = tk
                    t1, t3 = self.combine(ob, rb, tk, hh, g, j, 1, False)
                    self.rinv_rd = t1
                    c.wait("dve", t3)
                    ins = nc.vector.tensor_copy(out=self.OT[:, h0:h0 + 4, j * P:(j + 1) * P],
                                                in_=self.ACC[:, hh, :].rearrange("p (h t) -> p h t", h=4))
                    last_ot = c.mark("dve", ins)
                loc_rd[lb] = last_far
            for s_ in (skA, skB, svA, svB):
                self.ring.release(s_, last_far)
        t_x = c.dma("sp", self.XT[:, :, :], d["xsp"], waits=[last_far, last_ot, t_spill])
        self.xt_tok = t_x
        c.wait("dve", t_x)
        c.wait("act", t_x)
        self.ht_tok = last_ot
        self.b2_read = last_far
        self.b1_read = last_far
        self.rX.done("pe", last_far)
        last = self.linear_fm(8, KC, lambda k, h: self.OT[:, k, h * 512:(h + 1) * 512], [last_ot], self.add_to_xt)
        self.ht_read = last


def build_B(n_layers=2, dbg=False):
    nc = bass.Bass("TRN2", target_bir_lowering=False)
    dt = lambda name, shape, kind="ExternalInput", dtype=F32: nc.dram_tensor(name, list(shape), dtype, kind=kind).ap()
    d = {}
    xT_d = dt("xT", [D, T])
    pT_d = dt("pT", [2, PLE, T])
    nrm_d = dt("nrm", [P, NV, KC])
    knorm_d = dt("knormT", [P, 3])
    qn_d = dt("qnT", [P, 2])
    b_w_in = dt("b_w_in", [2, D, D + 48])
    b_w_out = dt("b_w_out", [2, D, D])
    if not dbg:
        ffn_w_in = dt("ffn_w_in", [2, D, 2 * FF])
        ffn_w_out = dt("ffn_w_out", [2, FF, D])
        ple_w = dt("ple_w", [2, PLE, D])
        ple_gate = dt("ple_gate", [2, D, D])
    d["cmp_w1"] = dt("cmp_w1", [2, 4096, 256])
    d["cmp_w2"] = dt("cmp_w2", [2, 256, P])
    d["peT"] = dt("peT", [P, 2, 32])
    d["relb"] = dt("relb", [1, 512])
    d["kcvT"] = dt("kcvT", [2, P, 2, SEQ], dtype=BF16)
    d["ksT_full"] = dt("ksT_full", [P, 2, SEQ], dtype=BF16)
    d["vs_full"] = dt("vs_full", [SEQ, 256], dtype=BF16)
    d["kw_loc"] = dt("kw_loc", [2, NT, P, 5 * P], dtype=BF16)
    d["vw_loc"] = dt("vw_loc", [2, NT, 5 * P, P], dtype=BF16)
    d["ks_loc"] = dt("ks_loc", [2, NT, P, 2 * P], dtype=BF16)
    d["vs_loc"] = dt("vs_loc", [2, NT, 2 * P, P], dtype=BF16)
    for name, shape in (("M0", [P, NB, P]), ("M1", [P, NB, P]), ("MC", [17, NB, P]), ("D4", [P, P]), ("SHC", [17, 8, 4, P]),
                        ("SELF", [P, 32, P]), ("SELN", [P, 8, 2, P]), ("SELG", [48, 48, P]), ("FV", [P, 8, P]), ("OV", [P, 4, P]),
                        ("IDENT", [P, P])):
        d[name] = dt(name, shape, dtype=BF16)
    d["KB"] = dt("KB", [P, 576])
    xo_d = dt("xT_out", [D, T], kind="ExternalOutput")
    it = lambda name, shape, dtype=BF16: nc.dram_tensor(name, list(shape), dtype, kind="Internal").ap()
    d["D0"] = it("D0s", [P, 16, P]); d["D1"] = it("D1s", [P, 16, P]); d["FCd"] = it("FCs", [17, 16, P])
    d["KCTd"] = it("KCTs", [P, 2, 512]); d["VCd"] = it("VCs", [P, 4, 2, P])
    d["xsp"] = it("xsp", [P, KC, T], F32)

    k = CoreB(nc)
    c = k.c
    k.plan_setup_B(d)
    for jl in range(n_layers):
        k.plan_nsa(b_w_in[jl], b_w_out[jl], d)
        if dbg:
            break
        k.plan_ffn(ffn_w_in[jl], ffn_w_out[jl])
        k.plan_ple(ple_w[jl], ple_gate[jl])
    k.setup_common(nrm_d, None)
    k.KN = nc.alloc_sbuf_tensor("KN", [P, 3], F32)
    k.QN = nc.alloc_sbuf_tensor("QN", [P, 2], F32)
    t_kn = c.dma("sp", k.KN[:, :], knorm_d)
    t_qn = c.dma("sp", k.QN[:, :], qn_d)
    c.wait("dve", t_kn, t_qn)
    ins = nc.vector.tensor_scalar(out=k.QN[:, :], in0=k.QN[:, :], scalar1=HSCALE, scalar2=None, op0=ALU.mult)
    c.mark("dve", ins)
    k.carve()
    k.setup_B(d)
    k.xt_tok = c.dma("sp", k.XT[:, :, :], xT_d.rearrange("(kc p) t -> p kc t", p=P), waits=k.setup_toks)
    c.wait("dve", k.xt_tok)
    k.b2_read = None
    for jl in range(n_layers):
        l = 2 + jl
        k.nsa(l, jl, d)
        if dbg:
            break
        k.ple_load(pT_d[jl])
        k.ffn(l)
        k.ple(l)
    outs = [c.dma("sp", xo_d.rearrange("(kc p) t -> p kc t", p=P), k.XT[:, :, :], waits=[k.xt_tok])]
    c.wait("sp", *outs)
    return nc


def t5_bucket_np(n):
    n = np.maximum(n, 0)
    nf = np.maximum(n, 1).astype(np.float32)
    large = 16 + (np.log(nf / np.float32(16)) / np.float32(np.log(128 / 16)) * np.float32(16)).astype(np.int32)
    large = np.minimum(large, 31)
    return np.where(n < 16, n, large)


def onehot_table(delta):
    b = np.where(delta < 0, 32, t5_bucket_np(delta))
    return (b[..., None] == np.arange(NB)).astype(np.float32)


def qmap(core, j):
    return 16 * (j // 2) + (core if j % 2 == 0 else 15 - core)


def host_tables_B(core):
    bf = lambda a: np.ascontiguousarray(a).astype(ml_dtypes.bfloat16)
    sig = np.arange(P)[:, None]
    tau = np.arange(P)[None, :]
    t = {}
    t["M0"] = bf(onehot_table(tau - sig).transpose(0, 2, 1))
    t["M1"] = bf(onehot_table(128 + tau - sig).transpose(0, 2, 1))
    io = np.arange(16)[:, None] - 9
    mc = onehot_table(tau - 16 * io - 31)
    fut = np.zeros((1, P, NB), np.float32); fut[:, :, 32] = 1.0
    t["MC"] = bf(np.concatenate([mc, fut], 0).transpose(0, 2, 1))
    t["D4"] = bf(np.where(tau >= sig, NEG, 0.0))
    shc = np.zeros((17, 8, 4, P), np.float32)
    seln = np.zeros((P, 8, 2, P), np.float32)
    fv = np.zeros((P, 8, P), np.float32)
    kbf = np.zeros((P, 8, 64), np.float32)
    blk = np.arange(P)
    for j in range(8):
        qg = qmap(core, j)
        idx = np.arange(512).reshape(4, P)
        for i_ in range(16):
            shc[i_, j] = (idx == 8 * qg + i_ - 9)
        shc[16, j] = (idx > 8 * qg + 6) | (idx >= NCMP)
        for r in range(2):
            kt = qg - 1 + r
            if kt >= 0:
                seln[:, j, r, :] = (blk[:, None] == 2 * kt + (np.arange(P)[None, :] // 64))
        tt = 128 * qg + np.arange(P)
        cur = tt // 64
        f = np.where(blk[None, :] <= cur[:, None], 0.0, -1e30).astype(np.float32)
        f[:, 0] = 1e30
        for q in range(P):
            if cur[q] >= 1:
                f[q, cur[q] - 1] = 3e30
            f[q, cur[q]] = 2e30
        fv[:, j, :] = f
        kg = np.arange(64)
        kbf[:, j, :] = np.where((kg == qg) | (kg == qg - 1) | (kg > qg), NEG, 0.0)[None, :]
    t["SHC"] = bf(shc); t["SELN"] = bf(seln); t["FV"] = bf(fv)
    selfar = np.zeros((P, 32, P), np.float32)
    for k_ in range(32):
        selfar[:, k_, :] = ((np.arange(P)[:, None] % 64) == 2 * k_ + (np.arange(P)[None, :] // 64))
    t["SELF"] = bf(selfar)
    t["SELG"] = bf(np.broadcast_to(np.eye(48, dtype=np.float32)[:, :, None], (48, 48, P)))
    c0 = np.arange(512) * 16
    s0 = np.arange(P) * 64
    lo = np.maximum(c0[:, None], s0[None, :]); hi = np.minimum(c0[:, None] + 32, s0[None, :] + 64)
    ov = np.maximum(hi - lo, 0).astype(np.float32) / 32.0
    ov[NCMP:] = 0
    t["OV"] = bf(ov.reshape(4, P, P).transpose(1, 0, 2))
    t["IDENT"] = bf(np.eye(P, dtype=np.float32))
    kb = np.zeros((P, 576), np.float32)
    for j in range(8):
        for r in range(5):
            kb[:, 5 * j + r] = NEG if (qmap(core, j) - 4 + r) < 0 else 0.0
        for r in range(2):
            kb[:, 40 + 2 * j + r] = NEG if (qmap(core, j) - 1 + r) < 0 else 0.0
    kb[:, 56:56 + 512] = kbf.reshape(P, 512)
    t["KB"] = kb
    return t


def host_inputs_B(inputs, core, x1, kvT, kvV):
    f = lambda a: np.ascontiguousarray(a, dtype=np.float32)
    rows = np.concatenate([np.arange(qmap(core, j) * P, (qmap(core, j) + 1) * P) for j in range(NT)])
    p = inputs["p"][2:4, 0][:, rows, :]
    vecs = np.zeros((NV, D), np.float32)
    vecs[NV_MIX:NV_MIX + 4] = inputs["norm_mix"]
    vecs[NV_FFN:NV_FFN + 4] = inputs["norm_ffn"]
    vecs[NV_PLE:NV_PLE + 4] = inputs["norm_ple"]
    nrm = vecs.reshape(NV, KC, P).transpose(2, 0, 1)
    fullT = lambda i: np.ascontiguousarray(np.concatenate([kvT[c_, i] for c_ in range(NCORES)], axis=-1))
    fullV = lambda i: np.ascontiguousarray(np.concatenate([kvV[c_, i] for c_ in range(NCORES)], axis=0))
    ksT, kwT = fullT(2), fullT(3)
    vs, vw = fullV(0), fullV(1)

    def locT(full, halo):
        out = np.zeros((2, NT, P, (halo + 1) * P), full.dtype)
        for j in range(NT):
            q = qmap(core, j)
            lo = (q - halo) * P
            s = max(lo, 0)
            out[:, j, :, s - lo:] = full[:, :, s:(q + 1) * P].transpose(1, 0, 2)
        return out

    def locV(full, halo):
        out = np.zeros((2, NT, (halo + 1) * P, P), full.dtype)
        for j in range(NT):
            q = qmap(core, j)
            lo = (q - halo) * P
            s = max(lo, 0)
            out[:, j, s - lo:, :] = full[s:(q + 1) * P].reshape(-1, 2, P).transpose(1, 0, 2)
        return out

    m = {
        "xT": f(x1[rows].T), "pT": f(p.transpose(0, 2, 1)), "nrm": f(nrm), "knormT": f(inputs["k_norm"].T), "qnT": f(inputs["b_q_norm"].T),
        "b_w_in": f(inputs["b_w_in"]), "b_w_out": f(inputs["b_w_out"]),
        "ffn_w_in": f(inputs["ffn_w_in"][2:4]), "ffn_w_out": f(inputs["ffn_w_out"][2:4]),
        "ple_w": f(inputs["ple_w"][2:4]), "ple_gate": f(inputs["ple_gate"][2:4]),
        "cmp_w1": f(np.stack([inputs["cmp_wk1"], inputs["cmp_wv1"]])), "cmp_w2": f(np.stack([inputs["cmp_wk2"], inputs["cmp_wv2"]])),
        "peT": f(np.stack([inputs["cmp_pe_k"].T, inputs["cmp_pe_v"].T], axis=1)),
        "relb": f(inputs["rel_bias"].reshape(1, 512)),
        "kcvT": np.ascontiguousarray(np.stack([fullT(0), fullT(1)])), "ksT_full": ksT, "vs_full": vs,
        "kw_loc": locT(kwT, 4), "vw_loc": locV(vw, 4), "ks_loc": locT(ksT, 1), "vs_loc": locV(vs, 1),
    }
    m.update(host_tables_B(core))
    return m


_NC_CACHE = {}


def kernel(**inputs):
    inputs = {k_: np.asarray(v) for k_, v in inputs.items()}
    if "A" not in _NC_CACHE:
        _NC_CACHE["A"] = build_A()
    resA = run_bass_kernel_spmd(_NC_CACHE["A"], [host_inputs_A(inputs, c_) for c_ in range(NCORES)], core_ids=list(range(NCORES)))
    kvT = np.stack([r["kvT"] for r in resA.results])
    kvV = np.stack([r["kvV"] for r in resA.results])
    x1 = np.concatenate([r["xT_out"].T for r in resA.results], axis=0)
    if "B" not in _NC_CACHE:
        _NC_CACHE["B"] = build_B()
    in_B = [host_inputs_B(inputs, c_, x1, kvT, kvV) for c_ in range(NCORES)]
    resB = run_bass_kernel_spmd(_NC_CACHE["B"], in_B, core_ids=list(range(NCORES)))
    out = np.zeros((SEQ, D), np.float32)
    for c_ in range(NCORES):
        y = resB.results[c_]["xT_out"].T
        for j in range(NT):
            q = qmap(c_, j)
            out[q * P:(q + 1) * P] = y[j * P:(j + 1) * P]
    return np.ascontiguousarray(out[None])
```

```python
import numpy as np
import ml_dtypes
import concourse.bass as bass
import concourse.mybir as mybir
from concourse.bass_utils import run_bass_kernel_spmd

F32 = mybir.dt.float32
BF16 = mybir.dt.bfloat16
AF = mybir.ActivationFunctionType
ALU = mybir.AluOpType

NCORES = 8
P = 128
SEQ = 8192
T = SEQ // NCORES
NT = T // P
D = 2048
KC = D // P
FF = 5632
HB = 11
PLE = 256
EPS = 1e-6
SLAB = 4096
NSLOT = 4

NV_MIX, NV_FFN, NV_PLE, NV_AV, NV_KV = 0, 4, 8, 12, 14
NV = 15


class Ctx:
    def __init__(self, nc):
        self.nc = nc
        self.E = {"pe": nc.tensor, "act": nc.scalar, "dve": nc.vector, "pool": nc.gpsimd, "sp": nc.sync}
        self.prog = {}
        self.seen = {}
        self.nsem = 0
        for e in ("pe", "act", "dve", "pool"):
            self.prog[e] = [self.sem("pg_" + e), 0]
        self.misc = [[self.sem("misc"), 0] for _ in range(12)]
        self.misc_i = 0

    def sem(self, name):
        self.nsem += 1
        return self.nc.alloc_semaphore(f"{name}_{self.nsem}")

    def mark(self, e, ins):
        p = self.prog[e]
        if p[1] >= 30000:
            p = self.prog[e] = [self.sem("pg_" + e), 0]
        p[1] += 1
        ins.then_inc(p[0], 1)
        return (p[0], p[1])

    def wait(self, e, *toks):
        for tok in toks:
            if tok is None:
                continue
            if isinstance(tok, list):
                self.wait(e, *tok)
                continue
            sem, val = tok
            k = (e, sem)
            if self.seen.get(k, 0) >= val:
                continue
            self.seen[k] = val
            self.E[e].wait_ge(sem, val)

    def dma(self, e, out, in_, waits=()):
        m = self.misc[self.misc_i % len(self.misc)]
        self.misc_i += 1
        if m[1] > 0:
            self.wait(e, (m[0], m[1]))
        self.wait(e, *waits)
        self.E[e].dma_start(out=out, in_=in_).then_inc(m[0], 16)
        m[1] += 16
        return (m[0], m[1])


class Ring:
    def __init__(self, c, nslot=NSLOT):
        self.c = c
        nc = c.nc
        self.n = nslot
        self.slots = [nc.alloc_sbuf_tensor(f"wslab{i}", [P, SLAB], BF16) for i in range(nslot)]
        self.sems = [c.sem("wr") for _ in range(nslot)]
        self.cnt = [0] * nslot
        self.free = [True] * nslot
        self.free_tok = [None] * nslot
        self.plan = []
        self.ready = {}
        self.issued = 0
        self.consumed = 0

    def add(self, src, kc, cols):
        self.plan.append((src, kc, cols))

    def view(self, s, kc, cols):
        return self.slots[s][:, 0:kc * cols].rearrange("p (k n) -> p k n", k=kc)

    def pump(self):
        while self.issued < len(self.plan):
            i = self.issued
            fs = [s for s in range(self.n) if self.free[s]]
            if not fs:
                break
            s = fs[0]
            src, kc, cols = self.plan[i]
            self.c.wait("pool", self.free_tok[s])
            self.c.nc.gpsimd.dma_start(out=self.view(s, kc, cols), in_=src).then_inc(self.sems[s], 16)
            self.cnt[s] += 1
            self.ready[i] = ((self.sems[s], 16 * self.cnt[s]), s)
            self.free[s] = False
            self.issued += 1

    def next(self):
        i = self.consumed
        self.consumed += 1
        self.pump()
        assert self.issued > i, "weight ring deadlock"
        tok, s = self.ready[i]
        _, kc, cols = self.plan[i]
        return self.view(s, kc, cols), tok, s

    def release(self, s, tok):
        self.free[s] = True
        self.free_tok[s] = tok
        self.pump()


class Psum:
    def __init__(self, c):
        self.c = c
        self.banks = [c.nc.alloc_psum_tensor(f"psb{i}", [P, 512], F32) for i in range(8)]
        self.free_tok = [None] * 8
        self.held = [False] * 8
        self.i = 0

    def get(self):
        for _ in range(8):
            b = self.i % 8
            self.i += 1
            if not self.held[b]:
                break
        else:
            raise RuntimeError("all PSUM banks held")
        self.held[b] = True
        self.c.wait("pe", self.free_tok[b])
        self.free_tok[b] = None
        return b

    def release(self, b, tok):
        self.held[b] = False
        self.free_tok[b] = tok


class Reg:
    def __init__(self, c):
        self.c = c
        self.toks = {}

    def acquire(self, e):
        self.c.wait(e, *[t for e2, t in self.toks.items() if e2 != e])

    def all(self):
        return list(self.toks.values())

    def done(self, e, tok):
        self.toks[e] = tok


def wslab(w2d, r0, nrows, c0, ncols):
    return w2d[r0:r0 + nrows, c0:c0 + ncols].rearrange("(kc p) n -> p kc n", p=P)


class Core:
    def __init__(self, nc):
        self.nc = nc
        self.c = Ctx(nc)
        self.ring = Ring(self.c)
        self.ps = Psum(self.c)
        a = nc.alloc_sbuf_tensor
        self.XT = a("XT", [P, KC, T], F32)
        self.HT = a("HT", [P, KC, T], BF16)
        self.B1 = a("B1", [P, KC * T], BF16)
        self.B2 = a("B2", [P, KC * T], BF16)
        self.SCR = a("SCR", [P, 6 * T], BF16)
        self.SQ = self.SCR[:, 0:2 * T].rearrange("p (a t) -> p a t", a=2)
        self.TMPB = self.SQ
        self.JUNK = self.SCR[:, 0:4 * T]
        self.BIAS = self.SCR[:, 0:4 * T].bitcast(F32)
        self.RSTD = self.SCR[:, 2 * T:4 * T].bitcast(F32)
        self.TMPF = self.RSTD.rearrange("p (a t) -> p a t", a=2)
        self.WST = self.SCR[:, 4 * T:6 * T].rearrange("p (g t) -> p g t", g=KC)
        self.PT = self.SCR[:, 4 * T:6 * T].rearrange("p (a t) -> p a t", a=2)
        self.rSQ, self.rRS, self.rX = Reg(self.c), Reg(self.c), Reg(self.c)
        self.NRM = a("NRM", [P, NV, KC], F32)
        self.ONES = a("ONES", [P, P], BF16)
        self.SMALL = a("SMALL", [P, 64], F32)
        self.EPSB = a("EPSB", [P, 1], F32)
        self.xt_tok = None
        self.ht_read = None
        self.ht_tok = None
        self.b1_read = None
        self.b2_read = None
        self.small_rd = None
        self.tmp_i = 0

    def rmsnorm(self, nv_idx):
        c, nc = self.c, self.nc
        pb = [self.ps.get(), self.ps.get()]
        self.rSQ.acquire("act")
        sq_rd = [None, None]
        last = None
        for k in range(KC):
            j = k % 2
            c.wait("act", self.xt_tok, sq_rd[j])
            ins = nc.scalar.activation(out=self.SQ[:, j, :], in_=self.XT[:, k, :], func=AF.Square)
            t = c.mark("act", ins)
            c.wait("pe", t)
            for h in range(2):
                ins = nc.tensor.matmul(self.ps.banks[pb[h]][:, :], self.ONES[:, :], self.SQ[:, j, h * 512:(h + 1) * 512],
                                       start=(k == 0), stop=(k == KC - 1))
            sq_rd[j] = last = c.mark("pe", ins)
        self.rSQ.done("act", t)
        self.rSQ.done("pe", last)
        c.wait("act", last)
        self.rRS.acquire("act")
        rel = None
        for h in range(2):
            sl = slice(h * 512, (h + 1) * 512)
            ins = nc.scalar.activation(out=self.RSTD[:, sl], in_=self.ps.banks[pb[h]][:, :], func=AF.Sqrt,
                                       bias=self.EPSB[:, 0:1], scale=1.0 / D)
            t = c.mark("act", ins)
            self.ps.release(pb[h], t)
            c.wait("dve", t)
            ins = nc.vector.reciprocal(out=self.RSTD[:, sl], in_=self.RSTD[:, sl])
            rel = c.mark("dve", ins)
        self.rRS.done("act", t)
        c.wait("dve", rel, self.ht_read, self.xt_tok)
        for k in range(KC):
            ins = nc.vector.scalar_tensor_tensor(out=self.HT[:, k, :], in0=self.XT[:, k, :],
                                                 scalar=self.NRM[:, nv_idx, k:k + 1], in1=self.RSTD[:, :],
                                                 op0=ALU.mult, op1=ALU.mult)
        self.ht_tok = c.mark("dve", ins)
        self.rRS.done("dve", self.ht_tok)
        return self.ht_tok

    def linear_fm(self, nslab, kc, rhs_fn, rhs_toks, epilogue):
        c, nc = self.c, self.nc
        last = None
        for s in range(nslab):
            view, rtok, slot = self.ring.next()
            c.wait("pe", rtok, *rhs_toks)
            ncol = view.shape[2] // P
            for mi in range(ncol):
                for h in range(2):
                    b = self.ps.get()
                    for k in range(kc):
                        ins = nc.tensor.matmul(self.ps.banks[b][:, :], view[:, k, mi * P:(mi + 1) * P], rhs_fn(k, h),
                                               start=(k == 0), stop=(k == kc - 1))
                    last = c.mark("pe", ins)
                    rel = epilogue(s * ncol + mi, h, self.ps.banks[b], last)
                    self.ps.release(b, rel)
            self.ring.release(slot, last)
        return last

    def add_to_xt(self, m, h, ps, tok):
        c, nc = self.c, self.nc
        sl = slice(h * 512, (h + 1) * 512)
        c.wait("dve", tok)
        ins = nc.vector.tensor_tensor(out=self.XT[:, m, sl], in0=ps[:, :], in1=self.XT[:, m, sl], op=ALU.add)
        self.xt_tok = c.mark("dve", ins)
        return self.xt_tok

    def plan_gmlp(self, w_in, w_out):
        for s in range(8):
            self.ring.add(wslab(w_in, 0, D, D + s * 256, 256), KC, 256)
        for s in range(8):
            self.ring.add(wslab(w_in, 0, D, s * 256, 256), KC, 256)
        for s in range(8):
            self.ring.add(wslab(w_out, 0, D, s * 256, 256), KC, 256)

    def gmlp(self, l, wsT_d, bs_d):
        c, nc = self.c, self.nc
        GV = self.B1[:, :].rearrange("p (t f) -> p t f", t=NT)
        SV = self.B2[:, :].rearrange("p (g t) -> p g t", g=KC)
        t_ws = c.dma("pool", self.WST[:, :, :], wsT_d[l].rearrange("g s t -> s g t"), waits=self.rX.all())
        c.wait("dve", t_ws, self.tri_tok)
        for g in range(KC):
            ins = nc.vector.tensor_tensor(out=self.WST[:, g, :], in0=self.WST[:, g, :], in1=self.TRI[:, :], op=ALU.mult)
        t_wst = c.mark("dve", ins)
        self.rX.done("dma", t_ws)
        self.rX.done("dve", t_wst)

        self.rmsnorm(NV_MIX + l)
        c.wait("act", self.b1_read)
        last = None
        for s in range(8):
            view, rtok, slot = self.ring.next()
            c.wait("pe", rtok, self.ht_tok)
            for t in range(NT):
                b = self.ps.get()
                for k in range(KC):
                    ins = nc.tensor.matmul(self.ps.banks[b][:, 0:256], self.HT[:, k, t * P:(t + 1) * P], view[:, k, :],
                                           start=(k == 0), stop=(k == KC - 1))
                last = c.mark("pe", ins)
                c.wait("act", last)
                ins = nc.scalar.activation(out=GV[:, t, s * 256:(s + 1) * 256], in_=self.ps.banks[b][:, 0:256],
                                           func=AF.Gelu_apprx_tanh)
                ta = c.mark("act", ins)
                self.ps.release(b, ta)
            self.ring.release(slot, last)
        SS = self.SMALL
        c.wait("dve", self.small_rd)
        ins = nc.vector.memset(SS[:, 0:16], 0.0)
        t0 = c.mark("dve", ins)
        c.wait("act", t0, ta)
        self.rSQ.acquire("act")
        self.rRS.acquire("act")
        for t in range(NT):
            ins = nc.scalar.activation(out=self.JUNK[:, 0:D], in_=GV[:, t, :], func=AF.Square, accum_out=SS[:, t:t + 1])
        t1 = c.mark("act", ins)
        self.rSQ.done("act", t1)
        self.rRS.done("act", t1)
        c.wait("act", t1)
        ins = nc.scalar.activation(out=SS[:, 8:16], in_=SS[:, 0:8], func=AF.Sqrt, bias=self.EPSB[:, 0:1], scale=1.0 / D)
        t1b = c.mark("act", ins)
        c.wait("dve", t1b)
        ins = nc.vector.reciprocal(out=SS[:, 8:16], in_=SS[:, 8:16])
        t2 = c.mark("dve", ins)
        c.wait("dve", t2)
        for t in range(NT):
            ins = nc.vector.tensor_scalar(out=GV[:, t, :], in0=GV[:, t, :], scalar1=SS[:, 8 + t:9 + t], scalar2=None,
                                          op0=ALU.mult)
        t_vn = c.mark("dve", ins)
        self.small_rd = t_vn
        t_bs = c.dma("sp", self.BIAS[:, :], bs_d[l:l + 1, :].to_broadcast([P, KC * P]), waits=self.rSQ.all() + self.rRS.all())
        self.rSQ.done("dma", t_bs)
        self.rRS.done("dma", t_bs)
        c.wait("pe", t_vn, t_wst)
        c.wait("dve", self.b2_read, t_bs)
        for t in range(NT):
            bb = [self.ps.get() for _ in range(4)]
            for g in range(KC):
                ins = nc.tensor.matmul(self.ps.banks[bb[g // 4]][:, (g % 4) * P:(g % 4 + 1) * P],
                                       GV[:, t, g * P:(g + 1) * P], self.WST[:, g, :], start=True, stop=True)
            tp = c.mark("pe", ins)
            c.wait("dve", tp)
            for g in range(KC):
                ins = nc.vector.scalar_tensor_tensor(out=SV[:, g, t * P:(t + 1) * P],
                                                     in0=self.ps.banks[bb[g // 4]][:, (g % 4) * P:(g % 4 + 1) * P],
                                                     scalar=self.NRM[:, NV_AV + l, g:g + 1],
                                                     in1=self.BIAS[:, g * P:(g + 1) * P], op0=ALU.mult, op1=ALU.add)
                if g % 4 == 3:
                    td = c.mark("dve", ins)
                    self.ps.release(bb[g // 4], td)
        self.b1_read = tp
        self.rX.done("pe", tp)
        self.rSQ.done("dve", td)
        self.rRS.done("dve", td)
        t_sv = td

        self.rSQ.acquire("act")
        tb_rd = [None, None]

        def epi_u(m, h, ps, tok):
            j = self.tmp_i % 2
            self.tmp_i += 1
            sl = slice(h * 512, (h + 1) * 512)
            c.wait("act", tok, tb_rd[j])
            ins = nc.scalar.activation(out=self.TMPB[:, j, 0:512], in_=ps[:, :], func=AF.Gelu_apprx_tanh)
            ta = c.mark("act", ins)
            self.rSQ.done("act", ta)
            c.wait("dve", ta, t_sv)
            ins = nc.vector.tensor_tensor(out=SV[:, m, sl], in0=self.TMPB[:, j, 0:512], in1=SV[:, m, sl], op=ALU.mult)
            tb_rd[j] = self.gt_tok = c.mark("dve", ins)
            self.rSQ.done("dve", self.gt_tok)
            return ta

        last = self.linear_fm(8, KC, lambda k, h: self.HT[:, k, h * 512:(h + 1) * 512], [self.ht_tok], epi_u)
        self.ht_read = last
        last = self.linear_fm(8, KC, lambda k, h: SV[:, k, h * 512:(h + 1) * 512], [self.gt_tok], self.add_to_xt)
        self.b2_read = last

    def plan_ffn(self, w_in, w_out):
        for hb in range(HB):
            for j in range(2):
                self.ring.add(wslab(w_in, 0, D, hb * 512 + j * 256, 256), KC, 256)
                self.ring.add(wslab(w_in, 0, D, FF + hb * 512 + j * 256, 256), KC, 256)
            for j in range(2):
                self.ring.add(wslab(w_out, hb * 512 + j * 256, 256, 0, D), 2, D)

    def ffn(self, l):
        c, nc = self.c, self.nc
        self.rmsnorm(NV_FFN + l)
        AB = self.B1[:, 0:3 * 4 * T].rearrange("p (r k t) -> p r k t", r=3, k=4)
        ab_rd = [self.b1_read, self.b1_read, self.b1_read]
        sg_rd = [None, None]
        SG = self.TMPB
        self.rSQ.acquire("act")
        rhs = lambda k, h: self.HT[:, k, h * 512:(h + 1) * 512]
        last_pe = None
        tp = None
        for hb in range(HB):
            r = hb % 3
            for j in range(2):
                def epi_g(m, h, ps, tok):
                    c.wait("act", tok, sg_rd[m] if h == 0 else None)
                    ins = nc.scalar.activation(out=SG[:, m, h * 512:(h + 1) * 512], in_=ps[:, :], func=AF.Silu)
                    self.sg_tok = c.mark("act", ins)
                    return self.sg_tok

                def epi_u(m, h, ps, tok, j=j, r=r):
                    c.wait("dve", tok, self.sg_tok, ab_rd[r])
                    ins = nc.vector.tensor_tensor(out=AB[:, r, 2 * j + m, h * 512:(h + 1) * 512], in0=ps[:, :],
                                                  in1=SG[:, m, h * 512:(h + 1) * 512], op=ALU.mult)
                    t = c.mark("dve", ins)
                    sg_rd[m] = t
                    self.ab_tok = t
                    return t

                self.linear_fm(1, KC, rhs, [self.ht_tok], epi_g)
                last_pe = self.linear_fm(1, KC, rhs, [self.ht_tok], epi_u)
            vA, rA, sA = self.ring.next()
            vB, rB, sB = self.ring.next()
            c.wait("pe", rA, rB, self.ab_tok)
            for m in range(KC):
                for h in range(2):
                    b = self.ps.get()
                    for kk in range(4):
                        v = vA if kk < 2 else vB
                        ins = nc.tensor.matmul(self.ps.banks[b][:, :], v[:, kk % 2, m * P:(m + 1) * P],
                                               AB[:, r, kk, h * 512:(h + 1) * 512], start=(kk == 0), stop=(kk == 3))
                    tp = c.mark("pe", ins)
                    rel = self.add_to_xt(m, h, self.ps.banks[b], tp)
                    self.ps.release(b, rel)
            self.ring.release(sA, tp)
            self.ring.release(sB, tp)
            ab_rd[r] = tp
        self.ht_read = last_pe
        self.b1_read = tp
        self.rSQ.done("act", self.sg_tok)
        self.rSQ.done("dve", self.ab_tok)

    def plan_ple(self, w_proj, w_gate):
        self.ring.add(wslab(w_proj, 0, PLE, 0, D), 2, D)
        for s in range(8):
            self.ring.add(wslab(w_gate, 0, D, s * 256, 256), KC, 256)

    def ple_load(self, pT_l):
        c = self.c
        self.pt_tok = c.dma("pool", self.PT[:, :, :], pT_l.rearrange("(kc p) t -> p kc t", p=P), waits=self.rX.all())
        self.rX.done("dma", self.pt_tok)

    def ple(self, l):
        c, nc = self.c, self.nc
        self.rmsnorm(NV_PLE + l)
        vW, rW, sW = self.ring.next()
        c.wait("pe", rW, self.pt_tok)
        state = {}
        self.rRS.acquire("act")
        tf_rd = [None, None]

        def epi(m, h, ps, tok):
            j = self.tmp_i % 2
            self.tmp_i += 1
            sl = slice(h * 512, (h + 1) * 512)
            c.wait("act", tok, tf_rd[j])
            ins = nc.scalar.activation(out=self.TMPF[:, j, :], in_=ps[:, :], func=AF.Sigmoid)
            ta = c.mark("act", ins)
            self.rRS.done("act", ta)
            b = self.ps.get()
            for k in range(2):
                ins = nc.tensor.matmul(self.ps.banks[b][:, :], vW[:, k, m * P:(m + 1) * P], self.PT[:, k, sl],
                                       start=(k == 0), stop=(k == 1))
            tp = c.mark("pe", ins)
            state["tp"] = tp
            c.wait("dve", ta, tp)
            ins = nc.vector.tensor_tensor(out=self.TMPF[:, j, :], in0=self.ps.banks[b][:, :], in1=self.TMPF[:, j, :], op=ALU.mult)
            t1 = c.mark("dve", ins)
            self.ps.release(b, t1)
            c.wait("dve", t1)
            ins = nc.vector.tensor_tensor(out=self.XT[:, m, sl], in0=self.TMPF[:, j, :], in1=self.XT[:, m, sl], op=ALU.add)
            self.xt_tok = tf_rd[j] = c.mark("dve", ins)
            self.rRS.done("dve", self.xt_tok)
            return ta

        last = self.linear_fm(8, KC, lambda k, h: self.HT[:, k, h * 512:(h + 1) * 512], [self.ht_tok], epi)
        self.ring.release(sW, state["tp"])
        self.ht_read = last
        self.rX.done("pe", state["tp"])

    def plan_kv(self, kv_w):
        for s in range(6):
            self.ring.add(wslab(kv_w, 0, D, s * 256, 256), KC, 256)

    def kv_proj(self, knorm_tile, kvT_d, kvV_d):
        c, nc = self.c, self.nc
        self.rmsnorm(NV_KV)
        KVT = self.B2[:, 0:4 * 2 * T].rearrange("p (i g t) -> p i g t", i=4, g=2)
        KVV = self.B1[:, 0:2 * NT * 256].rearrange("p (i t f) -> p i t f", i=2, t=NT)
        c.wait("act", self.b1_read, self.b2_read)
        c.wait("dve", self.b1_read, self.b2_read)
        self.rSQ.acquire("act")
        self.rRS.acquire("act")
        tb_rd = [None, None]
        tf_rd = [None, None]
        fm_idx = {0: 0, 1: 1, 2: 2, 4: 3}
        tm_idx = {3: 0, 5: 1}
        outs = []
        last = None
        for kind in range(6):
            view, rtok, slot = self.ring.next()
            c.wait("pe", rtok, self.ht_tok)
            if kind in tm_idx:
                for t in range(NT):
                    b = self.ps.get()
                    for k in range(KC):
                        ins = nc.tensor.matmul(self.ps.banks[b][:, 0:256], self.HT[:, k, t * P:(t + 1) * P], view[:, k, :],
                                               start=(k == 0), stop=(k == KC - 1))
                    last = c.mark("pe", ins)
                    c.wait("act", last)
                    ins = nc.scalar.copy(out=KVV[:, tm_idx[kind], t, :], in_=self.ps.banks[b][:, 0:256])
                    ta = c.mark("act", ins)
                    self.ps.release(b, ta)
                outs.append(c.dma("sp", kvV_d[tm_idx[kind]].rearrange("(t p) f -> p t f", p=P), KVV[:, tm_idx[kind], :, :], waits=[ta]))
            else:
                i = fm_idx[kind]
                for g in range(2):
                    for h in range(2):
                        sl = slice(h * 512, (h + 1) * 512)
                        b = self.ps.get()
                        for k in range(KC):
                            ins = nc.tensor.matmul(self.ps.banks[b][:, :], view[:, k, g * P:(g + 1) * P], self.HT[:, k, sl],
                                                   start=(k == 0), stop=(k == KC - 1))
                        last = c.mark("pe", ins)
                        if kind in (0, 1):
                            c.wait("act", last)
                            ins = nc.scalar.copy(out=KVT[:, i, g, sl], in_=self.ps.banks[b][:, :])
                            ta = c.mark("act", ins)
                            self.ps.release(b, ta)
                        else:
                            j = self.tmp_i % 2
                            self.tmp_i += 1
                            c.wait("act", last, tb_rd[j])
                            ins = nc.scalar.activation(out=self.TMPB[:, j, 0:512], in_=self.ps.banks[b][:, :], func=AF.Square)
                            t1 = c.mark("act", ins)
                            b2 = self.ps.get()
                            c.wait("pe", t1)
                            ins = nc.tensor.matmul(self.ps.banks[b2][:, :], self.ONES[:, :], self.TMPB[:, j, 0:512], start=True, stop=True)
                            t2 = c.mark("pe", ins)
                            tb_rd[j] = t2
                            c.wait("act", t2, tf_rd[j])
                            ins = nc.scalar.activation(out=self.TMPF[:, j, :], in_=self.ps.banks[b2][:, :], func=AF.Sqrt,
                                                       bias=self.EPSB[:, 0:1], scale=1.0 / P)
                            t3 = c.mark("act", ins)
                            self.ps.release(b2, t3)
                            c.wait("dve", t3)
                            ins = nc.vector.reciprocal(out=self.TMPF[:, j, :], in_=self.TMPF[:, j, :])
                            t4 = c.mark("dve", ins)
                            c.wait("dve", t4)
                            kn = 1 if kind == 2 else 2
                            ins = nc.vector.scalar_tensor_tensor(out=KVT[:, i, g, sl], in0=self.ps.banks[b][:, :],
                                                                 scalar=knorm_tile[:, kn:kn + 1],
                                                                 in1=self.TMPF[:, j, :], op0=ALU.mult, op1=ALU.mult)
                            ta = c.mark("dve", ins)
                            tf_rd[j] = ta
                            self.ps.release(b, ta)
                            self.rSQ.done("act", t1); self.rSQ.done("pe", t2)
                            self.rRS.done("act", t3); self.rRS.done("dve", ta)
                outs.append(c.dma("sp", kvT_d[i], KVT[:, i, :, :], waits=[ta]))
            self.ring.release(slot, last)
        self.ht_read = last
        return outs

    def setup_common(self, nrm_d, tri_d=None):
        c, nc = self.c, self.nc
        a = nc.alloc_sbuf_tensor
        ins = nc.vector.memset(self.ONES[:, :], 1.0)
        ins = nc.vector.memset(self.EPSB[:, :], EPS)
        t = c.mark("dve", ins)
        c.wait("act", t)
        c.wait("pe", t)
        t_n = c.dma("sp", self.NRM[:, :, :], nrm_d)
        c.wait("dve", t_n)
        if tri_d is not None:
            self.TRI = a("TRI", [P, P], BF16)
            self.tri_tok = c.dma("pool", self.TRI[:, :], tri_d)


def build_A(n_layers=2, do_kv=True, stop=None):
    nc = bass.Bass("TRN2", target_bir_lowering=False)
    dt = lambda name, shape, kind="ExternalInput", dtype=F32: nc.dram_tensor(name, list(shape), dtype, kind=kind).ap()
    xT_d = dt("xT", [D, T])
    pT_d = dt("pT", [4, PLE, T])
    nrm_d = dt("nrm", [P, NV, KC])
    tri_d = dt("tri", [P, P])
    knorm_d = dt("knormT", [P, 3])
    a_w_in = dt("a_w_in", [2, D, 2 * D])
    a_wsT = dt("a_wsT", [2, KC, P, P])
    a_b_s = dt("a_b_s", [2, KC * P])
    a_w_out = dt("a_w_out", [2, D, D])
    ffn_w_in = dt("ffn_w_in", [4, D, 2 * FF])
    ffn_w_out = dt("ffn_w_out", [4, FF, D])
    ple_w = dt("ple_w", [4, PLE, D])
    ple_gate = dt("ple_gate", [4, D, D])
    kv_w = dt("kv_w", [D, 1536])
    xo_d = dt("xT_out", [D, T], kind="ExternalOutput")
    kvT_d = dt("kvT", [4, P, 2, T], kind="ExternalOutput", dtype=BF16)
    kvV_d = dt("kvV", [2, T, 256], kind="ExternalOutput", dtype=BF16)

    k = Core(nc)
    c = k.c
    for l in range(n_layers):
        lastl = (l == n_layers - 1)
        k.plan_gmlp(a_w_in[l], a_w_out[l])
        if lastl and stop == "gmlp":
            break
        k.plan_ffn(ffn_w_in[l], ffn_w_out[l])
        if lastl and stop == "ffn":
            break
        k.plan_ple(ple_w[l], ple_gate[l])
    if do_kv:
        k.plan_kv(kv_w)
    k.setup_common(nrm_d, tri_d)
    KN = nc.alloc_sbuf_tensor("KN", [P, 3], F32)
    t_kn = c.dma("sp", KN[:, :], knorm_d)
    c.wait("dve", t_kn)
    k.xt_tok = c.dma("sp", k.XT[:, :, :], xT_d.rearrange("(kc p) t -> p kc t", p=P))
    c.wait("dve", k.xt_tok)
    for l in range(n_layers):
        lastl = (l == n_layers - 1)
        k.gmlp(l, a_wsT, a_b_s)
        if lastl and stop == "gmlp":
            break
        k.ple_load(pT_d[l])
        k.ffn(l)
        if lastl and stop == "ffn":
            break
        k.ple(l)
    outs = [c.dma("sp", xo_d.rearrange("(kc p) t -> p kc t", p=P), k.XT[:, :, :], waits=[k.xt_tok])]
    if do_kv:
        outs += k.kv_proj(KN, kvT_d, kvV_d)
    c.wait("sp", *outs)
    return nc


def host_inputs_A(inputs, core):
    f = lambda a: np.ascontiguousarray(a, dtype=np.float32)
    tok = slice(core * T, (core + 1) * T)
    x = inputs["x"][0, tok, :]
    p = inputs["p"][:, 0, tok, :]
    vecs = np.zeros((NV, D), np.float32)
    vecs[NV_MIX:NV_MIX + 4] = inputs["norm_mix"]
    vecs[NV_FFN:NV_FFN + 4] = inputs["norm_ffn"]
    vecs[NV_PLE:NV_PLE + 4] = inputs["norm_ple"]
    vecs[NV_AV:NV_AV + 2] = inputs["a_norm_v"]
    vecs[NV_KV] = inputs["kv_norm"]
    nrm = vecs.reshape(NV, KC, P).transpose(2, 0, 1)
    tri = (np.arange(P)[:, None] <= np.arange(P)[None, :]).astype(np.float32)
    return {
        "xT": f(x.T), "pT": f(p.transpose(0, 2, 1)), "nrm": f(nrm), "tri": tri,
        "knormT": f(inputs["k_norm"].T),
        "a_w_in": f(inputs["a_w_in"]), "a_wsT": f(np.transpose(inputs["a_w_s"], (0, 1, 3, 2))),
        "a_b_s": f(inputs["a_b_s"].reshape(2, KC * P)), "a_w_out": f(inputs["a_w_out"]),
        "ffn_w_in": f(inputs["ffn_w_in"]), "ffn_w_out": f(inputs["ffn_w_out"]),
        "ple_w": f(inputs["ple_w"]), "ple_gate": f(inputs["ple_gate"]), "kv_w": f(inputs["kv_w"]),
    }


NEG = -30000.0
HSCALE = 128.0 ** -0.5
NCMP = 511
NB = 33


class Flat:
    def __init__(self, ap):
        self.ap = ap
        self.off = 0

    def take(self, n):
        v = self.ap[:, self.off:self.off + n]
        self.off += n
        assert self.off <= self.ap.shape[1], (self.off, self.ap.shape)
        return v


class CoreB(Core):
    def carve(self):
        X = Flat(self.XT[:, :, :].rearrange("p a t -> p (a t)").bitcast(BF16))
        r = lambda v, s, **kw: v.rearrange(s, **kw)
        self.D0 = r(X.take(2048), "p (h t) -> p h t", h=16)
        self.D1 = r(X.take(2048), "p (h t) -> p h t", h=16)
        self.D4 = X.take(128)
        self.FC = r(X.take(2048), "p (h t) -> p h t", h=16)
        self.SHC = r(X.take(4096), "p (j c i) -> p j c i", j=8, c=4)
        self.SELF = r(X.take(4096), "p (k s) -> p k s", k=32)
        self.SELN = r(X.take(2048), "p (j r s) -> p j r s", j=8, r=2)
        self.SELG = r(X.take(6144), "p (c m) -> p c m", c=48)
        self.FV = r(X.take(1024), "p (j b) -> p j b", j=8)
        self.OV = r(X.take(512), "p (c b) -> p c b", c=4)
        self.KCT = r(X.take(1024), "p (g i) -> p g i", g=2)
        self.VC = r(X.take(1024), "p (c g d) -> p c g d", c=4, g=2)
        self.IDENT = X.take(128)
        Y = Flat(self.B2[:, :])
        self.LKW = [Y.take(640) for _ in range(2)]
        self.LVW = [r(Y.take(640), "p (t d) -> p t d", t=5) for _ in range(2)]
        self.LKS = [Y.take(256) for _ in range(2)]
        self.LVS = [r(Y.take(256), "p (t d) -> p t d", t=2) for _ in range(2)]
        self.KB = Y.take(1152).bitcast(F32)
        self.PC = r(Y.take(2048), "p (c n) -> p c n", c=4)
        self.PR = r(Y.take(1536), "p (c n) -> p c n", c=3)
        self.ACC = r(Y.take(2048).bitcast(F32), "p (a n) -> p a n", a=2)
        self.RINV = Y.take(1024).bitcast(F32)
        self.GREP = Y.take(1024).bitcast(F32)
        self.SCORE = Y.take(256).bitcast(F32)
        self.WORK = Y.take(256).bitcast(F32)
        self.SEL = Y.take(256).bitcast(F32)
        self.M8 = Y.take(32).bitcast(F32)
        self.SNT = Y.take(128)
        self.SELB = Y.take(128)
        self.GS = self.SCR[:, 4 * T:5 * T]
        self.QT = self.B1[:, :].rearrange("p (h t) -> p h t", h=16)
        self.OT = self.HT

    def setup_B(self, d):
        c, nc = self.c, self.nc
        X = Flat(self.XT[:, :, :].rearrange("p a t -> p (a t)").bitcast(BF16))
        MM = X.take(NB * 128).rearrange("p (b t) -> p b t", b=NB)
        TAB = X.take(2 * NB * 16).bitcast(F32).rearrange("p (b h) -> p b h", b=NB)
        ACCD = X.take(2 * 2048).bitcast(F32).rearrange("p (h t) -> p h t", h=16)
        OUTB = X.take(2048).rearrange("p (h t) -> p h t", h=16)
        HID = X.take(1024).rearrange("p (c i) -> p c i", c=2)
        PET = X.take(64).rearrange("p (a j) -> p a j", a=2)
        CB = X.take(8).bitcast(F32)
        KST = X.take(1024).rearrange("p (g i) -> p g i", g=2)
        VST = X.take(1024).rearrange("p (c g d) -> p c g d", c=4, g=2)
        SQT = X.take(512)
        RS = X.take(1024).bitcast(F32)
        TT = self.B1[:, :].rearrange("p (a t) -> p a t", a=2)
        t_tab = c.dma("sp", TAB[:, 0:32, :], d["relb"].to_broadcast([P, 512]).rearrange("p (b h) -> p b h", b=32))
        c.wait("dve", t_tab)
        for b in range(31):
            ins = nc.vector.tensor_tensor(out=TAB[:, b, :], in0=TAB[:, b, :], in1=TAB[:, 31, :], op=ALU.subtract)
        ins = nc.vector.memset(TAB[:, 31, :], 0.0)
        ins = nc.vector.memset(TAB[:, 32, :], NEG)
        t_prev = c.mark("dve", ins)
        dma_out = []
        for name, rows, dst in (("M0", P, d["D0"]), ("M1", P, d["D1"]), ("MC", 17, d["FCd"])):
            t_m = c.dma("pool", MM[0:rows, :, :], d[name], waits=[t_prev] + dma_out[-1:])
            c.wait("dve", t_m, t_prev)
            for h in range(16):
                ins = nc.vector.tensor_scalar(out=ACCD[0:rows, h, :], in0=MM[0:rows, 0, :], scalar1=TAB[0:rows, 0, h:h + 1],
                                              scalar2=None, op0=ALU.mult)
                for b in range(1, NB):
                    ins = nc.vector.scalar_tensor_tensor(out=ACCD[0:rows, h, :], in0=MM[0:rows, b, :],
                                                         scalar=TAB[0:rows, b, h:h + 1], in1=ACCD[0:rows, h, :],
                                                         op0=ALU.mult, op1=ALU.add)
            t1 = c.mark("dve", ins)
            c.wait("dve", t1, *dma_out[-1:])
            ins = nc.vector.tensor_copy(out=OUTB[0:rows, :, :], in_=ACCD[0:rows, :, :])
            t_prev = c.mark("dve", ins)
            dma_out.append(c.dma("sp", dst, OUTB[0:rows, :, :], waits=[t_prev]))
        ins = nc.vector.memset(HID[:, :, :], 0.0)
        t_h0 = c.mark("dve", ins)
        ins = nc.vector.memset(KST[:, :, :], 0.0)
        t_h0 = c.mark("dve", ins)
        t_pe = c.dma("pool", PET[:, :, :], d["peT"])
        c.wait("pe", t_pe, t_h0)
        c.wait("act", t_h0)
        tt_rd = [None, None]
        n = 0
        st_tok = None
        for kvi in range(2):
            w1a, r1a, s1a = self.ring.next()
            w1b, r1b, s1b = self.ring.next()
            w2, r2, s2 = self.ring.next()
            c.wait("pe", r1a, r1b, r2)
            w1 = lambda j: (w1a if j < 16 else w1b)[:, j % 16, :]
            bcb = self.ps.get()
            for hc in range(2):
                for j in range(32):
                    ins = nc.tensor.matmul(self.ps.banks[bcb][:, hc:hc + 1], w1(j)[:, hc * P:(hc + 1) * P], PET[:, kvi, j:j + 1],
                                           start=(j == 0), stop=(j == 31))
            tcb = c.mark("pe", ins)
            c.wait("act", tcb)
            ins = nc.scalar.copy(out=CB[:, 2 * kvi:2 * kvi + 2], in_=self.ps.banks[bcb][:, 0:2])
            t_cb = c.mark("act", ins)
            self.ps.release(bcb, t_cb)
            for g in range(2):
                a = n % 2
                n += 1
                t_tt = c.dma("sp", TT[:, a, :], d["kcvT"][kvi, :, g, :], waits=[tt_rd[a]])
                c.wait("pe", t_tt)
                for hc in range(2):
                    b = self.ps.get()
                    for j in range(32):
                        rhs = TT[:, a, j:j + 16 * (NCMP - 1) + 1:16]
                        ins = nc.tensor.matmul(self.ps.banks[b][:, 0:NCMP], w1(j)[:, hc * P:(hc + 1) * P], rhs,
                                               start=(j == 0), stop=(j == 31))
                    tp = c.mark("pe", ins)
                    c.wait("act", tp, t_cb)
                    ins = nc.scalar.activation(out=HID[:, hc, 0:NCMP], in_=self.ps.banks[b][:, 0:NCMP], func=AF.Gelu_apprx_tanh,
                                               bias=CB[:, 2 * kvi + hc:2 * kvi + hc + 1])
                    th = c.mark("act", ins)
                    self.ps.release(b, th)
                tt_rd[a] = tp
                c.wait("pe", th)
                if kvi == 0:
                    b = self.ps.get()
                    for hc in range(2):
                        ins = nc.tensor.matmul(self.ps.banks[b][:, :], w2[:, hc, :], HID[:, hc, :], start=(hc == 0), stop=(hc == 1))
                    t1 = c.mark("pe", ins)
                    c.wait("act", t1)
                    ins = nc.scalar.activation(out=SQT[:, :], in_=self.ps.banks[b][:, :], func=AF.Square)
                    t2 = c.mark("act", ins)
                    b2 = self.ps.get()
                    c.wait("pe", t2)
                    ins = nc.tensor.matmul(self.ps.banks[b2][:, :], self.ONES[:, :], SQT[:, :], start=True, stop=True)
                    t3 = c.mark("pe", ins)
                    c.wait("act", t3)
                    ins = nc.scalar.activation(out=RS[:, :], in_=self.ps.banks[b2][:, :], func=AF.Sqrt, bias=self.EPSB[:, 0:1], scale=1.0 / P)
                    t4 = c.mark("act", ins)
                    self.ps.release(b2, t4)
                    c.wait("dve", t4)
                    ins = nc.vector.reciprocal(out=RS[:, :], in_=RS[:, :])
                    t5 = c.mark("dve", ins)
                    c.wait("dve", t5)
                    ins = nc.vector.scalar_tensor_tensor(out=KST[:, g, 0:NCMP], in0=self.ps.banks[b][:, 0:NCMP], scalar=self.KN[:, 0:1],
                                                         in1=RS[:, 0:NCMP], op0=ALU.mult, op1=ALU.mult)
                    st_tok = c.mark("dve", ins)
                    self.ps.release(b, st_tok)
                    c.wait("pe", st_tok)
                    c.wait("act", st_tok)
                else:
                    for ch in range(4):
                        b = self.ps.get()
                        for hc in range(2):
                            ins = nc.tensor.matmul(self.ps.banks[b][:, 0:P], HID[:, hc, ch * P:(ch + 1) * P], w2[:, hc, :],
                                                   start=(hc == 0), stop=(hc == 1))
                        t1 = c.mark("pe", ins)
                        c.wait("act", t1)
                        ins = nc.scalar.copy(out=VST[:, ch, g, :], in_=self.ps.banks[b][:, 0:P])
                        st_tok = c.mark("act", ins)
                        self.ps.release(b, st_tok)
                    c.wait("pe", st_tok)
            self.ring.release(s1a, tp)
            self.ring.release(s1b, tp)
            self.ring.release(s2, t1)
            if kvi == 0:
                dma_out.append(c.dma("sp", d["KCTd"], KST[:, :, :], waits=[st_tok]))
            else:
                dma_out.append(c.dma("sp", d["VCd"], VST[:, :, :, :], waits=[st_tok]))
        self.setup_done = dma_out
        self.b1_read = tp
        self.setup_toks = [t_prev, st_tok, tp] + dma_out

    def plan_setup_B(self, d):
        for kvi in range(2):
            w1 = d["cmp_w1"][kvi]
            self.ring.add(wslab(w1, 0, 2048, 0, 256), KC, 256)
            self.ring.add(wslab(w1, 2048, 2048, 0, 256), KC, 256)
            self.ring.add(wslab(d["cmp_w2"][kvi], 0, 256, 0, P), 2, P)

    def plan_nsa(self, w_in, w_out, d):
        for s in range(8):
            self.ring.add(wslab(w_in, 0, D, s * 256, 256), KC, 256)
        self.ring.add(wslab(w_in, 0, D, D, 48), KC, 48)
        for g in range(2):
            self.ring.add(d["ksT_full"][:, g, 0:4096].rearrange("p (a t) -> p a t", a=1), 1, 4096)
            self.ring.add(d["ksT_full"][:, g, 4096:8192].rearrange("p (a t) -> p a t", a=1), 1, 4096)
            self.ring.add(d["vs_full"][0:4096, g * P:(g + 1) * P].rearrange("(t p) f -> p t f", p=P), 32, P)
            self.ring.add(d["vs_full"][4096:8192, g * P:(g + 1) * P].rearrange("(t p) f -> p t f", p=P), 32, P)
        for s in range(8):
            self.ring.add(wslab(w_out, 0, D, s * 256, 256), KC, 256)

    def normed_evac(self, b, tok, out_ap, scal_ap, st):
        c, nc = self.c, self.nc
        j = self.tmp_i % 2
        self.tmp_i += 1
        c.wait("act", tok, st["tb"][j])
        ins = nc.scalar.activation(out=self.TMPB[:, j, 0:512], in_=self.ps.banks[b][:, :], func=AF.Square)
        t1 = c.mark("act", ins)
        b2 = self.ps.get()
        c.wait("pe", t1)
        ins = nc.tensor.matmul(self.ps.banks[b2][:, :], self.ONES[:, :], self.TMPB[:, j, 0:512], start=True, stop=True)
        t2 = c.mark("pe", ins)
        st["tb"][j] = t2
        c.wait("act", t2, st["tf"][j])
        ins = nc.scalar.activation(out=self.TMPF[:, j, :], in_=self.ps.banks[b2][:, :], func=AF.Sqrt, bias=self.EPSB[:, 0:1], scale=1.0 / P)
        t3 = c.mark("act", ins)
        self.ps.release(b2, t3)
        c.wait("dve", t3)
        ins = nc.vector.reciprocal(out=self.TMPF[:, j, :], in_=self.TMPF[:, j, :])
        t4 = c.mark("dve", ins)
        c.wait("dve", t4)
        ins = nc.vector.scalar_tensor_tensor(out=out_ap, in0=self.ps.banks[b][:, :], scalar=scal_ap, in1=self.TMPF[:, j, :],
                                             op0=ALU.mult, op1=ALU.mult)
        ta = c.mark("dve", ins)
        st["tf"][j] = ta
        self.rSQ.done("act", t1); self.rSQ.done("pe", t2)
        self.rRS.done("act", t3); self.rRS.done("dve", ta)
        return ta

    def attn_unit(self, Q4, tiles, keepP=None):
        c, nc = self.c, self.nc
        ob, rb = self.ps.get(), self.ps.get()
        n = len(tiles)

        def qk(i):
            t = tiles[i]
            sb = self.ps.get()
            adds = t.get("adds", [])
            s3 = self.ps.banks[sb][:, :].rearrange("p (h t) -> p h t", h=4)
            ins = nc.tensor.matmul(s3, t["kT"], Q4, start=True, stop=(len(adds) == 0))
            for ai, (l_, r_) in enumerate(adds):
                ins = nc.tensor.matmul(s3, l_, r_, start=False, stop=(ai == len(adds) - 1))
            return sb, c.mark("pe", ins)

        LOOK = 3
        pend = [qk(i) for i in range(min(LOOK, n))]
        last = None
        for i in range(n):
            if i + LOOK < n:
                pend.append(qk(i + LOOK))
            sb, tq = pend.pop(0)
            t = tiles[i]
            if keepP is not None:
                pbuf = keepP[:, i, :]
                prd = None
            else:
                s = self.pr_i % 3
                self.pr_i += 1
                pbuf = self.PR[:, s, :]
                prd = self.pr_rd[s]
            c.wait("act", tq, prd)
            if t.get("kb") is not None:
                ins = nc.scalar.activation(out=pbuf, in_=self.ps.banks[sb][:, :], func=AF.Exp, bias=t["kb"])
            else:
                ins = nc.scalar.activation(out=pbuf, in_=self.ps.banks[sb][:, :], func=AF.Exp)
            te = c.mark("act", ins)
            self.ps.release(sb, te)
            c.wait("pe", te)
            nc.tensor.matmul(self.ps.banks[ob][:, :], t["v"], pbuf, start=(i == 0), stop=(i == n - 1))
            ins = nc.tensor.matmul(self.ps.banks[rb][:, :], self.ONES[:, :], pbuf, start=(i == 0), stop=(i == n - 1))
            last = c.mark("pe", ins)
            if keepP is None:
                self.pr_rd[s] = last
        return ob, rb, last

    def combine(self, ob, rb, tok, hh, g, j, br, first):
        c, nc = self.c, self.nc
        gb = self.ps.get()
        c.wait("pe", self.gs_tok)
        for hi in range(4):
            h = 8 * g + 4 * hh + hi
            ins = nc.tensor.matmul(self.ps.banks[gb][:, hi * P:(hi + 1) * P], self.SELG[0:48, 3 * h + br, :],
                                   self.GS[0:48, j * P:(j + 1) * P], start=True, stop=True)
        tg = c.mark("pe", ins)
        c.wait("dve", tok, tg, self.rinv_rd)
        ins = nc.vector.tensor_scalar(out=self.RINV[:, :], in0=self.ps.banks[rb][:, :], scalar1=1e-30, scalar2=None, op0=ALU.add)
        t0 = c.mark("dve", ins)
        c.wait("dve", t0)
        ins = nc.vector.reciprocal(out=self.RINV[:, :], in_=self.RINV[:, :])
        t1 = c.mark("dve", ins)
        self.ps.release(rb, t1)
        c.wait("dve", t1)
        ins = nc.vector.tensor_tensor(out=self.GREP[:, :], in0=self.ps.banks[gb][:, :], in1=self.RINV[:, :], op=ALU.mult)
        t2 = c.mark("dve", ins)
        self.ps.release(gb, t2)
        c.wait("dve", t2)
        if first:
            ins = nc.vector.tensor_tensor(out=self.ACC[:, hh, :], in0=self.ps.banks[ob][:, :], in1=self.GREP[:, :], op=ALU.mult)
            t3 = c.mark("dve", ins)
        else:
            ins = nc.vector.tensor_tensor(out=self.GREP[:, :], in0=self.ps.banks[ob][:, :], in1=self.GREP[:, :], op=ALU.mult)
            t3a = c.mark("dve", ins)
            c.wait("dve", t3a)
            ins = nc.vector.tensor_tensor(out=self.ACC[:, hh, :], in0=self.ACC[:, hh, :], in1=self.GREP[:, :], op=ALU.add)
            t3 = c.mark("dve", ins)
        self.ps.release(ob, t3)
        return t1, t3

    def nsa(self, l, jl, d):
        c, nc = self.c, self.nc
        self.rmsnorm(NV_MIX + l)
        t_spill = c.dma("sp", d["xsp"], self.XT[:, :, :], waits=[self.ht_tok, self.xt_tok])
        st = {"tb": [None, None], "tf": [None, None]}
        self.rSQ.acquire("act")
        self.rRS.acquire("act")
        c.wait("dve", self.b1_read)

        def epi_q(m, h, ps, tok):
            b = self.ps.banks.index(ps)
            return self.normed_evac(b, tok, self.QT[:, m, h * 512:(h + 1) * 512], self.QN[:, jl:jl + 1], st)

        rhs = lambda k, h: self.HT[:, k, h * 512:(h + 1) * 512]
        self.linear_fm(8, KC, rhs, [self.ht_tok], epi_q)
        view, rtok, slot = self.ring.next()
        c.wait("pe", rtok)
        self.rX.acquire("act")
        for h in range(2):
            b = self.ps.get()
            for k in range(KC):
                ins = nc.tensor.matmul(self.ps.banks[b][0:48, :], view[:, k, :], rhs(k, h), start=(k == 0), stop=(k == KC - 1))
            tp = c.mark("pe", ins)
            c.wait("act", tp)
            ins = nc.scalar.activation(out=self.GS[0:48, h * 512:(h + 1) * 512], in_=self.ps.banks[b][0:48, :], func=AF.Sigmoid)
            self.gs_tok = c.mark("act", ins)
            self.ps.release(b, self.gs_tok)
        self.ring.release(slot, tp)
        self.ht_read = tp
        self.rX.done("act", self.gs_tok)
        q_tok = st["tf"][0], st["tf"][1]
        ld = []
        w8 = [t_spill, self.b2_read]
        L = lambda dst, src: ld.append(c.dma("sp", dst, src, waits=w8))
        L(self.D0[:, :, :], d["D0"]); L(self.D1[:, :, :], d["D1"]); L(self.D4, d["D4"])
        L(self.FC[0:17, :, :], d["FCd"]); L(self.SHC[0:17, :, :, :], d["SHC"]); L(self.SELF[:, :, :], d["SELF"])
        L(self.SELN[:, :, :, :], d["SELN"]); L(self.SELG[0:48, :, :], d["SELG"]); L(self.FV[:, :, :], d["FV"])
        L(self.OV[:, :, :], d["OV"]); L(self.KCT[:, :, :], d["KCTd"]); L(self.VC[:, :, :, :], d["VCd"])
        L(self.IDENT, d["IDENT"])
        L(self.KB, d["KB"])
        for e in ("pe", "act", "dve"):
            c.wait(e, *ld)
        c.wait("pe", *q_tok)
        c.wait("act", self.ht_read)
        c.wait("dve", self.ht_read)
        self.pr_i = 0
        self.pr_rd = [None, None, None]
        self.rinv_rd = None
        KBW = self.KB[:, 0:40]
        KBS = self.KB[:, 40:56]
        KBF = self.KB[:, 56:56 + 512].rearrange("p (j k) -> p j k", j=8)
        loc_tok = {}
        loc_rd = [None, None]

        def load_local(it):
            g_, j_ = divmod(it, NT)
            b_ = it % 2
            w_ = w8 + [loc_rd[b_]]
            loc_tok[it] = [
                c.dma("sp", self.LKW[b_], d["kw_loc"][g_, j_], waits=w_),
                c.dma("sp", self.LVW[b_][:, :, :], d["vw_loc"][g_, j_].rearrange("(t p) e -> p t e", p=P), waits=w_),
                c.dma("sp", self.LKS[b_], d["ks_loc"][g_, j_], waits=w_),
                c.dma("sp", self.LVS[b_][:, :, :], d["vs_loc"][g_, j_].rearrange("(t p) e -> p t e", p=P), waits=w_),
            ]

        load_local(0)
        last_ot = None
        last_far = None
        self.pc_rd = None
        self.sel_rd = None
        for g in range(2):
            kA, rkA, skA = self.ring.next()
            kB, rkB, skB = self.ring.next()
            vA, rvA, svA = self.ring.next()
            vB, rvB, svB = self.ring.next()
            c.wait("pe", rkA, rkB, rvA, rvB)
            for j in range(NT):
                it = g * NT + j
                lb = it % 2
                if it + 1 < 2 * NT:
                    load_local(it + 1)
                c.wait("pe", *loc_tok[it])
                ib = self.ps.get()
                acc_t = [None, None]
                for hh in range(2):
                    h0 = 8 * g + 4 * hh
                    Q4 = self.QT[:, h0:h0 + 4, j * P:(j + 1) * P]
                    tiles = [dict(kT=self.KCT[:, g, ch * P:(ch + 1) * P], v=self.VC[:, ch, g, :],
                                  adds=[(self.SHC[0:17, j, ch, :], self.FC[0:17, h0:h0 + 4, :])]) for ch in range(4)]
                    c.wait("act", self.pc_rd)
                    ob, rb, tk = self.attn_unit(Q4, tiles, keepP=self.PC)
                    t1, t3 = self.combine(ob, rb, tk, hh, g, j, 0, True)
                    acc_t[hh] = t3
                    c.wait("dve", t1)
                    for ch in range(4):
                        ins = nc.vector.tensor_tensor(out=self.PC[:, ch, :], in0=self.PC[:, ch, :], in1=self.RINV[:, :], op=ALU.mult)
                    tn = c.mark("dve", ins)
                    self.rinv_rd = tn
                    c.wait("pe", tn)
                    for hi in range(4):
                        for ch in range(4):
                            ins = nc.tensor.matmul(self.ps.banks[ib][:, 0:P], self.PC[:, ch, hi * P:(hi + 1) * P], self.OV[:, ch, :],
                                                   start=(hh == 0 and hi == 0 and ch == 0), stop=(hh == 1 and hi == 3 and ch == 3))
                    self.pc_rd = c.mark("pe", ins)
                pc_rd = self.pc_rd
                c.wait("dve", pc_rd, self.sel_rd)
                ins = nc.vector.tensor_tensor(out=self.SCORE, in0=self.ps.banks[ib][:, 0:P], in1=self.FV[:, j, :], op=ALU.add)
                ts = c.mark("dve", ins)
                self.ps.release(ib, ts)
                c.wait("dve", ts)
                ins = nc.vector.max(out=self.M8[:, 0:8], in_=self.SCORE)
                ts = c.mark("dve", ins); c.wait("dve", ts)
                ins = nc.vector.match_replace(out=self.WORK, in_to_replace=self.M8[:, 0:8], in_values=self.SCORE, imm_value=-1e38)
                ts = c.mark("dve", ins); c.wait("dve", ts)
                ins = nc.vector.max(out=self.M8[:, 8:16], in_=self.WORK)
                ts = c.mark("dve", ins); c.wait("dve", ts)
                ins = nc.vector.match_replace(out=self.WORK, in_to_replace=self.M8[:, 8:16], in_values=self.WORK, imm_value=-1e38)
                ts = c.mark("dve", ins); c.wait("dve", ts)
                ins = nc.vector.tensor_tensor(out=self.WORK, in0=self.SCORE, in1=self.WORK, op=ALU.subtract)
                ts = c.mark("dve", ins); c.wait("dve", ts)
                ins = nc.vector.tensor_scalar(out=self.WORK, in0=self.WORK, scalar1=1.0, scalar2=None, op0=ALU.min)
                ts = c.mark("dve", ins); c.wait("dve", ts)
                ins = nc.vector.tensor_scalar(out=self.SEL, in0=self.SCORE, scalar1=-1e29, scalar2=None, op0=ALU.is_gt)
                ts = c.mark("dve", ins); c.wait("dve", ts)
                ins = nc.vector.tensor_tensor(out=self.SEL, in0=self.SEL, in1=self.WORK, op=ALU.mult)
                ts = c.mark("dve", ins); c.wait("dve", ts)
                ins = nc.vector.tensor_scalar(out=self.SELB, in0=self.SEL, scalar1=-NEG, scalar2=NEG, op0=ALU.mult, op1=ALU.add)
                ts = c.mark("dve", ins)
                for hh in range(2):
                    h0 = 8 * g + 4 * hh
                    Q4 = self.QT[:, h0:h0 + 4, j * P:(j + 1) * P]
                    tiles = []
                    for r in range(5):
                        t = dict(kT=self.LKW[lb][:, r * P:(r + 1) * P], v=self.LVW[lb][:, r, :], kb=KBW[:, 5 * j + r:5 * j + r + 1], adds=[])
                        if r == 4:
                            t["adds"].append((self.IDENT, self.D0[:, h0:h0 + 4, :]))
                        elif r == 3:
                            t["adds"].append((self.IDENT, self.D1[:, h0:h0 + 4, :]))
                        elif r == 0:
                            t["adds"].append((self.IDENT, self.D4.unsqueeze(1).to_broadcast([P, 4, P])))
                        tiles.append(t)
                    ob, rb, tk = self.attn_unit(Q4, tiles)
                    self.combine(ob, rb, tk, hh, g, j, 2, False)
                tb_ = self.ps.get()
                c.wait("pe", ts)
                ins = nc.tensor.matmul(self.ps.banks[tb_][:, 0:P], self.SELB, self.IDENT, start=True, stop=True)
                tt = c.mark("pe", ins)
                self.sel_rd = tt
                c.wait("act", tt, last_far)
                ins = nc.scalar.copy(out=self.SNT, in_=self.ps.banks[tb_][:, 0:P])
                t_snt = c.mark("act", ins)
                self.ps.release(tb_, t_snt)
                c.wait("pe", t_snt)
                for hh in range(2):
                    h0 = 8 * g + 4 * hh
                    Q4 = self.QT[:, h0:h0 + 4, j * P:(j + 1) * P]
                    SN4 = lambda rows: self.SNT[rows, :].unsqueeze(1).to_broadcast([rows.stop - rows.start, 4, P])
                    tiles = []
                    for r in range(2):
                        tiles.append(dict(kT=self.LKS[lb][:, r * P:(r + 1) * P], v=self.LVS[lb][:, r, :], kb=KBS[:, 2 * j + r:2 * j + r + 1],
                                          adds=[(self.IDENT, (self.D1 if r == 0 else self.D0)[:, h0:h0 + 4, :]),
                                                (self.SELN[:, j, r, :], SN4(slice(0, P)))]))
                    for kg in range(8 * (j + 1)):
                        kv_ = kA if kg < 32 else kB
                        vv_ = vA if kg < 32 else vB
                        a = kg // 32
                        tiles.append(dict(kT=kv_[:, 0, (kg % 32) * P:(kg % 32 + 1) * P], v=vv_[:, kg % 32, :], kb=KBF[:, j, kg:kg + 1],
                                          adds=[(self.SELF[64 * a:64 * a + 64, kg % 32, :], SN4(slice(64 * a, 64 * a + 64)))]))
                    ob, rb, tk = self.attn_unit(Q4, tiles)
                    last_far = tk
                    t1, t3 = self.combine(ob, rb, tk, hh, g, j, 1, False)
                    self.rinv_rd = t1
                    c.wait("dve", t3)
                    ins = nc.vector.tensor_copy(out=self.OT[:, h0:h0 + 4, j * P:(j + 1) * P],
                                                in_=self.ACC[:, hh, :].rearrange("p (h t) -> p h t", h=4))
                    last_ot = c.mark("dve", ins)
                loc_rd[lb] = last_far
            for s_ in (skA, skB, svA, svB):
                self.ring.release(s_, last_far)
        t_x = c.dma("sp", self.XT[:, :, :], d["xsp"], waits=[last_far, last_ot, t_spill])
        self.xt_tok = t_x
        c.wait("dve", t_x)
        c.wait("act", t_x)
        self.ht_tok = last_ot
        self.b2_read = last_far
        self.b1_read = last_far
        self.rX.done("pe", last_far)
        last = self.linear_fm(8, KC, lambda k, h: self.OT[:, k, h * 512:(h + 1) * 512], [last_ot], self.add_to_xt)
        self.ht_read = last


def build_B(n_layers=2, dbg=False):
    nc = bass.Bass("TRN2", target_bir_lowering=False)
    dt = lambda name, shape, kind="ExternalInput", dtype=F32: nc.dram_tensor(name, list(shape), dtype, kind=kind).ap()
    d = {}
    xT_d = dt("xT", [D, T])
    pT_d = dt("pT", [2, PLE, T])
    nrm_d = dt("nrm", [P, NV, KC])
    knorm_d = dt("knormT", [P, 3])
    qn_d = dt("qnT", [P, 2])
    b_w_in = dt("b_w_in", [2, D, D + 48])
    b_w_out = dt("b_w_out", [2, D, D])
    if not dbg:
        ffn_w_in = dt("ffn_w_in", [2, D, 2 * FF])
        ffn_w_out = dt("ffn_w_out", [2, FF, D])
        ple_w = dt("ple_w", [2, PLE, D])
        ple_gate = dt("ple_gate", [2, D, D])
    d["cmp_w1"] = dt("cmp_w1", [2, 4096, 256])
    d["cmp_w2"] = dt("cmp_w2", [2, 256, P])
    d["peT"] = dt("peT", [P, 2, 32])
    d["relb"] = dt("relb", [1, 512])
    d["kcvT"] = dt("kcvT", [2, P, 2, SEQ], dtype=BF16)
    d["ksT_full"] = dt("ksT_full", [P, 2, SEQ], dtype=BF16)
    d["vs_full"] = dt("vs_full", [SEQ, 256], dtype=BF16)
    d["kw_loc"] = dt("kw_loc", [2, NT, P, 5 * P], dtype=BF16)
    d["vw_loc"] = dt("vw_loc", [2, NT, 5 * P, P], dtype=BF16)
    d["ks_loc"] = dt("ks_loc", [2, NT, P, 2 * P], dtype=BF16)
    d["vs_loc"] = dt("vs_loc", [2, NT, 2 * P, P], dtype=BF16)
    for name, shape in (("M0", [P, NB, P]), ("M1", [P, NB, P]), ("MC", [17, NB, P]), ("D4", [P, P]), ("SHC", [17, 8, 4, P]),
                        ("SELF", [P, 32, P]), ("SELN", [P, 8, 2, P]), ("SELG", [48, 48, P]), ("FV", [P, 8, P]), ("OV", [P, 4, P]),
                        ("IDENT", [P, P])):
        d[name] = dt(name, shape, dtype=BF16)
    d["KB"] = dt("KB", [P, 576])
    xo_d = dt("xT_out", [D, T], kind="ExternalOutput")
    it = lambda name, shape, dtype=BF16: nc.dram_tensor(name, list(shape), dtype, kind="Internal").ap()
    d["D0"] = it("D0s", [P, 16, P]); d["D1"] = it("D1s", [P, 16, P]); d["FCd"] = it("FCs", [17, 16, P])
    d["KCTd"] = it("KCTs", [P, 2, 512]); d["VCd"] = it("VCs", [P, 4, 2, P])
    d["xsp"] = it("xsp", [P, KC, T], F32)

    k = CoreB(nc)
    c = k.c
    k.plan_setup_B(d)
    for jl in range(n_layers):
        k.plan_nsa(b_w_in[jl], b_w_out[jl], d)
        if dbg:
            break
        k.plan_ffn(ffn_w_in[jl], ffn_w_out[jl])
        k.plan_ple(ple_w[jl], ple_gate[jl])
    k.setup_common(nrm_d, None)
    k.KN = nc.alloc_sbuf_tensor("KN", [P, 3], F32)
    k.QN = nc.alloc_sbuf_tensor("QN", [P, 2], F32)
    t_kn = c.dma("sp", k.KN[:, :], knorm_d)
    t_qn = c.dma("sp", k.QN[:, :], qn_d)
    c.wait("dve", t_kn, t_qn)
    ins = nc.vector.tensor_scalar(out=k.QN[:, :], in0=k.QN[:, :], scalar1=HSCALE, scalar2=None, op0=ALU.mult)
    c.mark("dve", ins)
    k.carve()
    k.setup_B(d)
    k.xt_tok = c.dma("sp", k.XT[:, :, :], xT_d.rearrange("(kc p) t -> p kc t", p=P), waits=k.setup_toks)
    c.wait("dve", k.xt_tok)
    k.b2_read = None
    for jl in range(n_layers):
        l = 2 + jl
        k.nsa(l, jl, d)
        if dbg:
            break
        k.ple_load(pT_d[jl])
        k.ffn(l)
        k.ple(l)
    outs = [c.dma("sp", xo_d.rearrange("(kc p) t -> p kc t", p=P), k.XT[:, :, :], waits=[k.xt_tok])]
    c.wait("sp", *outs)
    return nc


def t5_bucket_np(n):
    n = np.maximum(n, 0)
    nf = np.maximum(n, 1).astype(np.float32)
    large = 16 + (np.log(nf / np.float32(16)) / np.float32(np.log(128 / 16)) * np.float32(16)).astype(np.int32)
    large = np.minimum(large, 31)
    return np.where(n < 16, n, large)


def onehot_table(delta):
    b = np.where(delta < 0, 32, t5_bucket_np(delta))
    return (b[..., None] == np.arange(NB)).astype(np.float32)


def qmap(core, j):
    return 16 * (j // 2) + (core if j % 2 == 0 else 15 - core)


def host_tables_B(core):
    bf = lambda a: np.ascontiguousarray(a).astype(ml_dtypes.bfloat16)
    sig = np.arange(P)[:, None]
    tau = np.arange(P)[None, :]
    t = {}
    t["M0"] = bf(onehot_table(tau - sig).transpose(0, 2, 1))
    t["M1"] = bf(onehot_table(128 + tau - sig).transpose(0, 2, 1))
    io = np.arange(16)[:, None] - 9
    mc = onehot_table(tau - 16 * io - 31)
    fut = np.zeros((1, P, NB), np.float32); fut[:, :, 32] = 1.0
    t["MC"] = bf(np.concatenate([mc, fut], 0).transpose(0, 2, 1))
    t["D4"] = bf(np.where(tau >= sig, NEG, 0.0))
    shc = np.zeros((17, 8, 4, P), np.float32)
    seln = np.zeros((P, 8, 2, P), np.float32)
    fv = np.zeros((P, 8, P), np.float32)
    kbf = np.zeros((P, 8, 64), np.float32)
    blk = np.arange(P)
    for j in range(8):
        qg = qmap(core, j)
        idx = np.arange(512).reshape(4, P)
        for i_ in range(16):
            shc[i_, j] = (idx == 8 * qg + i_ - 9)
        shc[16, j] = (idx > 8 * qg + 6) | (idx >= NCMP)
        for r in range(2):
            kt = qg - 1 + r
            if kt >= 0:
                seln[:, j, r, :] = (blk[:, None] == 2 * kt + (np.arange(P)[None, :] // 64))
        tt = 128 * qg + np.arange(P)
        cur = tt // 64
        f = np.where(blk[None, :] <= cur[:, None], 0.0, -1e30).astype(np.float32)
        f[:, 0] = 1e30
        for q in range(P):
            if cur[q] >= 1:
                f[q, cur[q] - 1] = 3e30
            f[q, cur[q]] = 2e30
        fv[:, j, :] = f
        kg = np.arange(64)
        kbf[:, j, :] = np.where((kg == qg) | (kg == qg - 1) | (kg > qg), NEG, 0.0)[None, :]
    t["SHC"] = bf(shc); t["SELN"] = bf(seln); t["FV"] = bf(fv)
    selfar = np.zeros((P, 32, P), np.float32)
    for k_ in range(32):
        selfar[:, k_, :] = ((np.arange(P)[:, None] % 64) == 2 * k_ + (np.arange(P)[None, :] // 64))
    t["SELF"] = bf(selfar)
    t["SELG"] = bf(np.broadcast_to(np.eye(48, dtype=np.float32)[:, :, None], (48, 48, P)))
    c0 = np.arange(512) * 16
    s0 = np.arange(P) * 64
    lo = np.maximum(c0[:, None], s0[None, :]); hi = np.minimum(c0[:, None] + 32, s0[None, :] + 64)
    ov = np.maximum(hi - lo, 0).astype(np.float32) / 32.0
    ov[NCMP:] = 0
    t["OV"] = bf(ov.reshape(4, P, P).transpose(1, 0, 2))
    t["IDENT"] = bf(np.eye(P, dtype=np.float32))
    kb = np.zeros((P, 576), np.float32)
    for j in range(8):
        for r in range(5):
            kb[:, 5 * j + r] = NEG if (qmap(core, j) - 4 + r) < 0 else 0.0
        for r in range(2):
            kb[:, 40 + 2 * j + r] = NEG if (qmap(core, j) - 1 + r) < 0 else 0.0
    kb[:, 56:56 + 512] = kbf.reshape(P, 512)
    t["KB"] = kb
    return t


def host_inputs_B(inputs, core, x1, kvT, kvV):
    f = lambda a: np.ascontiguousarray(a, dtype=np.float32)
    rows = np.concatenate([np.arange(qmap(core, j) * P, (qmap(core, j) + 1) * P) for j in range(NT)])
    p = inputs["p"][2:4, 0][:, rows, :]
    vecs = np.zeros((NV, D), np.float32)
    vecs[NV_MIX:NV_MIX + 4] = inputs["norm_mix"]
    vecs[NV_FFN:NV_FFN + 4] = inputs["norm_ffn"]
    vecs[NV_PLE:NV_PLE + 4] = inputs["norm_ple"]
    nrm = vecs.reshape(NV, KC, P).transpose(2, 0, 1)
    fullT = lambda i: np.ascontiguousarray(np.concatenate([kvT[c_, i] for c_ in range(NCORES)], axis=-1))
    fullV = lambda i: np.ascontiguousarray(np.concatenate([kvV[c_, i] for c_ in range(NCORES)], axis=0))
    ksT, kwT = fullT(2), fullT(3)
    vs, vw = fullV(0), fullV(1)

    def locT(full, halo):
        out = np.zeros((2, NT, P, (halo + 1) * P), full.dtype)
        for j in range(NT):
            q = qmap(core, j)
            lo = (q - halo) * P
            s = max(lo, 0)
            out[:, j, :, s - lo:] = full[:, :, s:(q + 1) * P].transpose(1, 0, 2)
        return out

    def locV(full, halo):
        out = np.zeros((2, NT, (halo + 1) * P, P), full.dtype)
        for j in range(NT):
            q = qmap(core, j)
            lo = (q - halo) * P
            s = max(lo, 0)
            out[:, j, s - lo:, :] = full[s:(q + 1) * P].reshape(-1, 2, P).transpose(1, 0, 2)
        return out

    m = {
        "xT": f(x1[rows].T), "pT": f(p.transpose(0, 2, 1)), "nrm": f(nrm), "knormT": f(inputs["k_norm"].T), "qnT": f(inputs["b_q_norm"].T),
        "b_w_in": f(inputs["b_w_in"]), "b_w_out": f(inputs["b_w_out"]),
        "ffn_w_in": f(inputs["ffn_w_in"][2:4]), "ffn_w_out": f(inputs["ffn_w_out"][2:4]),
        "ple_w": f(inputs["ple_w"][2:4]), "ple_gate": f(inputs["ple_gate"][2:4]),
        "cmp_w1": f(np.stack([inputs["cmp_wk1"], inputs["cmp_wv1"]])), "cmp_w2": f(np.stack([inputs["cmp_wk2"], inputs["cmp_wv2"]])),
        "peT": f(np.stack([inputs["cmp_pe_k"].T, inputs["cmp_pe_v"].T], axis=1)),
        "relb": f(inputs["rel_bias"].reshape(1, 512)),
        "kcvT": np.ascontiguousarray(np.stack([fullT(0), fullT(1)])), "ksT_full": ksT, "vs_full": vs,
        "kw_loc": locT(kwT, 4), "vw_loc": locV(vw, 4), "ks_loc": locT(ksT, 1), "vs_loc": locV(vs, 1),
    }
    m.update(host_tables_B(core))
    return m


_NC_CACHE = {}


def kernel(**inputs):
    inputs = {k_: np.asarray(v) for k_, v in inputs.items()}
    if "A" not in _NC_CACHE:
        _NC_CACHE["A"] = build_A()
    resA = run_bass_kernel_spmd(_NC_CACHE["A"], [host_inputs_A(inputs, c_) for c_ in range(NCORES)], core_ids=list(range(NCORES)))
    kvT = np.stack([r["kvT"] for r in resA.results])
    kvV = np.stack([r["kvV"] for r in resA.results])
    x1 = np.concatenate([r["xT_out"].T for r in resA.results], axis=0)
    if "B" not in _NC_CACHE:
        _NC_CACHE["B"] = build_B()
    in_B = [host_inputs_B(inputs, c_, x1, kvT, kvV) for c_ in range(NCORES)]
    resB = run_bass_kernel_spmd(_NC_CACHE["B"], in_B, core_ids=list(range(NCORES)))
    out = np.zeros((SEQ, D), np.float32)
    for c_ in range(NCORES):
        y = resB.results[c_]["xT_out"].T
        for j in range(NT):
            q = qmap(c_, j)
            out[q * P:(q + 1) * P] = y[j * P:(j + 1) * P]
    return np.ascontiguousarray(out[None])
```

```python
import numpy as np
import ml_dtypes
import concourse.bass as bass
import concourse.mybir as mybir
from concourse.bass_utils import run_bass_kernel_spmd

F32 = mybir.dt.float32
BF16 = mybir.dt.bfloat16
AF = mybir.ActivationFunctionType
ALU = mybir.AluOpType

NCORES = 8
P = 128
SEQ = 8192
T = SEQ // NCORES
NT = T // P
D = 2048
KC = D // P
FF = 5632
HB = 11
PLE = 256
EPS = 1e-6
SLAB = 4096
NSLOT = 4

NV_MIX, NV_FFN, NV_PLE, NV_AV, NV_KV = 0, 4, 8, 12, 14
NV = 15


class Ctx:
    def __init__(self, nc):
        self.nc = nc
        self.E = {"pe": nc.tensor, "act": nc.scalar, "dve": nc.vector, "pool": nc.gpsimd, "sp": nc.sync}
        self.prog = {}
        self.seen = {}
        self.nsem = 0
        for e in ("pe", "act", "dve", "pool"):
            self.prog[e] = [self.sem("pg_" + e), 0]
        self.misc = [[self.sem("misc"), 0] for _ in range(12)]
        self.misc_i = 0

    def sem(self, name):
        self.nsem += 1
        return self.nc.alloc_semaphore(f"{name}_{self.nsem}")

    def mark(self, e, ins):
        p = self.prog[e]
        if p[1] >= 30000:
            p = self.prog[e] = [self.sem("pg_" + e), 0]
        p[1] += 1
        ins.then_inc(p[0], 1)
        return (p[0], p[1])

    def wait(self, e, *toks):
        for tok in toks:
            if tok is None:
                continue
            if isinstance(tok, list):
                self.wait(e, *tok)
                continue
            sem, val = tok
            k = (e, sem)
            if self.seen.get(k, 0) >= val:
                continue
            self.seen[k] = val
            self.E[e].wait_ge(sem, val)

    def dma(self, e, out, in_, waits=()):
        m = self.misc[self.misc_i % len(self.misc)]
        self.misc_i += 1
        if m[1] > 0:
            self.wait(e, (m[0], m[1]))
        self.wait(e, *waits)
        self.E[e].dma_start(out=out, in_=in_).then_inc(m[0], 16)
        m[1] += 16
        return (m[0], m[1])


class Ring:
    def __init__(self, c, nslot=NSLOT):
        self.c = c
        nc = c.nc
        self.n = nslot
        self.slots = [nc.alloc_sbuf_tensor(f"wslab{i}", [P, SLAB], BF16) for i in range(nslot)]
        self.sems = [c.sem("wr") for _ in range(nslot)]
        self.cnt = [0] * nslot
        self.free = [True] * nslot
        self.free_tok = [None] * nslot
        self.plan = []
        self.ready = {}
        self.issued = 0
        self.consumed = 0

    def add(self, src, kc, cols):
        self.plan.append((src, kc, cols))

    def view(self, s, kc, cols):
        return self.slots[s][:, 0:kc * cols].rearrange("p (k n) -> p k n", k=kc)

    def pump(self):
        while self.issued < len(self.plan):
            i = self.issued
            fs = [s for s in range(self.n) if self.free[s]]
            if not fs:
                break
            s = fs[0]
            src, kc, cols = self.plan[i]
            self.c.wait("pool", self.free_tok[s])
            self.c.nc.gpsimd.dma_start(out=self.view(s, kc, cols), in_=src).then_inc(self.sems[s], 16)
            self.cnt[s] += 1
            self.ready[i] = ((self.sems[s], 16 * self.cnt[s]), s)
            self.free[s] = False
            self.issued += 1

    def next(self):
        i = self.consumed
        self.consumed += 1
        self.pump()
        assert self.issued > i, "weight ring deadlock"
        tok, s = self.ready[i]
        _, kc, cols = self.plan[i]
        return self.view(s, kc, cols), tok, s

    def release(self, s, tok):
        self.free[s] = True
        self.free_tok[s] = tok
        self.pump()


class Psum:
    def __init__(self, c):
        self.c = c
        self.banks = [c.nc.alloc_psum_tensor(f"psb{i}", [P, 512], F32) for i in range(8)]
        self.free_tok = [None] * 8
        self.held = [False] * 8
        self.i = 0

    def get(self):
        for _ in range(8):
            b = self.i % 8
            self.i += 1
            if not self.held[b]:
                break
        else:
            raise RuntimeError("all PSUM banks held")
        self.held[b] = True
        self.c.wait("pe", self.free_tok[b])
        self.free_tok[b] = None
        return b

    def release(self, b, tok):
        self.held[b] = False
        self.free_tok[b] = tok


class Reg:
    def __init__(self, c):
        self.c = c
        self.toks = {}

    def acquire(self, e):
        self.c.wait(e, *[t for e2, t in self.toks.items() if e2 != e])

    def all(self):
        return list(self.toks.values())

    def done(self, e, tok):
        self.toks[e] = tok


def wslab(w2d, r0, nrows, c0, ncols):
    return w2d[r0:r0 + nrows, c0:c0 + ncols].rearrange("(kc p) n -> p kc n", p=P)


class Core:
    def __init__(self, nc):
        self.nc = nc
        self.c = Ctx(nc)
        self.ring = Ring(self.c)
        self.ps = Psum(self.c)
        a = nc.alloc_sbuf_tensor
        self.XT = a("XT", [P, KC, T], F32)
        self.HT = a("HT", [P, KC, T], BF16)
        self.B1 = a("B1", [P, KC * T], BF16)
        self.B2 = a("B2", [P, KC * T], BF16)
        self.SCR = a("SCR", [P, 6 * T], BF16)
        self.SQ = self.SCR[:, 0:2 * T].rearrange("p (a t) -> p a t", a=2)
        self.TMPB = self.SQ
        self.JUNK = self.SCR[:, 0:4 * T]
        self.BIAS = self.SCR[:, 0:4 * T].bitcast(F32)
        self.RSTD = self.SCR[:, 2 * T:4 * T].bitcast(F32)
        self.TMPF = self.RSTD.rearrange("p (a t) -> p a t", a=2)
        self.WST = self.SCR[:, 4 * T:6 * T].rearrange("p (g t) -> p g t", g=KC)
        self.PT = self.SCR[:, 4 * T:6 * T].rearrange("p (a t) -> p a t", a=2)
        self.rSQ, self.rRS, self.rX = Reg(self.c), Reg(self.c), Reg(self.c)
        self.NRM = a("NRM", [P, NV, KC], F32)
        self.ONES = a("ONES", [P, P], BF16)
        self.SMALL = a("SMALL", [P, 64], F32)
        self.EPSB = a("EPSB", [P, 1], F32)
        self.xt_tok = None
        self.ht_read = None
        self.ht_tok = None
        self.b1_read = None
        self.b2_read = None
        self.small_rd = None
        self.tmp_i = 0

    def rmsnorm(self, nv_idx):
        c, nc = self.c, self.nc
        pb = [self.ps.get(), self.ps.get()]
        self.rSQ.acquire("act")
        sq_rd = [None, None]
        last = None
        for k in range(KC):
            j = k % 2
            c.wait("act", self.xt_tok, sq_rd[j])
            ins = nc.scalar.activation(out=self.SQ[:, j, :], in_=self.XT[:, k, :], func=AF.Square)
            t = c.mark("act", ins)
            c.wait("pe", t)
            for h in range(2):
                ins = nc.tensor.matmul(self.ps.banks[pb[h]][:, :], self.ONES[:, :], self.SQ[:, j, h * 512:(h + 1) * 512],
                                       start=(k == 0), stop=(k == KC - 1))
            sq_rd[j] = last = c.mark("pe", ins)
        self.rSQ.done("act", t)
        self.rSQ.done("pe", last)
        c.wait("act", last)
        self.rRS.acquire("act")
        rel = None
        for h in range(2):
            sl = slice(h * 512, (h + 1) * 512)
            ins = nc.scalar.activation(out=self.RSTD[:, sl], in_=self.ps.banks[pb[h]][:, :], func=AF.Sqrt,
                                       bias=self.EPSB[:, 0:1], scale=1.0 / D)
            t = c.mark("act", ins)
            self.ps.release(pb[h], t)
            c.wait("dve", t)
            ins = nc.vector.reciprocal(out=self.RSTD[:, sl], in_=self.RSTD[:, sl])
            rel = c.mark("dve", ins)
        self.rRS.done("act", t)
        c.wait("dve", rel, self.ht_read, self.xt_tok)
        for k in range(KC):
            ins = nc.vector.scalar_tensor_tensor(out=self.HT[:, k, :], in0=self.XT[:, k, :],
                                                 scalar=self.NRM[:, nv_idx, k:k + 1], in1=self.RSTD[:, :],
                                                 op0=ALU.mult, op1=ALU.mult)
        self.ht_tok = c.mark("dve", ins)
        self.rRS.done("dve", self.ht_tok)
        return self.ht_tok

    def linear_fm(self, nslab, kc, rhs_fn, rhs_toks, epilogue):
        c, nc = self.c, self.nc
        last = None
        for s in range(nslab):
            view, rtok, slot = self.ring.next()
            c.wait("pe", rtok, *rhs_toks)
            ncol = view.shape[2] // P
            for mi in range(ncol):
                for h in range(2):
                    b = self.ps.get()
                    for k in range(kc):
                        ins = nc.tensor.matmul(self.ps.banks[b][:, :], view[:, k, mi * P:(mi + 1) * P], rhs_fn(k, h),
                                               start=(k == 0), stop=(k == kc - 1))
                    last = c.mark("pe", ins)
                    rel = epilogue(s * ncol + mi, h, self.ps.banks[b], last)
                    self.ps.release(b, rel)
            self.ring.release(slot, last)
        return last

    def add_to_xt(self, m, h, ps, tok):
        c, nc = self.c, self.nc
        sl = slice(h * 512, (h + 1) * 512)
        c.wait("dve", tok)
        ins = nc.vector.tensor_tensor(out=self.XT[:, m, sl], in0=ps[:, :], in1=self.XT[:, m, sl], op=ALU.add)
        self.xt_tok = c.mark("dve", ins)
        return self.xt_tok

    def plan_gmlp(self, w_in, w_out):
        for s in range(8):
            self.ring.add(wslab(w_in, 0, D, D + s * 256, 256), KC, 256)
        for s in range(8):
            self.ring.add(wslab(w_in, 0, D, s * 256, 256), KC, 256)
        for s in range(8):
            self.ring.add(wslab(w_out, 0, D, s * 256, 256), KC, 256)

    def gmlp(self, l, wsT_d, bs_d):
        c, nc = self.c, self.nc
        GV = self.B1[:, :].rearrange("p (t f) -> p t f", t=NT)
        SV = self.B2[:, :].rearrange("p (g t) -> p g t", g=KC)
        t_ws = c.dma("pool", self.WST[:, :, :], wsT_d[l].rearrange("g s t -> s g t"), waits=self.rX.all())
        c.wait("dve", t_ws, self.tri_tok)
        for g in range(KC):
            ins = nc.vector.tensor_tensor(out=self.WST[:, g, :], in0=self.WST[:, g, :], in1=self.TRI[:, :], op=ALU.mult)
        t_wst = c.mark("dve", ins)
        self.rX.done("dma", t_ws)
        self.rX.done("dve", t_wst)

        self.rmsnorm(NV_MIX + l)
        c.wait("act", self.b1_read)
        last = None
        for s in range(8):
            view, rtok, slot = self.ring.next()
            c.wait("pe", rtok, self.ht_tok)
            for t in range(NT):
                b = self.ps.get()
                for k in range(KC):
                    ins = nc.tensor.matmul(self.ps.banks[b][:, 0:256], self.HT[:, k, t * P:(t + 1) * P], view[:, k, :],
                                           start=(k == 0), stop=(k == KC - 1))
                last = c.mark("pe", ins)
                c.wait("act", last)
                ins = nc.scalar.activation(out=GV[:, t, s * 256:(s + 1) * 256], in_=self.ps.banks[b][:, 0:256],
                                           func=AF.Gelu_apprx_tanh)
                ta = c.mark("act", ins)
                self.ps.release(b, ta)
            self.ring.release(slot, last)
        SS = self.SMALL
        c.wait("dve", self.small_rd)
        ins = nc.vector.memset(SS[:, 0:16], 0.0)
        t0 = c.mark("dve", ins)
        c.wait("act", t0, ta)
        self.rSQ.acquire("act")
        self.rRS.acquire("act")
        for t in range(NT):
            ins = nc.scalar.activation(out=self.JUNK[:, 0:D], in_=GV[:, t, :], func=AF.Square, accum_out=SS[:, t:t + 1])
        t1 = c.mark("act", ins)
        self.rSQ.done("act", t1)
        self.rRS.done("act", t1)
        c.wait("act", t1)
        ins = nc.scalar.activation(out=SS[:, 8:16], in_=SS[:, 0:8], func=AF.Sqrt, bias=self.EPSB[:, 0:1], scale=1.0 / D)
        t1b = c.mark("act", ins)
        c.wait("dve", t1b)
        ins = nc.vector.reciprocal(out=SS[:, 8:16], in_=SS[:, 8:16])
        t2 = c.mark("dve", ins)
        c.wait("dve", t2)
        for t in range(NT):
            ins = nc.vector.tensor_scalar(out=GV[:, t, :], in0=GV[:, t, :], scalar1=SS[:, 8 + t:9 + t], scalar2=None,
                                          op0=ALU.mult)
        t_vn = c.mark("dve", ins)
        self.small_rd = t_vn
        t_bs = c.dma("sp", self.BIAS[:, :], bs_d[l:l + 1, :].to_broadcast([P, KC * P]), waits=self.rSQ.all() + self.rRS.all())
        self.rSQ.done("dma", t_bs)
        self.rRS.done("dma", t_bs)
        c.wait("pe", t_vn, t_wst)
        c.wait("dve", self.b2_read, t_bs)
        for t in range(NT):
            bb = [self.ps.get() for _ in range(4)]
            for g in range(KC):
                ins = nc.tensor.matmul(self.ps.banks[bb[g // 4]][:, (g % 4) * P:(g % 4 + 1) * P],
                                       GV[:, t, g * P:(g + 1) * P], self.WST[:, g, :], start=True, stop=True)
            tp = c.mark("pe", ins)
            c.wait("dve", tp)
            for g in range(KC):
                ins = nc.vector.scalar_tensor_tensor(out=SV[:, g, t * P:(t + 1) * P],
                                                     in0=self.ps.banks[bb[g // 4]][:, (g % 4) * P:(g % 4 + 1) * P],
                                                     scalar=self.NRM[:, NV_AV + l, g:g + 1],
                                                     in1=self.BIAS[:, g * P:(g + 1) * P], op0=ALU.mult, op1=ALU.add)
                if g % 4 == 3:
                    td = c.mark("dve", ins)
                    self.ps.release(bb[g // 4], td)
        self.b1_read = tp
        self.rX.done("pe", tp)
        self.rSQ.done("dve", td)
        self.rRS.done("dve", td)
        t_sv = td

        self.rSQ.acquire("act")
        tb_rd = [None, None]

        def epi_u(m, h, ps, tok):
            j = self.tmp_i % 2
            self.tmp_i += 1
            sl = slice(h * 512, (h + 1) * 512)
            c.wait("act", tok, tb_rd[j])
            ins = nc.scalar.activation(out=self.TMPB[:, j, 0:512], in_=ps[:, :], func=AF.Gelu_apprx_tanh)
            ta = c.mark("act", ins)
            self.rSQ.done("act", ta)
            c.wait("dve", ta, t_sv)
            ins = nc.vector.tensor_tensor(out=SV[:, m, sl], in0=self.TMPB[:, j, 0:512], in1=SV[:, m, sl], op=ALU.mult)
            tb_rd[j] = self.gt_tok = c.mark("dve", ins)
            self.rSQ.done("dve", self.gt_tok)
            return ta

        last = self.linear_fm(8, KC, lambda k, h: self.HT[:, k, h * 512:(h + 1) * 512], [self.ht_tok], epi_u)
        self.ht_read = last
        last = self.linear_fm(8, KC, lambda k, h: SV[:, k, h * 512:(h + 1) * 512], [self.gt_tok], self.add_to_xt)
        self.b2_read = last

    def plan_ffn(self, w_in, w_out):
        for hb in range(HB):
            for j in range(2):
                self.ring.add(wslab(w_in, 0, D, hb * 512 + j * 256, 256), KC, 256)
                self.ring.add(wslab(w_in, 0, D, FF + hb * 512 + j * 256, 256), KC, 256)
            for j in range(2):
                self.ring.add(wslab(w_out, hb * 512 + j * 256, 256, 0, D), 2, D)

    def ffn(self, l):
        c, nc = self.c, self.nc
        self.rmsnorm(NV_FFN + l)
        AB = self.B1[:, 0:3 * 4 * T].rearrange("p (r k t) -> p r k t", r=3, k=4)
        ab_rd = [self.b1_read, self.b1_read, self.b1_read]
        sg_rd = [None, None]
        SG = self.TMPB
        self.rSQ.acquire("act")
        rhs = lambda k, h: self.HT[:, k, h * 512:(h + 1) * 512]
        last_pe = None
        tp = None
        for hb in range(HB):
            r = hb % 3
            for j in range(2):
                def epi_g(m, h, ps, tok):
                    c.wait("act", tok, sg_rd[m] if h == 0 else None)
                    ins = nc.scalar.activation(out=SG[:, m, h * 512:(h + 1) * 512], in_=ps[:, :], func=AF.Silu)
                    self.sg_tok = c.mark("act", ins)
                    return self.sg_tok

                def epi_u(m, h, ps, tok, j=j, r=r):
                    c.wait("dve", tok, self.sg_tok, ab_rd[r])
                    ins = nc.vector.tensor_tensor(out=AB[:, r, 2 * j + m, h * 512:(h + 1) * 512], in0=ps[:, :],
                                                  in1=SG[:, m, h * 512:(h + 1) * 512], op=ALU.mult)
                    t = c.mark("dve", ins)
                    sg_rd[m] = t
                    self.ab_tok = t
                    return t

                self.linear_fm(1, KC, rhs, [self.ht_tok], epi_g)
                last_pe = self.linear_fm(1, KC, rhs, [self.ht_tok], epi_u)
            vA, rA, sA = self.ring.next()
            vB, rB, sB = self.ring.next()
            c.wait("pe", rA, rB, self.ab_tok)
            for m in range(KC):
                for h in range(2):
                    b = self.ps.get()
                    for kk in range(4):
                        v = vA if kk < 2 else vB
                        ins = nc.tensor.matmul(self.ps.banks[b][:, :], v[:, kk % 2, m * P:(m + 1) * P],
                                               AB[:, r, kk, h * 512:(h + 1) * 512], start=(kk == 0), stop=(kk == 3))
                    tp = c.mark("pe", ins)
                    rel = self.add_to_xt(m, h, self.ps.banks[b], tp)
                    self.ps.release(b, rel)
            self.ring.release(sA, tp)
            self.ring.release(sB, tp)
            ab_rd[r] = tp
        self.ht_read = last_pe
        self.b1_read = tp
        self.rSQ.done("act", self.sg_tok)
        self.rSQ.done("dve", self.ab_tok)

    def plan_ple(self, w_proj, w_gate):
        self.ring.add(wslab(w_proj, 0, PLE, 0, D), 2, D)
        for s in range(8):
            self.ring.add(wslab(w_gate, 0, D, s * 256, 256), KC, 256)

    def ple_load(self, pT_l):
        c = self.c
        self.pt_tok = c.dma("pool", self.PT[:, :, :], pT_l.rearrange("(kc p) t -> p kc t", p=P), waits=self.rX.all())
        self.rX.done("dma", self.pt_tok)

    def ple(self, l):
        c, nc = self.c, self.nc
        self.rmsnorm(NV_PLE + l)
        vW, rW, sW = self.ring.next()
        c.wait("pe", rW, self.pt_tok)
        state = {}
        self.rRS.acquire("act")
        tf_rd = [None, None]

        def epi(m, h, ps, tok):
            j = self.tmp_i % 2
            self.tmp_i += 1
            sl = slice(h * 512, (h + 1) * 512)
            c.wait("act", tok, tf_rd[j])
            ins = nc.scalar.activation(out=self.TMPF[:, j, :], in_=ps[:, :], func=AF.Sigmoid)
            ta = c.mark("act", ins)
            self.rRS.done("act", ta)
            b = self.ps.get()
            for k in range(2):
                ins = nc.tensor.matmul(self.ps.banks[b][:, :], vW[:, k, m * P:(m + 1) * P], self.PT[:, k, sl],
                                       start=(k == 0), stop=(k == 1))
            tp = c.mark("pe", ins)
            state["tp"] = tp
            c.wait("dve", ta, tp)
            ins = nc.vector.tensor_tensor(out=self.TMPF[:, j, :], in0=self.ps.banks[b][:, :], in1=self.TMPF[:, j, :], op=ALU.mult)
            t1 = c.mark("dve", ins)
            self.ps.release(b, t1)
            c.wait("dve", t1)
            ins = nc.vector.tensor_tensor(out=self.XT[:, m, sl], in0=self.TMPF[:, j, :], in1=self.XT[:, m, sl], op=ALU.add)
            self.xt_tok = tf_rd[j] = c.mark("dve", ins)
            self.rRS.done("dve", self.xt_tok)
            return ta

        last = self.linear_fm(8, KC, lambda k, h: self.HT[:, k, h * 512:(h + 1) * 512], [self.ht_tok], epi)
        self.ring.release(sW, state["tp"])
        self.ht_read = last
        self.rX.done("pe", state["tp"])

    def plan_kv(self, kv_w):
        for s in range(6):
            self.ring.add(wslab(kv_w, 0, D, s * 256, 256), KC, 256)

    def kv_proj(self, knorm_tile, kvT_d, kvV_d):
        c, nc = self.c, self.nc
        self.rmsnorm(NV_KV)
        KVT = self.B2[:, 0:4 * 2 * T].rearrange("p (i g t) -> p i g t", i=4, g=2)
        KVV = self.B1[:, 0:2 * NT * 256].rearrange("p (i t f) -> p i t f", i=2, t=NT)
        c.wait("act", self.b1_read, self.b2_read)
        c.wait("dve", self.b1_read, self.b2_read)
        self.rSQ.acquire("act")
        self.rRS.acquire("act")
        tb_rd = [None, None]
        tf_rd = [None, None]
        fm_idx = {0: 0, 1: 1, 2: 2, 4: 3}
        tm_idx = {3: 0, 5: 1}
        outs = []
        last = None
        for kind in range(6):
            view, rtok, slot = self.ring.next()
            c.wait("pe", rtok, self.ht_tok)
            if kind in tm_idx:
                for t in range(NT):
                    b = self.ps.get()
                    for k in range(KC):
                        ins = nc.tensor.matmul(self.ps.banks[b][:, 0:256], self.HT[:, k, t * P:(t + 1) * P], view[:, k, :],
                                               start=(k == 0), stop=(k == KC - 1))
                    last = c.mark("pe", ins)
                    c.wait("act", last)
                    ins = nc.scalar.copy(out=KVV[:, tm_idx[kind], t, :], in_=self.ps.banks[b][:, 0:256])
                    ta = c.mark("act", ins)
                    self.ps.release(b, ta)
                outs.append(c.dma("sp", kvV_d[tm_idx[kind]].rearrange("(t p) f -> p t f", p=P), KVV[:, tm_idx[kind], :, :], waits=[ta]))
            else:
                i = fm_idx[kind]
                for g in range(2):
                    for h in range(2):
                        sl = slice(h * 512, (h + 1) * 512)
                        b = self.ps.get()
                        for k in range(KC):
                            ins = nc.tensor.matmul(self.ps.banks[b][:, :], view[:, k, g * P:(g + 1) * P], self.HT[:, k, sl],
                                                   start=(k == 0), stop=(k == KC - 1))
                        last = c.mark("pe", ins)
                        if kind in (0, 1):
                            c.wait("act", last)
                            ins = nc.scalar.copy(out=KVT[:, i, g, sl], in_=self.ps.banks[b][:, :])
                            ta = c.mark("act", ins)
                            self.ps.release(b, ta)
                        else:
                            j = self.tmp_i % 2
                            self.tmp_i += 1
                            c.wait("act", last, tb_rd[j])
                            ins = nc.scalar.activation(out=self.TMPB[:, j, 0:512], in_=self.ps.banks[b][:, :], func=AF.Square)
                            t1 = c.mark("act", ins)
                            b2 = self.ps.get()
                            c.wait("pe", t1)
                            ins = nc.tensor.matmul(self.ps.banks[b2][:, :], self.ONES[:, :], self.TMPB[:, j, 0:512], start=True, stop=True)
                            t2 = c.mark("pe", ins)
                            tb_rd[j] = t2
                            c.wait("act", t2, tf_rd[j])
                            ins = nc.scalar.activation(out=self.TMPF[:, j, :], in_=self.ps.banks[b2][:, :], func=AF.Sqrt,
                                                       bias=self.EPSB[:, 0:1], scale=1.0 / P)
                            t3 = c.mark("act", ins)
                            self.ps.release(b2, t3)
                            c.wait("dve", t3)
                            ins = nc.vector.reciprocal(out=self.TMPF[:, j, :], in_=self.TMPF[:, j, :])
                            t4 = c.mark("dve", ins)
                            c.wait("dve", t4)
                            kn = 1 if kind == 2 else 2
                            ins = nc.vector.scalar_tensor_tensor(out=KVT[:, i, g, sl], in0=self.ps.banks[b][:, :],
                                                                 scalar=knorm_tile[:, kn:kn + 1],
                                                                 in1=self.TMPF[:, j, :], op0=ALU.mult, op1=ALU.mult)
                            ta = c.mark("dve", ins)
                            tf_rd[j] = ta
                            self.ps.release(b, ta)
                            self.rSQ.done("act", t1); self.rSQ.done("pe", t2)
                            self.rRS.done("act", t3); self.rRS.done("dve", ta)
                outs.append(c.dma("sp", kvT_d[i], KVT[:, i, :, :], waits=[ta]))
            self.ring.release(slot, last)
        self.ht_read = last
        return outs

    def setup_common(self, nrm_d, tri_d=None):
        c, nc = self.c, self.nc
        a = nc.alloc_sbuf_tensor
        ins = nc.vector.memset(self.ONES[:, :], 1.0)
        ins = nc.vector.memset(self.EPSB[:, :], EPS)
        t = c.mark("dve", ins)
        c.wait("act", t)
        c.wait("pe", t)
        t_n = c.dma("sp", self.NRM[:, :, :], nrm_d)
        c.wait("dve", t_n)
        if tri_d is not None:
            self.TRI = a("TRI", [P, P], BF16)
            self.tri_tok = c.dma("pool", self.TRI[:, :], tri_d)


def build_A(n_layers=2, do_kv=True, stop=None):
    nc = bass.Bass("TRN2", target_bir_lowering=False)
    dt = lambda name, shape, kind="ExternalInput", dtype=F32: nc.dram_tensor(name, list(shape), dtype, kind=kind).ap()
    xT_d = dt("xT", [D, T])
    pT_d = dt("pT", [4, PLE, T])
    nrm_d = dt("nrm", [P, NV, KC])
    tri_d = dt("tri", [P, P])
    knorm_d = dt("knormT", [P, 3])
    a_w_in = dt("a_w_in", [2, D, 2 * D])
    a_wsT = dt("a_wsT", [2, KC, P, P])
    a_b_s = dt("a_b_s", [2, KC * P])
    a_w_out = dt("a_w_out", [2, D, D])
    ffn_w_in = dt("ffn_w_in", [4, D, 2 * FF])
    ffn_w_out = dt("ffn_w_out", [4, FF, D])
    ple_w = dt("ple_w", [4, PLE, D])
    ple_gate = dt("ple_gate", [4, D, D])
    kv_w = dt("kv_w", [D, 1536])
    xo_d = dt("xT_out", [D, T], kind="ExternalOutput")
    kvT_d = dt("kvT", [4, P, 2, T], kind="ExternalOutput", dtype=BF16)
    kvV_d = dt("kvV", [2, T, 256], kind="ExternalOutput", dtype=BF16)

    k = Core(nc)
    c = k.c
    for l in range(n_layers):
        lastl = (l == n_layers - 1)
        k.plan_gmlp(a_w_in[l], a_w_out[l])
        if lastl and stop == "gmlp":
            break
        k.plan_ffn(ffn_w_in[l], ffn_w_out[l])
        if lastl and stop == "ffn":
            break
        k.plan_ple(ple_w[l], ple_gate[l])
    if do_kv:
        k.plan_kv(kv_w)
    k.setup_common(nrm_d, tri_d)
    KN = nc.alloc_sbuf_tensor("KN", [P, 3], F32)
    t_kn = c.dma("sp", KN[:, :], knorm_d)
    c.wait("dve", t_kn)
    k.xt_tok = c.dma("sp", k.XT[:, :, :], xT_d.rearrange("(kc p) t -> p kc t", p=P))
    c.wait("dve", k.xt_tok)
    for l in range(n_layers):
        lastl = (l == n_layers - 1)
        k.gmlp(l, a_wsT, a_b_s)
        if lastl and stop == "gmlp":
            break
        k.ple_load(pT_d[l])
        k.ffn(l)
        if lastl and stop == "ffn":
            break
        k.ple(l)
    outs = [c.dma("sp", xo_d.rearrange("(kc p) t -> p kc t", p=P), k.XT[:, :, :], waits=[k.xt_tok])]
    if do_kv:
        outs += k.kv_proj(KN, kvT_d, kvV_d)
    c.wait("sp", *outs)
    return nc


def host_inputs_A(inputs, core):
    f = lambda a: np.ascontiguousarray(a, dtype=np.float32)
    tok = slice(core * T, (core + 1) * T)
    x = inputs["x"][0, tok, :]
    p = inputs["p"][:, 0, tok, :]
    vecs = np.zeros((NV, D), np.float32)
    vecs[NV_MIX:NV_MIX + 4] = inputs["norm_mix"]
    vecs[NV_FFN:NV_FFN + 4] = inputs["norm_ffn"]
    vecs[NV_PLE:NV_PLE + 4] = inputs["norm_ple"]
    vecs[NV_AV:NV_AV + 2] = inputs["a_norm_v"]
    vecs[NV_KV] = inputs["kv_norm"]
    nrm = vecs.reshape(NV, KC, P).transpose(2, 0, 1)
    tri = (np.arange(P)[:, None] <= np.arange(P)[None, :]).astype(np.float32)
    return {
        "xT": f(x.T), "pT": f(p.transpose(0, 2, 1)), "nrm": f(nrm), "tri": tri,
        "knormT": f(inputs["k_norm"].T),
        "a_w_in": f(inputs["a_w_in"]), "a_wsT": f(np.transpose(inputs["a_w_s"], (0, 1, 3, 2))),
        "a_b_s": f(inputs["a_b_s"].reshape(2, KC * P)), "a_w_out": f(inputs["a_w_out"]),
        "ffn_w_in": f(inputs["ffn_w_in"]), "ffn_w_out": f(inputs["ffn_w_out"]),
        "ple_w": f(inputs["ple_w"]), "ple_gate": f(inputs["ple_gate"]), "kv_w": f(inputs["kv_w"]),
    }


NEG = -30000.0
HSCALE = 128.0 ** -0.5
NCMP = 511
NB = 33


class Flat:
    def __init__(self, ap):
        self.ap = ap
        self.off = 0

    def take(self, n):
        v = self.ap[:, self.off:self.off + n]
        self.off += n
        assert self.off <= self.ap.shape[1], (self.off, self.ap.shape)
        return v


class CoreB(Core):
    def carve(self):
        X = Flat(self.XT[:, :, :].rearrange("p a t -> p (a t)").bitcast(BF16))
        r = lambda v, s, **kw: v.rearrange(s, **kw)
        self.D0 = r(X.take(2048), "p (h t) -> p h t", h=16)
        self.D1 = r(X.take(2048), "p (h t) -> p h t", h=16)
        self.D4 = X.take(128)
        self.FC = r(X.take(2048), "p (h t) -> p h t", h=16)
        self.SHC = r(X.take(4096), "p (j c i) -> p j c i", j=8, c=4)
        self.SELF = r(X.take(4096), "p (k s) -> p k s", k=32)
        self.SELN = r(X.take(2048), "p (j r s) -> p j r s", j=8, r=2)
        self.SELG = r(X.take(6144), "p (c m) -> p c m", c=48)
        self.FV = r(X.take(1024), "p (j b) -> p j b", j=8)
        self.OV = r(X.take(512), "p (c b) -> p c b", c=4)
        self.KCT = r(X.take(1024), "p (g i) -> p g i", g=2)
        self.VC = r(X.take(1024), "p (c g d) -> p c g d", c=4, g=2)
        self.IDENT = X.take(128)
        self.JREV = X.take(128)
        Y = Flat(self.B2[:, :])
        self.LKW = [Y.take(640) for _ in range(2)]
        self.LVW = [r(Y.take(640), "p (t d) -> p t d", t=5) for _ in range(2)]
        self.LKS = [Y.take(256) for _ in range(2)]
        self.LVS = [r(Y.take(256), "p (t d) -> p t d", t=2) for _ in range(2)]
        self.KB = Y.take(1152).bitcast(F32)
        self.PC = r(Y.take(2048), "p (c n) -> p c n", c=4)
        self.PR = r(Y.take(1536), "p (c n) -> p c n", c=3)
        self.ACC = r(Y.take(2048).bitcast(F32), "p (a n) -> p a n", a=2)
        self.RINV = Y.take(1024).bitcast(F32)
        self.GREP = Y.take(1024).bitcast(F32)
        self.SCORE = Y.take(256).bitcast(F32)
        self.WORK = Y.take(256).bitcast(F32)
        self.SEL = Y.take(256).bitcast(F32)
        self.M8 = Y.take(32).bitcast(F32)
        self.SNT = Y.take(128)
        self.SELB = Y.take(128)
        self.GS = self.SCR[:, 4 * T:5 * T]
        self.QT = self.B1[:, :].rearrange("p (h t) -> p h t", h=16)
        self.OT = self.HT

    def setup_B(self, d):
        c, nc = self.c, self.nc
        X = Flat(self.XT[:, :, :].rearrange("p a t -> p (a t)").bitcast(BF16))
        MM = X.take(NB * 128).rearrange("p (b t) -> p b t", b=NB)
        TAB = X.take(2 * NB * 16).bitcast(F32).rearrange("p (b h) -> p b h", b=NB)
        ACCD = X.take(2 * 2048).bitcast(F32).rearrange("p (h t) -> p h t", h=16)
        OUTB = X.take(2048).rearrange("p (h t) -> p h t", h=16)
        HID = X.take(1024).rearrange("p (c i) -> p c i", c=2)
        PET = X.take(64).rearrange("p (a j) -> p a j", a=2)
        CB = X.take(8).bitcast(F32)
        KST = X.take(1024).rearrange("p (g i) -> p g i", g=2)
        VST = X.take(1024).rearrange("p (c g d) -> p c g d", c=4, g=2)
        SQT = X.take(512)
        RS = X.take(1024).bitcast(F32)
        TT = self.B1[:, :].rearrange("p (a t) -> p a t", a=2)
        TABF = X.take(32).bitcast(F32)
        TABB = X.take(16)
        OHE = X.take(384)
        EBS = X.take(384)
        t_tab = c.dma("sp", TABF[0:32, :], d["relb"].rearrange("a (b h) -> (a b) h", b=32))
        t_oh = c.dma("sp", OHE[0:NB, 0:383], d["OHE"])
        c.wait("dve", t_tab, t_oh)
        ins = nc.vector.memset(TABF[32:33, :], NEG)
        t0_ = c.mark("dve", ins)
        c.wait("dve", t0_)
        ins = nc.vector.tensor_copy(out=TABB[0:NB, :], in_=TABF[0:NB, :])
        t1_ = c.mark("dve", ins)
        beb = self.ps.get()
        c.wait("pe", t1_, t_oh)
        ins = nc.tensor.matmul(self.ps.banks[beb][0:16, 0:383], TABB[0:NB, :], OHE[0:NB, 0:383], start=True, stop=True)
        t2_ = c.mark("pe", ins)
        c.wait("act", t2_)
        ins = nc.scalar.copy(out=EBS[0:16, 0:383], in_=self.ps.banks[beb][0:16, 0:383])
        t_prev = c.mark("act", ins)
        self.ps.release(beb, t_prev)
        dma_out = [c.dma("sp", d["EB"], EBS[0:16, 0:383], waits=[t_prev])]
        ins = nc.vector.memset(HID[:, :, :], 0.0)
        t_h0 = c.mark("dve", ins)
        ins = nc.vector.memset(KST[:, :, :], 0.0)
        t_h0 = c.mark("dve", ins)
        t_pe = c.dma("pool", PET[:, :, :], d["peT"])
        c.wait("pe", t_pe, t_h0)
        c.wait("act", t_h0)
        tt_rd = [None, None]
        n = 0
        st_tok = None
        for kvi in range(2):
            w1a, r1a, s1a = self.ring.next()
            w1b, r1b, s1b = self.ring.next()
            w2, r2, s2 = self.ring.next()
            c.wait("pe", r1a, r1b, r2)
            w1 = lambda j: (w1a if j < 16 else w1b)[:, j % 16, :]
            bcb = self.ps.get()
            for hc in range(2):
                for j in range(32):
                    ins = nc.tensor.matmul(self.ps.banks[bcb][:, hc:hc + 1], w1(j)[:, hc * P:(hc + 1) * P], PET[:, kvi, j:j + 1],
                                           start=(j == 0), stop=(j == 31))
            tcb = c.mark("pe", ins)
            c.wait("act", tcb)
            ins = nc.scalar.copy(out=CB[:, 2 * kvi:2 * kvi + 2], in_=self.ps.banks[bcb][:, 0:2])
            t_cb = c.mark("act", ins)
            self.ps.release(bcb, t_cb)
            for g in range(2):
                a = n % 2
                n += 1
                t_tt = c.dma("sp", TT[:, a, :], d["kcvT"][kvi, :, g, :], waits=[tt_rd[a]])
                c.wait("pe", t_tt)
                for hc in range(2):
                    b = self.ps.get()
                    for j in range(32):
                        rhs = TT[:, a, j:j + 16 * (NCMP - 1) + 1:16]
                        ins = nc.tensor.matmul(self.ps.banks[b][:, 0:NCMP], w1(j)[:, hc * P:(hc + 1) * P], rhs,
                                               start=(j == 0), stop=(j == 31))
                    tp = c.mark("pe", ins)
                    c.wait("act", tp, t_cb)
                    ins = nc.scalar.activation(out=HID[:, hc, 0:NCMP], in_=self.ps.banks[b][:, 0:NCMP], func=AF.Gelu_apprx_tanh,
                                               bias=CB[:, 2 * kvi + hc:2 * kvi + hc + 1])
                    th = c.mark("act", ins)
                    self.ps.release(b, th)
                tt_rd[a] = tp
                c.wait("pe", th)
                if kvi == 0:
                    b = self.ps.get()
                    for hc in range(2):
                        ins = nc.tensor.matmul(self.ps.banks[b][:, :], w2[:, hc, :], HID[:, hc, :], start=(hc == 0), stop=(hc == 1))
                    t1 = c.mark("pe", ins)
                    c.wait("act", t1)
                    ins = nc.scalar.activation(out=SQT[:, :], in_=self.ps.banks[b][:, :], func=AF.Square)
                    t2 = c.mark("act", ins)
                    b2 = self.ps.get()
                    c.wait("pe", t2)
                    ins = nc.tensor.matmul(self.ps.banks[b2][:, :], self.ONES[:, :], SQT[:, :], start=True, stop=True)
                    t3 = c.mark("pe", ins)
                    c.wait("act", t3)
                    ins = nc.scalar.activation(out=RS[:, :], in_=self.ps.banks[b2][:, :], func=AF.Sqrt, bias=self.EPSB[:, 0:1], scale=1.0 / P)
                    t4 = c.mark("act", ins)
                    self.ps.release(b2, t4)
                    c.wait("dve", t4)
                    ins = nc.vector.reciprocal(out=RS[:, :], in_=RS[:, :])
                    t5 = c.mark("dve", ins)
                    c.wait("dve", t5)
                    ins = nc.vector.scalar_tensor_tensor(out=KST[:, g, 0:NCMP], in0=self.ps.banks[b][:, 0:NCMP], scalar=self.KN[:, 0:1],
                                                         in1=RS[:, 0:NCMP], op0=ALU.mult, op1=ALU.mult)
                    st_tok = c.mark("dve", ins)
                    self.ps.release(b, st_tok)
                    c.wait("pe", st_tok)
                    c.wait("act", st_tok)
                else:
                    for ch in range(4):
                        b = self.ps.get()
                        for hc in range(2):
                            ins = nc.tensor.matmul(self.ps.banks[b][:, 0:P], HID[:, hc, ch * P:(ch + 1) * P], w2[:, hc, :],
                                                   start=(hc == 0), stop=(hc == 1))
                        t1 = c.mark("pe", ins)
                        c.wait("act", t1)
                        ins = nc.scalar.copy(out=VST[:, ch, g, :], in_=self.ps.banks[b][:, 0:P])
                        st_tok = c.mark("act", ins)
                        self.ps.release(b, st_tok)
                    c.wait("pe", st_tok)
            self.ring.release(s1a, tp)
            self.ring.release(s1b, tp)
            self.ring.release(s2, t1)
            if kvi == 0:
                dma_out.append(c.dma("sp", d["KCTd"], KST[:, :, :], waits=[st_tok]))
            else:
                dma_out.append(c.dma("sp", d["VCd"], VST[:, :, :, :], waits=[st_tok]))
        self.setup_done = dma_out
        self.b1_read = tp
        self.setup_toks = [t_prev, st_tok, tp] + dma_out

    def plan_setup_B(self, d):
        for kvi in range(2):
            w1 = d["cmp_w1"][kvi]
            self.ring.add(wslab(w1, 0, 2048, 0, 256), KC, 256)
            self.ring.add(wslab(w1, 2048, 2048, 0, 256), KC, 256)
            self.ring.add(wslab(d["cmp_w2"][kvi], 0, 256, 0, P), 2, P)

    def plan_nsa(self, w_in, w_out, d):
        for s in range(8):
            self.ring.add(wslab(w_in, 0, D, s * 256, 256), KC, 256)
        self.ring.add(wslab(w_in, 0, D, D, 48), KC, 48)
        for g in range(2):
            self.ring.add(d["ksT_full"][:, g, 0:4096].rearrange("p (a t) -> p a t", a=1), 1, 4096)
            self.ring.add(d["ksT_full"][:, g, 4096:8192].rearrange("p (a t) -> p a t", a=1), 1, 4096)
            self.ring.add(d["vs_full"][0:4096, g * P:(g + 1) * P].rearrange("(t p) f -> p t f", p=P), 32, P)
            self.ring.add(d["vs_full"][4096:8192, g * P:(g + 1) * P].rearrange("(t p) f -> p t f", p=P), 32, P)
        for s in range(8):
            self.ring.add(wslab(w_out, 0, D, s * 256, 256), KC, 256)

    def normed_evac(self, b, tok, out_ap, scal_ap, st):
        c, nc = self.c, self.nc
        j = self.tmp_i % 2
        self.tmp_i += 1
        c.wait("act", tok, st["tb"][j])
        ins = nc.scalar.activation(out=self.TMPB[:, j, 0:512], in_=self.ps.banks[b][:, :], func=AF.Square)
        t1 = c.mark("act", ins)
        b2 = self.ps.get()
        c.wait("pe", t1)
        ins = nc.tensor.matmul(self.ps.banks[b2][:, :], self.ONES[:, :], self.TMPB[:, j, 0:512], start=True, stop=True)
        t2 = c.mark("pe", ins)
        st["tb"][j] = t2
        c.wait("act", t2, st["tf"][j])
        ins = nc.scalar.activation(out=self.TMPF[:, j, :], in_=self.ps.banks[b2][:, :], func=AF.Sqrt, bias=self.EPSB[:, 0:1], scale=1.0 / P)
        t3 = c.mark("act", ins)
        self.ps.release(b2, t3)
        c.wait("dve", t3)
        ins = nc.vector.reciprocal(out=self.TMPF[:, j, :], in_=self.TMPF[:, j, :])
        t4 = c.mark("dve", ins)
        c.wait("dve", t4)
        ins = nc.vector.scalar_tensor_tensor(out=out_ap, in0=self.ps.banks[b][:, :], scalar=scal_ap, in1=self.TMPF[:, j, :],
                                             op0=ALU.mult, op1=ALU.mult)
        ta = c.mark("dve", ins)
        st["tf"][j] = ta
        self.rSQ.done("act", t1); self.rSQ.done("pe", t2)
        self.rRS.done("act", t3); self.rRS.done("dve", ta)
        return ta

    def attn_unit(self, Q4, tiles, keepP=None):
        c, nc = self.c, self.nc
        ob, rb = self.ps.get(), self.ps.get()
        n = len(tiles)

        def qk(i):
            t = tiles[i]
            sb = self.ps.get()
            adds = t.get("adds", [])
            s3 = self.ps.banks[sb][:, :].rearrange("p (h t) -> p h t", h=4)
            ins = nc.tensor.matmul(s3, t["kT"], Q4, start=True, stop=(len(adds) == 0))
            for ai, (l_, r_) in enumerate(adds):
                ins = nc.tensor.matmul(s3, l_, r_, start=False, stop=(ai == len(adds) - 1))
            return sb, c.mark("pe", ins)

        LOOK = 3
        pend = [qk(i) for i in range(min(LOOK, n))]
        last = None
        for i in range(n):
            if i + LOOK < n:
                pend.append(qk(i + LOOK))
            sb, tq = pend.pop(0)
            t = tiles[i]
            if keepP is not None:
                pbuf = keepP[:, i, :]
                prd = None
            else:
                s = self.pr_i % 3
                self.pr_i += 1
                pbuf = self.PR[:, s, :]
                prd = self.pr_rd[s]
            c.wait("act", tq, prd)
            if t.get("kb") is not None:
                ins = nc.scalar.activation(out=pbuf, in_=self.ps.banks[sb][:, :], func=AF.Exp, bias=t["kb"])
            else:
                ins = nc.scalar.activation(out=pbuf, in_=self.ps.banks[sb][:, :], func=AF.Exp)
            te = c.mark("act", ins)
            self.ps.release(sb, te)
            c.wait("pe", te)
            nc.tensor.matmul(self.ps.banks[ob][:, :], t["v"], pbuf, start=(i == 0), stop=(i == n - 1))
            ins = nc.tensor.matmul(self.ps.banks[rb][:, :], self.ONES[:, :], pbuf, start=(i == 0), stop=(i == n - 1))
            last = c.mark("pe", ins)
            if keepP is None:
                self.pr_rd[s] = last
        return ob, rb, last

    def combine(self, ob, rb, tok, hh, g, j, br, first):
        c, nc = self.c, self.nc
        gb = self.ps.get()
        c.wait("pe", self.gs_tok)
        for hi in range(4):
            h = 8 * g + 4 * hh + hi
            ins = nc.tensor.matmul(self.ps.banks[gb][:, hi * P:(hi + 1) * P], self.SELG[0:48, 3 * h + br, :],
                                   self.GS[0:48, j * P:(j + 1) * P], start=True, stop=True)
        tg = c.mark("pe", ins)
        c.wait("dve", tok, tg, self.rinv_rd)
        ins = nc.vector.tensor_scalar(out=self.RINV[:, :], in0=self.ps.banks[rb][:, :], scalar1=1e-30, scalar2=None, op0=ALU.add)
        t0 = c.mark("dve", ins)
        c.wait("dve", t0)
        ins = nc.vector.reciprocal(out=self.RINV[:, :], in_=self.RINV[:, :])
        t1 = c.mark("dve", ins)
        self.ps.release(rb, t1)
        c.wait("dve", t1)
        ins = nc.vector.tensor_tensor(out=self.GREP[:, :], in0=self.ps.banks[gb][:, :], in1=self.RINV[:, :], op=ALU.mult)
        t2 = c.mark("dve", ins)
        self.ps.release(gb, t2)
        c.wait("dve", t2)
        if first:
            ins = nc.vector.tensor_tensor(out=self.ACC[:, hh, :], in0=self.ps.banks[ob][:, :], in1=self.GREP[:, :], op=ALU.mult)
            t3 = c.mark("dve", ins)
        else:
            ins = nc.vector.tensor_tensor(out=self.GREP[:, :], in0=self.ps.banks[ob][:, :], in1=self.GREP[:, :], op=ALU.mult)
            t3a = c.mark("dve", ins)
            c.wait("dve", t3a)
            ins = nc.vector.tensor_tensor(out=self.ACC[:, hh, :], in0=self.ACC[:, hh, :], in1=self.GREP[:, :], op=ALU.add)
            t3 = c.mark("dve", ins)
        self.ps.release(ob, t3)
        return t1, t3

    def nsa(self, l, jl, d):
        c, nc = self.c, self.nc
        self.rmsnorm(NV_MIX + l)
        t_spill = c.dma("sp", d["xsp"], self.XT[:, :, :], waits=[self.ht_tok, self.xt_tok])
        st = {"tb": [None, None], "tf": [None, None]}
        self.rSQ.acquire("act")
        self.rRS.acquire("act")
        c.wait("dve", self.b1_read)

        def epi_q(m, h, ps, tok):
            b = self.ps.banks.index(ps)
            return self.normed_evac(b, tok, self.QT[:, m, h * 512:(h + 1) * 512], self.QN[:, jl:jl + 1], st)

        rhs = lambda k, h: self.HT[:, k, h * 512:(h + 1) * 512]
        self.linear_fm(8, KC, rhs, [self.ht_tok], epi_q)
        view, rtok, slot = self.ring.next()
        c.wait("pe", rtok)
        self.rX.acquire("act")
        for h in range(2):
            b = self.ps.get()
            for k in range(KC):
                ins = nc.tensor.matmul(self.ps.banks[b][0:48, :], view[:, k, :], rhs(k, h), start=(k == 0), stop=(k == KC - 1))
            tp = c.mark("pe", ins)
            c.wait("act", tp)
            ins = nc.scalar.activation(out=self.GS[0:48, h * 512:(h + 1) * 512], in_=self.ps.banks[b][0:48, :], func=AF.Sigmoid)
            self.gs_tok = c.mark("act", ins)
            self.ps.release(b, self.gs_tok)
        self.ring.release(slot, tp)
        self.ht_read = tp
        self.rX.done("act", self.gs_tok)
        q_tok = st["tf"][0], st["tf"][1]
        ld = []
        w8 = [t_spill, self.b2_read]
        L = lambda dst, src: ld.append(c.dma("sp", dst, src, waits=w8))
        ebt = d["EB"].tensor
        w8 = w8 + self.setup_toks
        L(self.D0[:, :, :], bass.AP(ebt, 0, [[1, P], [383, 16], [1, P]]))
        L(self.D1[:, :, :], bass.AP(ebt, 128, [[1, P], [383, 16], [1, P]]))
        L(self.D4, d["D4"])
        L(self.FC[0:16, :, :], bass.AP(ebt, 0, [[16, 16], [383, 16], [1, P]]))
        L(self.FC[16:17, :, :], d["FCNEG"])
        L(self.JREV, d["JREV"])
        L(self.SHC[0:17, :, :, :], d["SHC"]); L(self.SELF[:, :, :], d["SELF"])
        L(self.SELN[:, :, :, :], d["SELN"]); L(self.SELG[0:48, :, :], d["SELG"]); L(self.FV[:, :, :], d["FV"])
        L(self.OV[:, :, :], d["OV"]); L(self.KCT[:, :, :], d["KCTd"]); L(self.VC[:, :, :, :], d["VCd"])
        L(self.IDENT, d["IDENT"])
        L(self.KB, d["KB"])
        for e in ("pe", "act", "dve"):
            c.wait(e, *ld)
        c.wait("pe", *q_tok)
        c.wait("act", self.ht_read)
        c.wait("dve", self.ht_read)
        self.pr_i = 0
        self.pr_rd = [None, None, None]
        self.rinv_rd = None
        KBW = self.KB[:, 0:40]
        KBS = self.KB[:, 40:56]
        KBF = self.KB[:, 56:56 + 512].rearrange("p (j k) -> p j k", j=8)
        loc_tok = {}
        loc_rd = [None, None]

        def load_local(it):
            g_, j_ = divmod(it, NT)
            b_ = it % 2
            w_ = w8 + [loc_rd[b_]]
            loc_tok[it] = [
                c.dma("sp", self.LKW[b_], d["kw_loc"][g_, j_], waits=w_),
                c.dma("sp", self.LVW[b_][:, :, :], d["vw_loc"][g_, j_].rearrange("(t p) e -> p t e", p=P), waits=w_),
                c.dma("sp", self.LKS[b_], d["ks_loc"][g_, j_], waits=w_),
                c.dma("sp", self.LVS[b_][:, :, :], d["vs_loc"][g_, j_].rearrange("(t p) e -> p t e", p=P), waits=w_),
            ]

        load_local(0)
        last_ot = None
        last_far = None
        self.pc_rd = None
        self.sel_rd = None
        for g in range(2):
            kA, rkA, skA = self.ring.next()
            kB, rkB, skB = self.ring.next()
            vA, rvA, svA = self.ring.next()
            vB, rvB, svB = self.ring.next()
            c.wait("pe", rkA, rkB, rvA, rvB)
            for j in range(NT):
                it = g * NT + j
                lb = it % 2
                if it + 1 < 2 * NT:
                    load_local(it + 1)
                c.wait("pe", *loc_tok[it])
                ib = self.ps.get()
                acc_t = [None, None]
                for hh in range(2):
                    h0 = 8 * g + 4 * hh
                    Q4 = self.QT[:, h0:h0 + 4, j * P:(j + 1) * P]
                    tiles = [dict(kT=self.KCT[:, g, ch * P:(ch + 1) * P], v=self.VC[:, ch, g, :],
                                  adds=[(self.SHC[0:17, j, ch, :], self.FC[0:17, h0:h0 + 4, :])]) for ch in range(4)]
                    c.wait("act", self.pc_rd)
                    ob, rb, tk = self.attn_unit(Q4, tiles, keepP=self.PC)
                    t1, t3 = self.combine(ob, rb, tk, hh, g, j, 0, True)
                    acc_t[hh] = t3
                    c.wait("dve", t1)
                    for ch in range(4):
                        ins = nc.vector.tensor_tensor(out=self.PC[:, ch, :], in0=self.PC[:, ch, :], in1=self.RINV[:, :], op=ALU.mult)
                    tn = c.mark("dve", ins)
                    self.rinv_rd = tn
                    c.wait("pe", tn)
                    for hi in range(4):
                        for ch in range(4):
                            ins = nc.tensor.matmul(self.ps.banks[ib][:, 0:P], self.PC[:, ch, hi * P:(hi + 1) * P], self.OV[:, ch, :],
                                                   start=(hh == 0 and hi == 0 and ch == 0), stop=(hh == 1 and hi == 3 and ch == 3))
                    self.pc_rd = c.mark("pe", ins)
                pc_rd = self.pc_rd
                c.wait("dve", pc_rd, self.sel_rd)
                ins = nc.vector.tensor_tensor(out=self.SCORE, in0=self.ps.banks[ib][:, 0:P], in1=self.FV[:, j, :], op=ALU.add)
                ts = c.mark("dve", ins)
                self.ps.release(ib, ts)
                c.wait("dve", ts)
                ins = nc.vector.max(out=self.M8[:, 0:8], in_=self.SCORE)
                ts = c.mark("dve", ins); c.wait("dve", ts)
                ins = nc.vector.match_replace(out=self.WORK, in_to_replace=self.M8[:, 0:8], in_values=self.SCORE, imm_value=-1e38)
                ts = c.mark("dve", ins); c.wait("dve", ts)
                ins = nc.vector.max(out=self.M8[:, 8:16], in_=self.WORK)
                ts = c.mark("dve", ins); c.wait("dve", ts)
                ins = nc.vector.match_replace(out=self.WORK, in_to_replace=self.M8[:, 8:16], in_values=self.WORK, imm_value=-1e38)
                ts = c.mark("dve", ins); c.wait("dve", ts)
                ins = nc.vector.tensor_tensor(out=self.WORK, in0=self.SCORE, in1=self.WORK, op=ALU.subtract)
                ts = c.mark("dve", ins); c.wait("dve", ts)
                ins = nc.vector.tensor_scalar(out=self.WORK, in0=self.WORK, scalar1=1.0, scalar2=None, op0=ALU.min)
                ts = c.mark("dve", ins); c.wait("dve", ts)
                ins = nc.vector.tensor_scalar(out=self.SEL, in0=self.SCORE, scalar1=-1e29, scalar2=None, op0=ALU.is_gt)
                ts = c.mark("dve", ins); c.wait("dve", ts)
                ins = nc.vector.tensor_tensor(out=self.SEL, in0=self.SEL, in1=self.WORK, op=ALU.mult)
                ts = c.mark("dve", ins); c.wait("dve", ts)
                ins = nc.vector.tensor_scalar(out=self.SELB, in0=self.SEL, scalar1=-NEG, scalar2=NEG, op0=ALU.mult, op1=ALU.add)
                ts = c.mark("dve", ins)
                for hh in range(2):
                    h0 = 8 * g + 4 * hh
                    Q4 = self.QT[:, h0:h0 + 4, j * P:(j + 1) * P]
                    tiles = []
                    for r in range(5):
                        t = dict(kT=self.LKW[lb][:, r * P:(r + 1) * P], v=self.LVW[lb][:, r, :], kb=KBW[:, 5 * j + r:5 * j + r + 1], adds=[])
                        if r == 4:
                            t["adds"].append((self.JREV, self.D0[:, h0:h0 + 4, :]))
                        elif r == 3:
                            t["adds"].append((self.JREV, self.D1[:, h0:h0 + 4, :]))
                        elif r == 0:
                            t["adds"].append((self.IDENT, self.D4.unsqueeze(1).to_broadcast([P, 4, P])))
                        tiles.append(t)
                    ob, rb, tk = self.attn_unit(Q4, tiles)
                    self.combine(ob, rb, tk, hh, g, j, 2, False)
                tb_ = self.ps.get()
                c.wait("pe", ts)
                ins = nc.tensor.matmul(self.ps.banks[tb_][:, 0:P], self.SELB, self.IDENT, start=True, stop=True)
                tt = c.mark("pe", ins)
                self.sel_rd = tt
                c.wait("act", tt, last_far)
                ins = nc.scalar.copy(out=self.SNT, in_=self.ps.banks[tb_][:, 0:P])
                t_snt = c.mark("act", ins)
                self.ps.release(tb_, t_snt)
                c.wait("pe", t_snt)
                for hh in range(2):
                    h0 = 8 * g + 4 * hh
                    Q4 = self.QT[:, h0:h0 + 4, j * P:(j + 1) * P]
                    SN4 = lambda rows: self.SNT[rows, :].unsqueeze(1).to_broadcast([rows.stop - rows.start, 4, P])
                    tiles = []
                    for r in range(2):
                        tiles.append(dict(kT=self.LKS[lb][:, r * P:(r + 1) * P], v=self.LVS[lb][:, r, :], kb=KBS[:, 2 * j + r:2 * j + r + 1],
                                          adds=[(self.JREV, (self.D1 if r == 0 else self.D0)[:, h0:h0 + 4, :]),
                                                (self.SELN[:, j, r, :], SN4(slice(0, P)))]))
                    for kg in range(8 * (j + 1)):
                        kv_ = kA if kg < 32 else kB
                        vv_ = vA if kg < 32 else vB
                        a = kg // 32
                        tiles.append(dict(kT=kv_[:, 0, (kg % 32) * P:(kg % 32 + 1) * P], v=vv_[:, kg % 32, :], kb=KBF[:, j, kg:kg + 1],
                                          adds=[(self.SELF[64 * a:64 * a + 64, kg % 32, :], SN4(slice(64 * a, 64 * a + 64)))]))
                    ob, rb, tk = self.attn_unit(Q4, tiles)
                    last_far = tk
                    t1, t3 = self.combine(ob, rb, tk, hh, g, j, 1, False)
                    self.rinv_rd = t1
                    c.wait("dve", t3)
                    ins = nc.vector.tensor_copy(out=self.OT[:, h0:h0 + 4, j * P:(j + 1) * P],
                                                in_=self.ACC[:, hh, :].rearrange("p (h t) -> p h t", h=4))
                    last_ot = c.mark("dve", ins)
                loc_rd[lb] = last_far
            for s_ in (skA, skB, svA, svB):
                self.ring.release(s_, last_far)
        t_x = c.dma("sp", self.XT[:, :, :], d["xsp"], waits=[last_far, last_ot, t_spill])
        self.xt_tok = t_x
        c.wait("dve", t_x)
        c.wait("act", t_x)
        self.ht_tok = last_ot
        self.b2_read = last_far
        self.b1_read = last_far
        self.rX.done("pe", last_far)
        last = self.linear_fm(8, KC, lambda k, h: self.OT[:, k, h * 512:(h + 1) * 512], [last_ot], self.add_to_xt)
        self.ht_read = last


def build_B(n_layers=2, dbg=False):
    nc = bass.Bass("TRN2", target_bir_lowering=False)
    dt = lambda name, shape, kind="ExternalInput", dtype=F32: nc.dram_tensor(name, list(shape), dtype, kind=kind).ap()
    d = {}
    xT_d = dt("xT", [D, T])
    pT_d = dt("pT", [2, PLE, T])
    nrm_d = dt("nrm", [P, NV, KC])
    knorm_d = dt("knormT", [P, 3])
    qn_d = dt("qnT", [P, 2])
    b_w_in = dt("b_w_in", [2, D, D + 48])
    b_w_out = dt("b_w_out", [2, D, D])
    if not dbg:
        ffn_w_in = dt("ffn_w_in", [2, D, 2 * FF])
        ffn_w_out = dt("ffn_w_out", [2, FF, D])
        ple_w = dt("ple_w", [2, PLE, D])
        ple_gate = dt("ple_gate", [2, D, D])
    d["cmp_w1"] = dt("cmp_w1", [2, 4096, 256])
    d["cmp_w2"] = dt("cmp_w2", [2, 256, P])
    d["peT"] = dt("peT", [P, 2, 32])
    d["relb"] = dt("relb", [1, 512])
    d["kcvT"] = dt("kcvT", [2, P, 2, SEQ], dtype=BF16)
    d["ksT_full"] = dt("ksT_full", [P, 2, SEQ], dtype=BF16)
    d["vs_full"] = dt("vs_full", [SEQ, 256], dtype=BF16)
    d["kw_loc"] = dt("kw_loc", [2, NT, P, 5 * P], dtype=BF16)
    d["vw_loc"] = dt("vw_loc", [2, NT, 5 * P, P], dtype=BF16)
    d["ks_loc"] = dt("ks_loc", [2, NT, P, 2 * P], dtype=BF16)
    d["vs_loc"] = dt("vs_loc", [2, NT, 2 * P, P], dtype=BF16)
    d["OHE"] = dt("OHE", [NB, 383], dtype=BF16)
    d["JREV"] = dt("JREV", [P, P], dtype=BF16)
    d["FCNEG"] = dt("FCNEG", [1, 16, P], dtype=BF16)
    for name, shape in (("D4", [P, P]), ("SHC", [17, 8, 4, P]),
                        ("SELF", [P, 32, P]), ("SELN", [P, 8, 2, P]), ("SELG", [48, 48, P]), ("FV", [P, 8, P]), ("OV", [P, 4, P]),
                        ("IDENT", [P, P])):
        d[name] = dt(name, shape, dtype=BF16)
    d["KB"] = dt("KB", [P, 576])
    xo_d = dt("xT_out", [D, T], kind="ExternalOutput")
    it = lambda name, shape, dtype=BF16: nc.dram_tensor(name, list(shape), dtype, kind="Internal").ap()
    d["EB"] = it("EBs", [16, 383])
    d["KCTd"] = it("KCTs", [P, 2, 512]); d["VCd"] = it("VCs", [P, 4, 2, P])
    d["xsp"] = it("xsp", [P, KC, T], F32)

    k = CoreB(nc)
    c = k.c
    k.plan_setup_B(d)
    for jl in range(n_layers):
        k.plan_nsa(b_w_in[jl], b_w_out[jl], d)
        if dbg:
            break
        k.plan_ffn(ffn_w_in[jl], ffn_w_out[jl])
        k.plan_ple(ple_w[jl], ple_gate[jl])
    k.setup_common(nrm_d, None)
    k.KN = nc.alloc_sbuf_tensor("KN", [P, 3], F32)
    k.QN = nc.alloc_sbuf_tensor("QN", [P, 2], F32)
    t_kn = c.dma("sp", k.KN[:, :], knorm_d)
    t_qn = c.dma("sp", k.QN[:, :], qn_d)
    c.wait("dve", t_kn, t_qn)
    ins = nc.vector.tensor_scalar(out=k.QN[:, :], in0=k.QN[:, :], scalar1=HSCALE, scalar2=None, op0=ALU.mult)
    c.mark("dve", ins)
    k.carve()
    k.setup_B(d)
    k.xt_tok = c.dma("sp", k.XT[:, :, :], xT_d.rearrange("(kc p) t -> p kc t", p=P), waits=k.setup_toks)
    c.wait("dve", k.xt_tok)
    k.b2_read = None
    for jl in range(n_layers):
        l = 2 + jl
        k.nsa(l, jl, d)
        if dbg:
            break
        k.ple_load(pT_d[jl])
        k.ffn(l)
        k.ple(l)
    outs = [c.dma("sp", xo_d.rearrange("(kc p) t -> p kc t", p=P), k.XT[:, :, :], waits=[k.xt_tok])]
    c.wait("sp", *outs)
    return nc


def t5_bucket_np(n):
    n = np.maximum(n, 0)
    nf = np.maximum(n, 1).astype(np.float32)
    large = 16 + (np.log(nf / np.float32(16)) / np.float32(np.log(128 / 16)) * np.float32(16)).astype(np.int32)
    large = np.minimum(large, 31)
    return np.where(n < 16, n, large)


def onehot_table(delta):
    b = np.where(delta < 0, 32, t5_bucket_np(delta))
    return (b[..., None] == np.arange(NB)).astype(np.float32)


def qmap(core, j):
    return 16 * (j // 2) + (core if j % 2 == 0 else 15 - core)


def host_tables_B(core):
    bf = lambda a: np.ascontiguousarray(a).astype(ml_dtypes.bfloat16)
    sig = np.arange(P)[:, None]
    tau = np.arange(P)[None, :]
    t = {}
    delta = np.arange(383) - 127
    ohe = onehot_table(delta).T.copy()
    ohe[31, delta >= 0] -= 1.0
    t["OHE"] = bf(ohe)
    t["JREV"] = bf(np.eye(P, dtype=np.float32)[::-1])
    t["FCNEG"] = bf(np.full((1, 16, P), NEG, np.float32))
    t["D4"] = bf(np.where(tau >= sig, NEG, 0.0))
    shc = np.zeros((17, 8, 4, P), np.float32)
    seln = np.zeros((P, 8, 2, P), np.float32)
    fv = np.zeros((P, 8, P), np.float32)
    kbf = np.zeros((P, 8, 64), np.float32)
    blk = np.arange(P)
    for j in range(8):
        qg = qmap(core, j)
        idx = np.arange(512).reshape(4, P)
        for i_ in range(16):
            shc[15 - i_, j] = (idx == 8 * qg + i_ - 9)
        shc[16, j] = (idx > 8 * qg + 6) | (idx >= NCMP)
        for r in range(2):
            kt = qg - 1 + r
            if kt >= 0:
                seln[:, j, r, :] = (blk[:, None] == 2 * kt + (np.arange(P)[None, :] // 64))
        tt = 128 * qg + np.arange(P)
        cur = tt // 64
        f = np.where(blk[None, :] <= cur[:, None], 0.0, -1e30).astype(np.float32)
        f[:, 0] = 1e30
        for q in range(P):
            if cur[q] >= 1:
                f[q, cur[q] - 1] = 3e30
            f[q, cur[q]] = 2e30
        fv[:, j, :] = f
        kg = np.arange(64)
        kbf[:, j, :] = np.where((kg == qg) | (kg == qg - 1) | (kg > qg), NEG, 0.0)[None, :]
    t["SHC"] = bf(shc); t["SELN"] = bf(seln); t["FV"] = bf(fv)
    selfar = np.zeros((P, 32, P), np.float32)
    for k_ in range(32):
        selfar[:, k_, :] = ((np.arange(P)[:, None] % 64) == 2 * k_ + (np.arange(P)[None, :] // 64))
    t["SELF"] = bf(selfar)
    t["SELG"] = bf(np.broadcast_to(np.eye(48, dtype=np.float32)[:, :, None], (48, 48, P)))
    c0 = np.arange(512) * 16
    s0 = np.arange(P) * 64
    lo = np.maximum(c0[:, None], s0[None, :]); hi = np.minimum(c0[:, None] + 32, s0[None, :] + 64)
    ov = np.maximum(hi - lo, 0).astype(np.float32) / 32.0
    ov[NCMP:] = 0
    t["OV"] = bf(ov.reshape(4, P, P).transpose(1, 0, 2))
    t["IDENT"] = bf(np.eye(P, dtype=np.float32))
    kb = np.zeros((P, 576), np.float32)
    for j in range(8):
        for r in range(5):
            kb[:, 5 * j + r] = NEG if (qmap(core, j) - 4 + r) < 0 else 0.0
        for r in range(2):
            kb[:, 40 + 2 * j + r] = NEG if (qmap(core, j) - 1 + r) < 0 else 0.0
    kb[:, 56:56 + 512] = kbf.reshape(P, 512)
    t["KB"] = kb
    return t


def host_inputs_B(inputs, core, x1, kvT, kvV):
    f = lambda a: np.ascontiguousarray(a, dtype=np.float32)
    rows = np.concatenate([np.arange(qmap(core, j) * P, (qmap(core, j) + 1) * P) for j in range(NT)])
    p = inputs["p"][2:4, 0][:, rows, :]
    vecs = np.zeros((NV, D), np.float32)
    vecs[NV_MIX:NV_MIX + 4] = inputs["norm_mix"]
    vecs[NV_FFN:NV_FFN + 4] = inputs["norm_ffn"]
    vecs[NV_PLE:NV_PLE + 4] = inputs["norm_ple"]
    nrm = vecs.reshape(NV, KC, P).transpose(2, 0, 1)
    fullT = lambda i: np.ascontiguousarray(np.concatenate([kvT[c_, i] for c_ in range(NCORES)], axis=-1))
    fullV = lambda i: np.ascontiguousarray(np.concatenate([kvV[c_, i] for c_ in range(NCORES)], axis=0))
    ksT, kwT = fullT(2), fullT(3)
    vs, vw = fullV(0), fullV(1)

    def locT(full, halo):
        out = np.zeros((2, NT, P, (halo + 1) * P), full.dtype)
        for j in range(NT):
            q = qmap(core, j)
            lo = (q - halo) * P
            s = max(lo, 0)
            out[:, j, :, s - lo:] = full[:, :, s:(q + 1) * P].transpose(1, 0, 2)
        return out

    def locV(full, halo):
        out = np.zeros((2, NT, (halo + 1) * P, P), full.dtype)
        for j in range(NT):
            q = qmap(core, j)
            lo = (q - halo) * P
            s = max(lo, 0)
            out[:, j, s - lo:, :] = full[s:(q + 1) * P].reshape(-1, 2, P).transpose(1, 0, 2)
        return out

    m = {
        "xT": f(x1[rows].T), "pT": f(p.transpose(0, 2, 1)), "nrm": f(nrm), "knormT": f(inputs["k_norm"].T), "qnT": f(inputs["b_q_norm"].T),
        "b_w_in": f(inputs["b_w_in"]), "b_w_out": f(inputs["b_w_out"]),
        "ffn_w_in": f(inputs["ffn_w_in"][2:4]), "ffn_w_out": f(inputs["ffn_w_out"][2:4]),
        "ple_w": f(inputs["ple_w"][2:4]), "ple_gate": f(inputs["ple_gate"][2:4]),
        "cmp_w1": f(np.stack([inputs["cmp_wk1"], inputs["cmp_wv1"]])), "cmp_w2": f(np.stack([inputs["cmp_wk2"], inputs["cmp_wv2"]])),
        "peT": f(np.stack([inputs["cmp_pe_k"].T, inputs["cmp_pe_v"].T], axis=1)),
        "relb": f(inputs["rel_bias"].reshape(1, 512)),
        "kcvT": np.ascontiguousarray(np.stack([fullT(0), fullT(1)])), "ksT_full": ksT, "vs_full": vs,
        "kw_loc": locT(kwT, 4), "vw_loc": locV(vw, 4), "ks_loc": locT(ksT, 1), "vs_loc": locV(vs, 1),
    }
    m.update(host_tables_B(core))
    return m


_NC_CACHE = {}


def kernel(**inputs):
    inputs = {k_: np.asarray(v) for k_, v in inputs.items()}
    if "A" not in _NC_CACHE:
        _NC_CACHE["A"] = build_A()
    resA = run_bass_kernel_spmd(_NC_CACHE["A"], [host_inputs_A(inputs, c_) for c_ in range(NCORES)], core_ids=list(range(NCORES)))
    kvT = np.stack([r["kvT"] for r in resA.results])
    kvV = np.stack([r["kvV"] for r in resA.results])
    x1 = np.concatenate([r["xT_out"].T for r in resA.results], axis=0)
    if "B" not in _NC_CACHE:
        _NC_CACHE["B"] = build_B()
    in_B = [host_inputs_B(inputs, c_, x1, kvT, kvV) for c_ in range(NCORES)]
    resB = run_bass_kernel_spmd(_NC_CACHE["B"], in_B, core_ids=list(range(NCORES)))
    out = np.zeros((SEQ, D), np.float32)
    for c_ in range(NCORES):
        y = resB.results[c_]["xT_out"].T
        for j in range(NT):
            q = qmap(c_, j)
            out[q * P:(q + 1) * P] = y[j * P:(j + 1) * P]
    return np.ascontiguousarray(out[None])
```
